# Optimizing a Trainium2 kernel written in Bass

```python
import jax, jax.numpy as jnp
from jax import lax
import numpy as np

D_MODEL = 1024
BATCH = 4
SEQ = 4096
DEPTH = 2

GLA_HEADS = 4
GLA_KEY = D_MODEL // 4
GLA_VAL = D_MODEL // 2
GLA_DK = GLA_KEY // GLA_HEADS
GLA_DV = GLA_VAL // GLA_HEADS
GLA_RANK = 16
GLA_TAU = 16.0
GLA_CHUNK = 64
SGU_GROUPS = 4
SGU_WIDTH = D_MODEL // 4
SGU_GD = SGU_WIDTH // SGU_GROUPS
SGU_CHUNK = 128
POOL_WINDOWS = (2, 4, 8, 16)
POOL_WIDTH = D_MODEL // 4
POOL_GD = POOL_WIDTH // len(POOL_WINDOWS)
N_BRANCH = 3
IN_SIZES = (GLA_KEY, GLA_KEY, GLA_VAL, GLA_VAL, GLA_RANK, 2 * SGU_WIDTH, POOL_WIDTH, N_BRANCH * D_MODEL)
IN_SPLITS = tuple(sum(IN_SIZES[:i + 1]) for i in range(len(IN_SIZES) - 1))
N_IN = sum(IN_SIZES)
MEM_LEN = 256
XA_HEADS = 4
XA_DH = D_MODEL // XA_HEADS
N_EXPERTS = 32
TOP_K = 4
EXPERT_FF = D_MODEL
SWIGLU_LIMIT = 7.0
SWIGLU_ALPHA = 1.702
MOE_BLOCK = 128
DEEPNORM_ALPHA = (2 * DEPTH) ** 0.25
DEEPNORM_BETA = (8 * DEPTH) ** -0.25
LN_EPS = 1e-5

kernel_name = 'hybrid_gla_sgu_pool_moe_block'


def layer_norm(x, g, b):
    xf = x.astype(jnp.float32)
    mu = xf.mean(-1, keepdims=True)
    var = jnp.square(xf - mu).mean(-1, keepdims=True)
    return ((xf - mu) * lax.rsqrt(var + LN_EPS) * g + b).astype(x.dtype)


def rms_norm(x, g):
    xf = x.astype(jnp.float32)
    return xf * lax.rsqrt(jnp.square(xf).mean(-1, keepdims=True) + LN_EPS) * g


def gla_chunked(q, k, v, log_a):
    B, S, H, dk = q.shape
    dv = v.shape[-1]
    C = GLA_CHUNK
    N = S // C
    f32 = jnp.float32
    q = q.astype(f32).reshape(B, N, C, H, dk) * (dk ** -0.5)
    k = k.astype(f32).reshape(B, N, C, H, dk)
    v = v.astype(f32).reshape(B, N, C, H, dv)
    b = jnp.cumsum(log_a.astype(f32).reshape(B, N, C, H, dk), axis=2)
    b_last = b[:, :, -1:]
    q_dec = q * jnp.exp(b)
    k_dec = k * jnp.exp(-b)
    k_tail = k * jnp.exp(b_last - b)
    causal = jnp.tril(jnp.ones((C, C), bool))
    scores = jnp.einsum('bnthd,bnshd->bnhts', q_dec, k_dec)
    scores = jnp.where(causal, scores, 0.0)
    o_intra = jnp.einsum('bnhts,bnshe->bnthe', scores, v)
    chunk_kv = jnp.einsum('bnshd,bnshe->nbhde', k_tail, v)
    chunk_decay = jnp.exp(jnp.moveaxis(b_last[:, :, 0], 1, 0))

    def step(state, inp):
        dec, kv = inp
        return dec[..., None] * state + kv, state

    _, s_in = lax.scan(step, jnp.zeros((B, H, dk, dv), f32), (chunk_decay, chunk_kv))
    o_inter = jnp.einsum('bnthd,nbhde->bnthe', q_dec, s_in)
    return (o_intra + o_inter).reshape(B, S, H, dv)


def spatial_gating(uv_pre, ln_g, ln_b, ws, bs):
    B, S, _ = uv_pre.shape
    z = jax.nn.gelu(uv_pre, approximate=False)
    u, v = jnp.split(z, 2, axis=-1)
    v = layer_norm(v, ln_g, ln_b)
    N = S // SGU_CHUNK
    v = v.reshape(B, N, SGU_CHUNK, SGU_GROUPS, SGU_GD)
    ws_causal = jnp.tril(ws)
    s = jnp.einsum('gts,bnsgd->bntgd', ws_causal, v) + bs.T[:, :, None]
    return u * s.reshape(B, S, SGU_WIDTH)


def pool_mixer(xc, pool_w, pool_scale):
    B, S, _ = xc.shape
    G = len(POOL_WINDOWS)
    xg = xc.astype(jnp.float32).reshape(B, S, G, POOL_GD)
    csum = jnp.cumsum(xg, axis=1)
    t = jnp.arange(S)
    pooled = []
    for i, w in enumerate(POOL_WINDOWS):
        c_i = csum[:, :, i]
        c_prev = jnp.pad(c_i, ((0, 0), (w, 0), (0, 0)))[:, :S]
        count = jnp.minimum(t + 1, w).astype(jnp.float32)[None, :, None]
        pooled.append((c_i - c_prev) / count)
    y = jnp.stack(pooled, axis=2) - xg
    y = jnp.einsum('bsgc,gcd->bsgd', y, pool_w.astype(jnp.float32))
    return (y.reshape(B, S, POOL_WIDTH) * pool_scale).astype(xc.dtype)


def hybrid_mixer(x, w_in, b_in, gla_wg2, gla_bg, gla_norm_g, sgu_ln_g, sgu_ln_b, sgu_ws, sgu_bs,
                 pool_w, pool_scale, w_up_a, w_up_b, w_up_c, w_o):
    B, S, D = x.shape
    p = x @ w_in + b_in
    q, k, v, r, g_low, uv, xc, gate_pre = jnp.split(p, IN_SPLITS, axis=-1)
    log_a = jax.nn.log_sigmoid((g_low @ gla_wg2 + gla_bg).astype(jnp.float32)) / GLA_TAU
    o = gla_chunked(q.reshape(B, S, GLA_HEADS, GLA_DK), k.reshape(B, S, GLA_HEADS, GLA_DK),
                    v.reshape(B, S, GLA_HEADS, GLA_DV), log_a.reshape(B, S, GLA_HEADS, GLA_DK))
    o = rms_norm(o, gla_norm_g.reshape(GLA_HEADS, GLA_DV))
    y_a = o.reshape(B, S, GLA_VAL).astype(x.dtype) * jax.nn.silu(r)
    y_b = spatial_gating(uv, sgu_ln_g, sgu_ln_b, sgu_ws, sgu_bs)
    y_c = pool_mixer(xc, pool_w, pool_scale)
    gates = jax.nn.sigmoid(gate_pre.reshape(B, S, N_BRANCH, D))
    merged = (gates[:, :, 0] * (y_a @ w_up_a) + gates[:, :, 1] * (y_b @ w_up_b)
              + gates[:, :, 2] * (y_c @ w_up_c))
    return merged @ w_o


def memory_cross_attention(x, mem, wq, wk, wv, wo):
    B, S, D = x.shape
    M = mem.shape[1]
    q = (x @ wq).reshape(B, S, XA_HEADS, XA_DH)
    k = (mem @ wk).reshape(B, M, XA_HEADS, XA_DH)
    v = (mem @ wv).reshape(B, M, XA_HEADS, XA_DH)
    s = jnp.einsum('bshd,bmhd->bhsm', q, k).astype(jnp.float32) * (XA_DH ** -0.5)
    p = jax.nn.softmax(s, axis=-1).astype(v.dtype)
    o = jnp.einsum('bhsm,bmhd->bshd', p, v).reshape(B, S, D)
    return o @ wo


def moe_ffn(x, router_w, router_b, w_gu, b_gu, w_down, b_down):
    B, S, D = x.shape
    x2 = x.reshape(-1, D)
    N = x2.shape[0]
    logits = (x2 @ router_w).astype(jnp.float32) + router_b
    top_vals, top_idx = lax.top_k(logits, TOP_K)
    gate = jax.nn.softmax(top_vals, axis=-1)
    A = N * TOP_K
    flat_e = top_idx.reshape(-1)
    flat_tok = jnp.repeat(jnp.arange(N, dtype=jnp.int32), TOP_K)
    flat_g = gate.reshape(-1)
    order = jnp.argsort(flat_e)
    se, st, sg = flat_e[order], flat_tok[order], flat_g[order]
    counts = jnp.bincount(flat_e, length=N_EXPERTS)
    padded = ((counts + MOE_BLOCK - 1) // MOE_BLOCK) * MOE_BLOCK
    pad_end = jnp.cumsum(padded)
    pad_start = pad_end - padded
    start = jnp.cumsum(counts) - counts
    dest = pad_start[se] + (jnp.arange(A) - start[se])
    n_blocks = -(-A // MOE_BLOCK) + N_EXPERTS
    slot_tok = jnp.zeros((n_blocks * MOE_BLOCK,), jnp.int32).at[dest].set(st)
    slot_gate = jnp.zeros((n_blocks * MOE_BLOCK,), jnp.float32).at[dest].set(sg)
    block_expert = jnp.minimum(jnp.searchsorted(pad_end, jnp.arange(n_blocks) * MOE_BLOCK, side='right'),
                               N_EXPERTS - 1)

    def expert_block(args):
        e, tok, g = args
        h = x2[tok] @ w_gu[e] + b_gu[e]
        h_glu = jnp.minimum(h[:, :EXPERT_FF], SWIGLU_LIMIT)
        h_lin = jnp.clip(h[:, EXPERT_FF:], -SWIGLU_LIMIT, SWIGLU_LIMIT)
        a = h_glu * jax.nn.sigmoid(SWIGLU_ALPHA * h_glu) * (h_lin + 1.0)
        out = a @ w_down[e] + b_down[e]
        return out * g.astype(out.dtype)[:, None]

    ys = lax.map(expert_block, (block_expert, slot_tok.reshape(n_blocks, MOE_BLOCK),
                                slot_gate.reshape(n_blocks, MOE_BLOCK)))
    y = jnp.zeros_like(x2).at[slot_tok].add(ys.reshape(-1, D).astype(x2.dtype))
    return y.reshape(B, S, D)


def setup_inputs(seed: int = 0) -> dict:
    key = jax.random.key(seed)
    ks = iter(jax.random.split(key, 40))
    f32 = jnp.float32
    L, D = DEPTH, D_MODEL
    beta = DEEPNORM_BETA

    def nrm(shape, scale):
        return jax.random.normal(next(ks), shape, f32) * scale

    return {
        'x': nrm((BATCH, SEQ, D), 1.0),
        'mem': nrm((BATCH, MEM_LEN, D), 1.0),
        'w_in': nrm((L, D, N_IN), D ** -0.5),
        'b_in': nrm((L, N_IN), 0.02),
        'gla_wg2': nrm((L, GLA_RANK, GLA_KEY), GLA_RANK ** -0.5),
        'gla_bg': nrm((L, GLA_KEY), 0.1),
        'gla_norm_g': 1.0 + nrm((L, GLA_VAL), 0.02),
        'sgu_ln_g': 1.0 + nrm((L, SGU_WIDTH), 0.02),
        'sgu_ln_b': nrm((L, SGU_WIDTH), 0.02),
        'sgu_ws': nrm((L, SGU_GROUPS, SGU_CHUNK, SGU_CHUNK), SGU_CHUNK ** -0.5),
        'sgu_bs': 1.0 + nrm((L, SGU_GROUPS, SGU_CHUNK), 0.1),
        'pool_w': nrm((L, len(POOL_WINDOWS), POOL_GD, POOL_GD), POOL_GD ** -0.5),
        'pool_scale': 1.0 + nrm((L, POOL_WIDTH), 0.1),
        'w_up_a': nrm((L, GLA_VAL, D), beta * GLA_VAL ** -0.5),
        'w_up_b': nrm((L, SGU_WIDTH, D), beta * SGU_WIDTH ** -0.5),
        'w_up_c': nrm((L, POOL_WIDTH, D), beta * POOL_WIDTH ** -0.5),
        'w_o': nrm((L, D, D), beta * D ** -0.5),
        'ln1_g': 1.0 + nrm((L, D), 0.02),
        'ln1_b': nrm((L, D), 0.02),
        'xa_wq': nrm((L, D, D), D ** -0.5),
        'xa_wk': nrm((L, D, D), D ** -0.5),
        'xa_wv': nrm((L, D, D), beta * D ** -0.5),
        'xa_wo': nrm((L, D, D), beta * D ** -0.5),
        'ln2_g': 1.0 + nrm((L, D), 0.02),
        'ln2_b': nrm((L, D), 0.02),
        'router_w': nrm((L, D, N_EXPERTS), D ** -0.5),
        'router_b': nrm((L, N_EXPERTS), 0.01),
        'exp_w_gu': nrm((L, N_EXPERTS, D, 2 * EXPERT_FF), beta * D ** -0.5),
        'exp_b_gu': nrm((L, N_EXPERTS, 2 * EXPERT_FF), 0.02),
        'exp_w_down': nrm((L, N_EXPERTS, EXPERT_FF, D), beta * EXPERT_FF ** -0.5),
        'exp_b_down': nrm((L, N_EXPERTS, D), 0.02),
        'ln3_g': 1.0 + nrm((L, D), 0.02),
        'ln3_b': nrm((L, D), 0.02),
    }


def reference(x, mem, w_in, b_in, gla_wg2, gla_bg, gla_norm_g, sgu_ln_g, sgu_ln_b, sgu_ws, sgu_bs,
              pool_w, pool_scale, w_up_a, w_up_b, w_up_c, w_o, ln1_g, ln1_b,
              xa_wq, xa_wk, xa_wv, xa_wo, ln2_g, ln2_b,
              router_w, router_b, exp_w_gu, exp_b_gu, exp_w_down, exp_b_down, ln3_g, ln3_b):
    alpha = DEEPNORM_ALPHA
    for l in range(DEPTH):
        h = hybrid_mixer(x, w_in[l], b_in[l], gla_wg2[l], gla_bg[l], gla_norm_g[l], sgu_ln_g[l], sgu_ln_b[l],
                         sgu_ws[l], sgu_bs[l], pool_w[l], pool_scale[l], w_up_a[l], w_up_b[l], w_up_c[l], w_o[l])
        x = layer_norm(alpha * x + h, ln1_g[l], ln1_b[l])
        h = memory_cross_attention(x, mem, xa_wq[l], xa_wk[l], xa_wv[l], xa_wo[l])
        x = layer_norm(alpha * x + h, ln2_g[l], ln2_b[l])
        h = moe_ffn(x, router_w[l], router_b[l], exp_w_gu[l], exp_b_gu[l], exp_w_down[l], exp_b_down[l])
        x = layer_norm(alpha * x + h, ln3_g[l], ln3_b[l])
    return x
```

```python
import numpy as np
import concourse.bass as bass
import concourse.mybir as mybir
from concourse.bass_utils import run_bass_kernel_spmd


F32 = mybir.dt.float32
BF16 = mybir.dt.bfloat16
I32 = mybir.dt.int32
U32 = mybir.dt.uint32
AF = mybir.ActivationFunctionType
ALU = mybir.AluOpType
AX = mybir.AxisListType

SEM_LIMIT = 4000


class Sched:
    ENGS = ("pe", "act", "dve", "pool", "sp")

    def __init__(self, nc):
        self.nc = nc
        self.ops = []
        self.last_w = {}
        self.readers = {}
        self.pending_barrier = {}
        self.last_op_eng = {}
        self.last_op_key = {}

    def op(self, eng, fn, reads=(), writes=(), key=None):
        idx = len(self.ops)
        deps = set()
        for r in reads:
            if r in self.last_w:
                deps.add(self.last_w[r])
        for w in writes:
            if w in self.last_w:
                deps.add(self.last_w[w])
            for i in self.readers.get(w, {}).values():
                deps.add(i)
        if eng in self.pending_barrier:
            deps |= self.pending_barrier.pop(eng)
        rec = dict(eng=eng, fn=fn, deps=deps, key=key, signal=False)
        self.ops.append(rec)
        if key is None:
            self.last_op_eng[eng] = idx
        else:
            self.last_op_key[key] = idx
        tag = eng if key is None else ("dma", key)
        for r in reads:
            self.readers.setdefault(r, {})[tag] = idx
        for w in writes:
            self.last_w[w] = idx
            self.readers[w] = {}
        return idx

    def barrier(self):
        deps = set(self.last_op_eng.values()) | set(self.last_op_key.values())
        for e in self.ENGS:
            self.pending_barrier[e] = set(deps) | self.pending_barrier.get(e, set())
        self.last_w = {}
        self.readers = {}

    def mm(self, out, lhsT, rhs, start, stop, reads, writes):
        self.op("pe", lambda e: e.matmul(out, lhsT, rhs, start=start, stop=stop), reads, writes)

    def tr(self, out, in_, ident, reads, writes):
        self.op("pe", lambda e: e.transpose(out, in_, ident), reads, writes)

    def dma(self, eng, out, in_, reads, writes, key, **kw):
        self.op(eng, lambda e: e.dma_start(out, in_, **kw), reads, writes, key=key)


    def act(self, out, in_, func, reads, writes, **kw):
        self.op("act", lambda e: e.activation(out, in_, func, **kw), reads, writes)

    def copy(self, eng, out, in_, reads, writes):
        if eng == "act":
            self.op("act", lambda e: e.copy(out, in_), reads, writes)
        else:
            self.op(eng, lambda e: e.tensor_copy(out, in_), reads, writes)

    def tt(self, eng, out, in0, in1, op, reads, writes):
        self.op(eng, lambda e: e.tensor_tensor(out, in0, in1, op), reads, writes)

    def ts(self, eng, out, in0, s1, s2, op0, op1, reads, writes):
        if op1 is None:
            self.op(eng, lambda e: e.tensor_scalar(out, in0, s1, None, op0), reads, writes)
        else:
            self.op(eng, lambda e: e.tensor_scalar(out, in0, s1, s2, op0, op1), reads, writes)

    def stt(self, out, in0, scalar, in1, op0, op1, reads, writes):
        self.op("dve", lambda e: e.scalar_tensor_tensor(out, in0, scalar, in1, op0, op1), reads, writes)

    def memset(self, eng, ap, val, writes):
        self.op(eng, lambda e: e.memset(ap, val), [], writes)

    def gen(self, eng, name, args, reads, writes, **kw):
        self.op(eng, lambda e: getattr(e, name)(*args, **kw), reads, writes)

    def emit(self):
        nc = self.nc
        ops = self.ops
        for o in ops:
            o["deps"] = {d for d in o["deps"] if not (ops[d]["eng"] == "pe" and o["eng"] == "pe" and ops[d]["key"] is None and o["key"] is None)}
            for d in o["deps"]:
                ops[d]["signal"] = True
        eng_sem = {}
        key_sem = {}
        waited = {e: {} for e in self.ENGS}
        per_eng = {e: [] for e in self.ENGS}
        nsem = 0
        for o in ops:
            e = o["eng"]
            waits = []
            for d in sorted(o["deps"]):
                od = ops[d]
                if od["key"] is not None:
                    sem, cnt = key_sem[od["key"]]
                    tk = (sem, cnt)
                else:
                    tk = od["ticket"]
                sem, val = tk
                sid = id(sem)
                if waited[e].get(sid, (None, 0))[1] >= val:
                    continue
                waited[e][sid] = (sem, val)
                waits.append((sem, val))
            m = {}
            for sem, val in waits:
                if id(sem) not in m or m[id(sem)][1] < val:
                    m[id(sem)] = (sem, val)
            waits = list(m.values())
            sig = None
            if o["key"] is not None:
                if o["key"] not in key_sem:
                    key_sem[o["key"]] = [nc.alloc_semaphore(f"k{nsem}"), 0]
                    nsem += 1
                ks = key_sem[o["key"]]
                ks[1] += 16
                sig = (ks[0], 16)
            elif o["signal"]:
                if e not in eng_sem or eng_sem[e][1] >= SEM_LIMIT:
                    eng_sem[e] = [nc.alloc_semaphore(f"e{nsem}"), 0]
                    nsem += 1
                es = eng_sem[e]
                es[1] += 1
                o["ticket"] = (es[0], es[1])
                sig = (es[0], 1)
            per_eng[e].append((waits, o["fn"], sig))
        self.nsem = nsem
        final_waits = [(s, c) for (s, c) in key_sem.values()]

        def run(engine, lst, final=False):
            for waits, fn, sig in lst:
                for sem, val in waits:
                    engine.wait_ge(sem, val)
                inst = fn(engine)
                if sig is not None:
                    inst.then_inc(sig[0], sig[1])
            if final:
                for sem, val in final_waits:
                    engine.wait_ge(sem, val)

        with nc.Block() as block:
            @block.tensor
            def _(eng):
                run(eng, per_eng["pe"])

            @block.scalar
            def _(eng):
                run(eng, per_eng["act"])

            @block.vector
            def _(eng):
                run(eng, per_eng["dve"])

            @block.gpsimd
            def _(eng):
                run(eng, per_eng["pool"])

            @block.sync
            def _(eng):
                run(eng, per_eng["sp"], final=True)
        return {e: len(per_eng[e]) for e in self.ENGS}


class PsumPool:
    def __init__(self, nc, names):
        self.tiles = [(n, nc.alloc_psum_tensor(n, [128, 512], F32)) for n in names]
        self.i = 0

    def get(self):
        n, t = self.tiles[self.i % len(self.tiles)]
        self.i += 1
        return n, t


class Arena:
    BASE = 16512
    TOP = 229376

    def __init__(self, nc):
        self.nc = nc
        self.off = self.BASE
        self.n = 0

    def reset(self, to=None):
        self.off = self.BASE if to is None else to

    def __call__(self, name, shape, dtype):
        n = 1
        for s in shape[1:]:
            n *= s
        nbytes = n * (4 if dtype in (F32, I32, U32) else 2)
        nbytes = (nbytes + 63) // 64 * 64
        assert self.off + nbytes <= self.TOP, (name, self.off, nbytes)
        t = self.nc.alloc_sbuf_tensor_at(f"{name}_{self.n}", shape, dtype, offset=self.off)
        self.n += 1
        self.off += nbytes
        return t


T = 4096
NQ = T // 512
NT = T // 128
ALPHA = 4.0 ** 0.25
LN_EPS = 1e-5
NE = 32
CAP = 768
BLOCKS = ((0, 3), (3, 6))
CB = 384
CQ, CK, CV, CR, CG, CUV, CXC, GATE0 = 0, 256, 512, 1024, 1536, 1552, 2064, 2320


class Ctx:
    pass


def ln_tail(S, X, y, stats, mv, rstd, g_bc, b_bc, o_t, tagp):
    ry, rst, ro = tagp
    for hh in range(2):
        S.gen("dve", "bn_stats", (stats[:, hh, :], y[:, hh * 512:(hh + 1) * 512]), [ry], [rst + ("s", hh)])
    S.gen("dve", "bn_aggr", (mv[:, :], stats[:, :, :].rearrange("p a b -> p (a b)")), [rst + ("s", 0), rst + ("s", 1)], [rst + ("mv",)])
    S.act(rstd[:, 0:1], mv[:, 1:2], AF.Ln, [rst + ("mv",), "eps"], [rst + ("r0",)], bias=X.eps[:, 0:1], scale=1.0)
    S.act(rstd[:, 1:2], rstd[:, 0:1], AF.Exp, [rst + ("r0",)], [rst + ("r1",)], scale=-0.5)
    S.ts("dve", y[:, :], y[:, :], mv[:, 0:1], rstd[:, 1:2], ALU.subtract, ALU.mult, [ry, rst + ("mv",), rst + ("r1",)], [ry])
    S.tt("pool", y[:, :], y[:, :], g_bc[:, :], ALU.mult, [ry, "gb"], [ry])
    S.tt("pool", o_t, y[:, :], b_bc[:, :], ALU.add, [ry, "gb2"], [ro])


def make_evac(S):
    cp_i = [0]

    def evac(out, in_, reads, writes):
        cp_i[0] += 1
        S.copy("act" if cp_i[0] % 2 else "dve", out, in_, reads, writes)
    return evac


def x_front(S, X, sb_x, src_d, q, xq, xbf, xT, evac):
    S.dma("sp", xq[:, :, :], src_d[q * 512:(q + 1) * 512, :].rearrange("(t p) d -> p t d", p=128), [], ["x"], key="xld")
    for tt in range(4):
        S.copy("pool", xbf[:, tt, :], xq[:, tt, :], ["x"], ["xbf"])
    for fc in range(8):
        pn, pst = X.PST[fc % 2]
        for tt in range(4):
            S.tr(pst[:, tt * 128:(tt + 1) * 128], xbf[:, tt, fc * 128:(fc + 1) * 128], X.ident[:, :], ["xbf", "ident"], [pn])
        evac(xT[:, fc, :], pst[:, 0:512], [pn], ["xT"])


def phase_1a(nc, S, X, sb, l, x_d, yT_d, D):
    PS = X.PS
    PST = X.PST
    evac = make_evac(S)
    win_d = D["w_in"][l]
    wA = sb("wA", [128, 8, 2320], BF16)
    brow = sb("brow_bf", [1, 1792], BF16)
    bqk = sb("bqk_sb", [64, 8], F32)
    bgl = sb("bgl_sb", [16, 1], F32)
    bxc = sb("bxc_sb", [128, 2], F32)
    wg2 = sb("wg2_bf", [16, 256], BF16)
    bg = sb("bg_bf", [1, 256], BF16)
    gn_bc = sb("gn_bc", [128, 512], F32)
    sg_bc = sb("sg_bc", [128, 256], F32)
    sb_bc = sb("sb_bc", [128, 256], F32)
    wsT32 = sb("wsT32", [128, 4, 128], F32)
    wsT = sb("wsT_bf", [128, 4, 128], BF16)
    bsT = sb("bsT_sb", [128, 4], F32)
    pw = sb("pw_bf", [128, 2, 128], BF16)
    psc = sb("psc_sb", [128, 2], F32)
    tri_i = sb("tri_i_sb", [128, 128], F32)
    tri_a = sb("tri_a_sb", [128, 128], F32)
    cm4 = sb("cm4", [128, 4, 128], F32)
    cmf = sb("cmf", [128, 128], F32)
    invc = sb("invc_sb", [128, 2, 16], F32)
    xq = sb("xq", [128, 4, 1024], F32)
    xbf = sb("xbf", [128, 4, 1024], BF16)
    xT = sb("xT", [128, 8, 512], BF16)
    qTs = sb("qTs", [64, 4, 512], F32)
    kTs = sb("kTs", [64, 4, 512], F32)
    glT = sb("glT", [16, 512], BF16)
    yTs = sb("yTs", [128, 8, 512], F32)
    xcb = [sb(f"xcb{c}", [128, 528], F32) for c in range(2)]
    pa = [sb(f"pa{c}", [128, 528], F32) for c in range(2)]
    pb = [sb(f"pb{c}", [128, 528], F32) for c in range(2)]
    ypre = sb("ypre", [128, 2, 512], BF16)
    ptmp = sb("ptmp", [128, 16], F32)
    e1 = sb("e1", [128, 256], F32)
    l32 = sb("l32", [128, 256], F32)
    ebT = sb("ebT", [64, 4, 128], F32)
    enbT = sb("enbT", [64, 4, 128], F32)
    erem = sb("erem", [128, 256], F32)
    ktl = sb("ktl", [128, 256], BF16)
    vbf = sb("vbf", [128, 512], BF16)
    qdA = sb("qdA", [64, 4, 128], BF16)
    qdB = sb("qdB", [64, 4, 128], BF16)
    kdT = sb("kdT", [64, 4, 128], BF16)
    scm = sb("scm", [128, 512], BF16)
    Sa = sb("Sa", [64, 512], F32)
    Sb_ = sb("Sb", [64, 512], F32)
    S0b = sb("S0b", [64, 512], BF16)
    S1b = sb("S1b", [64, 512], BF16)
    sr = sb("sr", [128, 512], F32)
    sgn = sb("sgn", [128, 512], F32)
    junk = sb("junk", [128, 128], F32)
    osb = sb("osb", [128, 512], F32)
    ssq = sb("ssq", [128, 4], F32)
    rs4 = sb("rs4", [128, 8], F32)
    yabf = sb("yabf", [128, 512], BF16)
    zz = sb("zz", [128, 512], F32)
    vn = sb("vn", [128, 256], F32)
    vnb = sb("vnb", [128, 256], BF16)
    ybbf = sb("ybbf", [128, 256], BF16)
    stats = sb("stats", [128, 6], F32)
    mv = sb("mv", [128, 2], F32)
    rstd = sb("rstd", [128, 2], F32)

    for j, (c0, c1) in enumerate(((0, 1024), (1024, 2048), (2048, 2320))):
        S.dma("pool", wA[:, :, c0:c1], win_d[:, c0:c1].rearrange("(k p) n -> p k n", p=128), [], [("wA", j)], key=("wA", j))
    WA = [("wA", 0), ("wA", 1), ("wA", 2)]
    CST = ["cst"]
    S.dma("pool", brow[:, :], D["brow"][l], [], CST, key="cst")
    S.dma("pool", wg2[:, :], D["wg2"][l], [], CST, key="cst")
    S.dma("pool", bg[:, :], D["bg"][l], [], CST, key="cst")
    S.dma("pool", pw[:, :, :], D["pwbd"][l], [], CST, key="cst")
    S.dma("sp", bqk[:, :], D["bqk"][l], [], CST, key="cst")
    S.dma("sp", bgl[:, :], D["bglow"][l], [], CST, key="cst")
    S.dma("sp", bxc[:, :], D["bxc"][l], [], CST, key="cst")
    S.dma("sp", gn_bc[:, :], D["gnorm"][l].partition_broadcast(128), [], CST, key="cst")
    S.dma("sp", sg_bc[:, :], D["sgu_g"][l].partition_broadcast(128), [], CST, key="cst")
    S.dma("sp", sb_bc[:, :], D["sgu_b"][l].partition_broadcast(128), [], CST, key="cst")
    S.dma("sp", wsT32[:, :, :], D["wsT"][l], [], CST, key="cst")
    S.dma("sp", bsT[:, :], D["bsT"][l], [], CST, key="cst")
    S.dma("sp", psc[:, :], D["pscT"][l], [], CST, key="cst")
    S.dma("sp", tri_i[:, :], D["tri_i"], [], CST, key="cst")
    S.dma("sp", tri_a[:, :], D["tri_a"], [], CST, key="cst")
    for h in range(4):
        S.dma("sp", cm4[:, h, :], D["cmask"], [], CST, key="cst")
    S.dma("sp", cmf[:, :], D["cmfull"], [], CST, key="cst")
    S.dma("sp", invc[:, :, :], D["invc"], [], CST, key="cst")
    S.memset("dve", Sa[:, :], 0.0, ["Sa"])
    S.memset("dve", qdA[:, :, :], 0.0, ["qdA"])
    S.memset("dve", qdB[:, :, :], 0.0, ["qdB"])
    for c in range(2):
        S.memset("pool", xcb[c][:, :], 0.0, [("xcb", c)])
    for g in range(4):
        S.tt("dve", wsT[:, g, :], wsT32[:, g, :], cmf[:, :], ALU.mult, CST, ["wsT"])
    S.copy("pool", S0b[:, :], Sa[:, :], ["Sa"], ["S0b"])

    def quad_front(q):
        x_front(S, X, sb, x_d, q, xq, xbf, xT, evac)
        pn, ps = PS.get()
        for kc in range(8):
            S.mm(ps[0:16, :], wA[:, kc, CG:CG + 16], xT[:, kc, :], kc == 0, kc == 7, WA + ["xT"], [pn])
        S.act(glT[:, :], ps[0:16, :], AF.Identity, [pn] + CST, ["glT"], bias=bgl[:, 0:1], scale=1.0)
        for h in range(4):
            pn, ps = PS.get()
            for kc in range(8):
                S.mm(ps[0:64, :], wA[:, kc, CQ + h * 64:CQ + (h + 1) * 64], xT[:, kc, :], kc == 0, kc == 7, WA + ["xT"], [pn])
            S.ts("dve", qTs[:, h, :], ps[0:64, :], bqk[:, h:h + 1], 0.125, ALU.add, ALU.mult, [pn] + CST, ["qTs"])
            pn, ps = PS.get()
            for kc in range(8):
                S.mm(ps[0:64, :], wA[:, kc, CK + h * 64:CK + (h + 1) * 64], xT[:, kc, :], kc == 0, kc == 7, WA + ["xT"], [pn])
            S.act(kTs[:, h, :], ps[0:64, :], AF.Identity, [pn] + CST, ["kTs"], bias=bqk[:, 4 + h:5 + h], scale=1.0)
        for c in range(2):
            rxc = ("xcb", c)
            if q > 0:
                S.copy("pool", xcb[c][:, 0:16], xcb[c][:, 512:528], [rxc], [rxc])
            pn, ps = PS.get()
            for kc in range(8):
                S.mm(ps[:, :], wA[:, kc, CXC + c * 128:CXC + (c + 1) * 128], xT[:, kc, :], kc == 0, kc == 7, WA + ["xT"], [pn])
            S.act(xcb[c][:, 16:528], ps[:, :], AF.Identity, [pn] + CST, [rxc], bias=bxc[:, c:c + 1], scale=1.0)

    def gla_tile(tt):
        tsl = slice(tt * 128, (tt + 1) * 128)
        pnk, psk = PS.get()
        for kc in range(8):
            S.mm(psk[:, 0:256], xT[:, kc, tsl], wA[:, kc, CK:CK + 256], kc == 0, False, WA + ["xT"], [pnk])
        S.mm(psk[:, 0:256], X.onesb[0:1, :], brow[0:1, 0:256], False, True, ["onesb"] + CST, [pnk])
        pnv, psv = PS.get()
        for kc in range(8):
            S.mm(psv[:, :], xT[:, kc, tsl], wA[:, kc, CV:CV + 512], kc == 0, False, WA + ["xT"], [pnv])
        S.mm(psv[:, :], X.onesb[0:1, :], brow[0:1, 256:768], False, True, ["onesb"] + CST, [pnv])
        evac(vbf[:, :], psv[:, :], [pnv], ["vbf"])
        pnz, psz = PS.get()
        S.mm(psz[:, 0:256], glT[0:16, tsl], wg2[0:16, :], True, False, ["glT"] + CST, [pnz])
        S.mm(psz[:, 0:256], X.onesb[0:1, :], bg[0:1, :], False, True, ["onesb"] + CST, [pnz])
        S.act(e1[:, :], psz[:, 0:256], AF.Exp, [pnz], ["e1"], scale=-1.0)
        S.act(l32[:, :], e1[:, :], AF.Ln, ["e1"], ["l32"], bias=1.0, scale=1.0)
        pnb, psb = PS.get()
        for h in range(4):
            S.mm(psb[0:64, h * 128:(h + 1) * 128], l32[:, h * 64:(h + 1) * 64], tri_i[:, :], True, True, ["l32"] + CST, [pnb])
        pnr, psr = PS.get()
        S.mm(psr[:, 0:256], tri_a[:, :], l32[:, :], True, True, ["l32"] + CST, [pnr])
        S.act(ebT[:, :, :].rearrange("p a b -> p (a b)"), psb[0:64, :], AF.Exp, [pnb], ["ebT"], scale=1.0)
        S.act(enbT[:, :, :].rearrange("p a b -> p (a b)"), psb[0:64, :], AF.Exp, [pnb], ["enbT"], scale=-1.0)
        S.act(erem[:, :], psr[:, 0:256], AF.Exp, [pnr], ["erem"], scale=1.0)
        S.tt("dve", ktl[:, :], psk[:, 0:256], erem[:, :], ALU.mult, [pnk, "erem"], ["ktl"])
        kvs = []
        for c in range(2):
            pn, ps = PS.get()
            for h in range(4):
                S.mm(ps[0:64, h * 128:(h + 1) * 128], ktl[c * 64:(c + 1) * 64, h * 64:(h + 1) * 64], vbf[c * 64:(c + 1) * 64, h * 128:(h + 1) * 128], True, True, ["ktl", "vbf"], [pn])
            kvs.append((pn, ps))
        S.tt("dve", qdA[:, :, 0:64], qTs[:, :, tt * 128:tt * 128 + 64], ebT[:, :, 0:64], ALU.mult, ["qTs", "ebT"], ["qdA"])
        S.tt("dve", qdB[:, :, 64:128], qTs[:, :, tt * 128 + 64:tt * 128 + 128], ebT[:, :, 64:128], ALU.mult, ["qTs", "ebT"], ["qdB"])
        S.tt("dve", kdT[:, :, :], kTs[:, :, tsl], enbT[:, :, :], ALU.mult, ["kTs", "enbT"], ["kdT"])
        pns, pss = PS.get()
        for h in range(4):
            S.mm(pss[:, h * 128:h * 128 + 64], kdT[:, h, :], qdA[:, h, 0:64], True, True, ["kdT", "qdA"], [pns])
            S.mm(pss[:, h * 128 + 64:h * 128 + 128], kdT[:, h, :], qdB[:, h, 64:128], True, True, ["kdT", "qdB"], [pns])
        S.tt("dve", scm[:, :], pss[:, :], cm4[:, :, :].rearrange("p a b -> p (a b)"), ALU.mult, [pns] + CST, ["scm"])
        for h in range(4):
            hs = slice(h * 128, (h + 1) * 128)
            S.stt(Sb_[:, hs], Sa[:, hs], ebT[:, h, 63:64], kvs[0][1][0:64, hs], ALU.mult, ALU.add, ["Sa", "ebT", kvs[0][0]], ["Sb"])
        S.copy("pool", S1b[:, :], Sb_[:, :], ["Sb"], ["S1b"])
        pno, pso = PS.get()
        for h in range(4):
            hs = slice(h * 128, (h + 1) * 128)
            S.mm(pso[:, hs], scm[:, hs], vbf[:, hs], True, False, ["scm", "vbf"], [pno])
            S.mm(pso[:, hs], qdA[:, h, :], S0b[:, hs], False, False, ["qdA", "S0b"], [pno])
            S.mm(pso[:, hs], qdB[:, h, :], S1b[:, hs], False, True, ["qdB", "S1b"], [pno])
        for h in range(4):
            hs = slice(h * 128, (h + 1) * 128)
            S.stt(Sa[:, hs], Sb_[:, hs], ebT[:, h, 127:128], kvs[1][1][0:64, hs], ALU.mult, ALU.add, ["Sb", "ebT", kvs[1][0]], ["Sa"])
        S.copy("pool", S0b[:, :], Sa[:, :], ["Sa"], ["S0b"])
        pnr2, psr2 = PS.get()
        for kc in range(8):
            S.mm(psr2[:, :], xT[:, kc, tsl], wA[:, kc, CR:CR + 512], kc == 0, False, WA + ["xT"], [pnr2])
        S.mm(psr2[:, :], X.onesb[0:1, :], brow[0:1, 768:1280], False, True, ["onesb"] + CST, [pnr2])
        S.act(sr[:, :], psr2[:, :], AF.Silu, [pnr2], ["sr"])
        S.tt("pool", sgn[:, :], sr[:, :], gn_bc[:, :], ALU.mult, ["sr"] + CST, ["sgn"])
        S.copy("act", osb[:, :], pso[:, :], [pno], ["osb"])
        for h in range(4):
            hs = slice(h * 128, (h + 1) * 128)
            S.op("dve", (lambda a, b, d: (lambda e: e.scalar_tensor_tensor(a, b, 1.0, b, ALU.mult, ALU.mult, accum_out=d)))(junk[:, :], osb[:, hs], ssq[:, h:h + 1]),
                 ["osb"], ["junk", ("ssq", h)])
        S.act(rs4[:, 0:4], ssq[:, :], AF.Ln, [("ssq", h) for h in range(4)] + ["eps"], ["rs4a"], bias=X.eps[:, 0:1], scale=1.0 / 128.0)
        S.act(rs4[:, 4:8], rs4[:, 0:4], AF.Exp, ["rs4a"], ["rs4b"], scale=-0.5)
        for h in range(4):
            hs = slice(h * 128, (h + 1) * 128)
            S.stt(yabf[:, hs], osb[:, hs], rs4[:, 4 + h:5 + h], sgn[:, hs], ALU.mult, ALU.mult, ["osb", "rs4b", "sgn"], ["yabf"])
        pn, pst = PST[0]
        for h in range(4):
            S.tr(pst[:, h * 128:(h + 1) * 128], yabf[:, h * 128:(h + 1) * 128], X.ident[:, :], ["yabf", "ident"], [pn])
        evac(yTs[:, 0:4, tsl], pst[:, 0:512].rearrange("p (a b) -> p a b", b=128), [pn], [("yTs", "a")])

    def sgu_tile(tt):
        tsl = slice(tt * 128, (tt + 1) * 128)
        pn, ps = PS.get()
        for kc in range(8):
            S.mm(ps[:, :], xT[:, kc, tsl], wA[:, kc, CUV:CUV + 512], kc == 0, False, WA + ["xT"], [pn])
        S.mm(ps[:, :], X.onesb[0:1, :], brow[0:1, 1280:1792], False, True, ["onesb"] + CST, [pn])
        S.act(zz[:, :], ps[:, :], AF.Gelu, [pn], ["zz"])
        S.gen("dve", "bn_stats", (stats[:, :], zz[:, 256:512]), ["zz"], ["stats"])
        S.gen("dve", "bn_aggr", (mv[:, :], stats[:, :]), ["stats"], ["mv"])
        S.act(rstd[:, 0:1], mv[:, 1:2], AF.Ln, ["mv", "eps"], ["rstd0"], bias=X.eps[:, 0:1], scale=1.0)
        S.act(rstd[:, 1:2], rstd[:, 0:1], AF.Exp, ["rstd0"], ["rstd1"], scale=-0.5)
        S.ts("dve", vn[:, :], zz[:, 256:512], mv[:, 0:1], rstd[:, 1:2], ALU.subtract, ALU.mult, ["zz", "mv", "rstd1"], ["vn"])
        S.tt("pool", vn[:, :], vn[:, :], sg_bc[:, :], ALU.mult, ["vn"] + CST, ["vn"])
        S.tt("pool", vnb[:, :], vn[:, :], sb_bc[:, :], ALU.add, ["vn"] + CST, ["vnb"])
        pn2, ps2 = PS.get()
        for g in range(4):
            S.mm(ps2[:, g * 64:(g + 1) * 64], wsT[:, g, :], vnb[:, g * 64:(g + 1) * 64], True, True, ["wsT", "vnb"], [pn2])
        for g in range(4):
            gs = slice(g * 64, (g + 1) * 64)
            S.stt(ybbf[:, gs], ps2[:, gs], bsT[:, g:g + 1], zz[:, gs], ALU.add, ALU.mult, [pn2, "zz"] + CST, ["ybbf"])
        pn, pst = PST[1]
        for c in range(2):
            S.tr(pst[:, c * 128:(c + 1) * 128], ybbf[:, c * 128:(c + 1) * 128], X.ident[:, :], ["ybbf", "ident"], [pn])
        evac(yTs[:, 4:6, tsl], pst[:, 0:256].rearrange("p (a b) -> p a b", b=128), [pn], [("yTs", "b")])

    def pool_quad(q):
        for c in range(2):
            rxc = ("xcb", c)
            Xc = xcb[c]
            A = pa[c]
            B = pb[c]
            S.tt("pool", A[:, 1:528], Xc[:, 1:528], Xc[:, 0:527], ALU.add, [rxc], [("pa", c)])
            S.tt("pool", B[:, 3:528], A[:, 3:528], A[:, 1:526], ALU.add, [("pa", c)], [("pb", c)])
            if c == 0:
                S.stt(ypre[0:64, 0, :], A[0:64, 16:528], 0.5, Xc[0:64, 16:528], ALU.mult, ALU.subtract, [("pa", c), rxc], [("ypre", 0)])
                S.stt(ypre[64:128, 0, :], B[64:128, 16:528], 0.25, Xc[64:128, 16:528], ALU.mult, ALU.subtract, [("pb", c), rxc], [("ypre", 0)])
            else:
                S.tt("pool", A[:, 7:528], B[:, 7:528], B[:, 3:524], ALU.add, [("pb", c)], [("pa", c)])
                S.tt("pool", B[64:128, 15:528], A[64:128, 15:528], A[64:128, 7:520], ALU.add, [("pa", c)], [("pb", c)])
                S.stt(ypre[0:64, 1, :], A[0:64, 16:528], 0.125, Xc[0:64, 16:528], ALU.mult, ALU.subtract, [("pa", c), rxc], [("ypre", 1)])
                S.stt(ypre[64:128, 1, :], B[64:128, 16:528], 0.0625, Xc[64:128, 16:528], ALU.mult, ALU.subtract, [("pb", c), rxc], [("ypre", 1)])
            if q == 0:
                for (Z, p0, p1, nm) in ((A, 0, 64, "pa"), (B, 64, 128, "pb")):
                    S.tt("dve", ptmp[p0:p1, :], Z[p0:p1, 16:32], invc[p0:p1, c, :], ALU.mult, [(nm, c)] + CST, ["ptmp"])
                    S.tt("dve", ypre[p0:p1, c, 0:16], ptmp[p0:p1, :], Xc[p0:p1, 16:32], ALU.subtract, ["ptmp", rxc], [("ypre", c)])
            pn, ps = PS.get()
            S.mm(ps[:, :], pw[:, c, :], ypre[:, c, :], True, True, [("ypre", c)] + CST, [pn])
            S.ts("dve", yTs[:, 6 + c, :], ps[:, :], psc[:, c:c + 1], None, ALU.mult, None, [pn] + CST, [("yTs", "c", c)])

    for q in range(NQ):
        quad_front(q)
        for tt in range(4):
            gla_tile(tt)
            sgu_tile(tt)
        pool_quad(q)
        S.dma("sp", yT_d[:, q * 512:(q + 1) * 512].rearrange("(k p) t -> p k t", p=128), yTs[:, :, :],
              [("yTs", "a"), ("yTs", "b"), ("yTs", "c", 0), ("yTs", "c", 1)], [], key="yst")


def phase_1b(nc, S, X, sb, l, x_d, yT_d, y_d, D):
    PS = X.PS
    evac = make_evac(S)
    win_d = D["w_in"][l]
    wg = sb("wg", [128, 8, 3072], BF16)
    wup = sb("wup", [128, 8, 1024], BF16)
    wo = sb("wo", [128, 8, 1024], BF16)
    bgate = sb("bgate", [128, 24], F32)
    g_bc = sb("g_bc", [128, 1024], F32)
    b_bc = sb("b_bc", [128, 1024], F32)
    xq = sb("xq", [128, 4, 1024], F32)
    xbf = sb("xbf", [128, 4, 1024], BF16)
    xT = sb("xT", [128, 8, 512], BF16)
    yTq = sb("yTq", [128, 8, 512], BF16)
    sg = [sb(f"sg{j}", [128, 512], F32) for j in range(3)]
    mm_ = [sb(f"m{j}", [128, 512], F32) for j in range(3)]
    mT = sb("mT", [128, 8, 512], BF16)
    yb = [sb(f"yb{i}", [128, 1024], F32) for i in range(2)]
    ob = [sb(f"ob{i}", [128, 1024], F32) for i in range(2)]
    stats = [sb(f"st{i}", [128, 2, 6], F32) for i in range(2)]
    mv = [sb(f"mv{i}", [128, 2], F32) for i in range(2)]
    rstd = [sb(f"rstd{i}", [128, 2], F32) for i in range(2)]

    for j in range(3):
        S.dma("pool", wg[:, :, j * 1024:(j + 1) * 1024], win_d[:, GATE0 + j * 1024:GATE0 + (j + 1) * 1024].rearrange("(k p) n -> p k n", p=128), [], [("wg", j)], key=("wA", j))
    for r0, r1, nm in ((0, 4, "w_up_a"), (4, 6, "w_up_b"), (6, 8, "w_up_c")):
        S.dma("pool", wup[:, r0:r1, :], D[nm][l].rearrange("(k p) n -> p k n", p=128), [], [("wup", r0)], key="wup")
    WUP = [("wup", 0), ("wup", 4), ("wup", 6)]
    S.dma("pool", wo[:, :, :], D["w_o"][l].rearrange("(k p) n -> p k n", p=128), [], ["wo"], key="wo")
    S.dma("sp", bgate[:, :], D["b_gateT"][l], [], ["bgate"], key="cst")
    S.dma("sp", g_bc[:, :], D["ln1_g"][l].partition_broadcast(128), [], ["gb"], key="cst")
    S.dma("sp", b_bc[:, :], D["ln1_b"][l].partition_broadcast(128), [], ["gb2"], key="cst")

    for q in range(NQ):
        x_front(S, X, sb, x_d, q, xq, xbf, xT, evac)
        S.dma("pool", yTq[:, :, :], yT_d[:, q * 512:(q + 1) * 512].rearrange("(k p) t -> p k t", p=128), [], ["yTq"], key="yld")
        for fo in range(8):
            fsl = slice(fo * 128, (fo + 1) * 128)
            ups = []
            for (k0, k1) in ((0, 4), (4, 6), (6, 8)):
                pn, ps = PS.get()
                for kc in range(k0, k1):
                    S.mm(ps[:, :], wup[:, kc, fsl], yTq[:, kc, :], kc == k0, kc == k1 - 1, WUP + ["yTq"], [pn])
                ups.append((pn, ps))
            for j in range(3):
                pn, ps = PS.get()
                for kc in range(8):
                    S.mm(ps[:, :], wg[:, kc, j * 1024 + fo * 128:j * 1024 + (fo + 1) * 128], xT[:, kc, :], kc == 0, kc == 7, [("wg", j), "xT"], [pn])
                S.act(sg[j][:, :], ps[:, :], AF.Identity, [pn, "bgate"], [("sg", j)], bias=bgate[:, j * 8 + fo:j * 8 + fo + 1], scale=1.0)
                S.act(sg[j][:, :], sg[j][:, :], AF.Sigmoid, [("sg", j)], [("sg", j)], scale=1.0)
                S.tt("dve", mm_[j][:, :], ups[j][1][:, :], sg[j][:, :], ALU.mult, [ups[j][0], ("sg", j)], [("m", j)])
            S.tt("pool", mm_[0][:, :], mm_[0][:, :], mm_[1][:, :], ALU.add, [("m", 0), ("m", 1)], [("m", 0)])
            S.tt("pool", mT[:, fo, :], mm_[0][:, :], mm_[2][:, :], ALU.add, [("m", 0), ("m", 2)], ["mT"])
        for tt in range(4):
            i = tt % 2
            ry = ("y", i)
            for hf in range(2):
                pn, ps = PS.get()
                for kc in range(8):
                    S.mm(ps[:, :], mT[:, kc, tt * 128:(tt + 1) * 128], wo[:, kc, hf * 512:(hf + 1) * 512], kc == 0, kc == 7, ["mT", "wo"], [pn])
                S.stt(yb[i][:, hf * 512:(hf + 1) * 512], xq[:, tt, hf * 512:(hf + 1) * 512], ALPHA, ps[:, :], ALU.mult, ALU.add, [pn, "x"], [ry])
            ln_tail(S, X, yb[i], stats[i], mv[i], rstd[i], g_bc, b_bc, ob[i][:, :], (ry, ("st", i), ("o", i)))
            tok0 = q * 512 + tt * 128
            S.dma("sp", y_d[tok0:tok0 + 128, :], ob[i][:, :], [("o", i)], [], key=("yst", i))


def phase_2(nc, S, X, sb, l, x_d, y_d, D):
    PS = X.PS
    evac = make_evac(S)
    w_bf = {n: sb(n + "_bf", [128, 8, 1024], BF16) for n in ("xa_wq", "xa_wk", "xa_wv", "xa_wo")}
    memT_bf = sb("memT_bf", [128, 8, 256], BF16)
    kT_bf = sb("kT_bf", [128, 8, 256], BF16)
    V_bf = sb("V_bf", [128, 2, 1024], BF16)
    g_bc = sb("g_bc", [128, 1024], F32)
    b_bc = sb("b_bc", [128, 1024], F32)
    xq = [sb(f"xq{i}", [128, 4, 1024], F32) for i in range(2)]
    xbf = sb("xbf", [128, 4, 1024], BF16)
    xT = sb("xT", [128, 8, 512], BF16)
    qT = sb("qT", [128, 8, 512], BF16)
    PT = [sb(f"PT{i}", [128, 2, 512], BF16) for i in range(2)]
    rs = sb("rs", [128, 512], F32)
    oT = sb("oT", [128, 8, 512], BF16)
    yb = [sb(f"yb{i}", [128, 1024], F32) for i in range(2)]
    ob = [sb(f"ob{i}", [128, 1024], F32) for i in range(2)]
    stats = [sb(f"st{i}", [128, 2, 6], F32) for i in range(2)]
    mv = [sb(f"mv{i}", [128, 2], F32) for i in range(2)]
    rstd = [sb(f"rstd{i}", [128, 2], F32) for i in range(2)]
    for n in w_bf:
        S.dma("pool", w_bf[n][:, :, :], D[n][l].rearrange("(k p) n -> p k n", p=128), [], [n], key=n)
    S.dma("pool", memT_bf[:, :, :], D["memT"].rearrange("(k p) n -> p k n", p=128), [], ["memT"], key="cst")
    S.dma("sp", g_bc[:, :], D["ln2_g"][l].partition_broadcast(128), [], ["gb"], key="cst")
    S.dma("sp", b_bc[:, :], D["ln2_b"][l].partition_broadcast(128), [], ["gb2"], key="cst")
    for fc in range(8):
        pn, ps = PS.get()
        for kc in range(8):
            S.mm(ps[:, 0:256], w_bf["xa_wk"][:, kc, fc * 128:(fc + 1) * 128], memT_bf[:, kc, :], kc == 0, kc == 7, ["xa_wk", "memT"], [pn])
        evac(kT_bf[:, fc, :], ps[:, 0:256], [pn], ["kT"])
    for mc in range(2):
        for hf in range(2):
            pn, ps = PS.get()
            for kc in range(8):
                S.mm(ps[:, :], memT_bf[:, kc, mc * 128:(mc + 1) * 128], w_bf["xa_wv"][:, kc, hf * 512:(hf + 1) * 512], kc == 0, kc == 7, ["xa_wv", "memT"], [pn])
            evac(V_bf[:, mc, hf * 512:(hf + 1) * 512], ps[:, :], [pn], ["V"])
    for q in range(NQ):
        xb = xq[q % 2]
        rx = ("x", q % 2)
        S.dma("sp", xb[:, :, :], x_d[q * 512:(q + 1) * 512, :].rearrange("(t p) d -> p t d", p=128), [], [rx], key=("xld2", q % 2))
        for tt in range(4):
            S.copy("pool", xbf[:, tt, :], xb[:, tt, :], [rx], ["xbf"])
        for fc in range(8):
            pn, pst = X.PST[fc % 2]
            for tt in range(4):
                S.tr(pst[:, tt * 128:(tt + 1) * 128], xbf[:, tt, fc * 128:(fc + 1) * 128], X.ident[:, :], ["xbf", "ident"], [pn])
            evac(xT[:, fc, :], pst[:, 0:512], [pn], ["xT"])
        for fc in range(8):
            pn, ps = PS.get()
            for kc in range(8):
                S.mm(ps[:, :], w_bf["xa_wq"][:, kc, fc * 128:(fc + 1) * 128], xT[:, kc, :], kc == 0, kc == 7, ["xa_wq", "xT"], [pn])
            evac(qT[:, fc, :], ps[:, :], [pn], ["qT"])
        for h in range(4):
            pt = PT[h % 2]
            rpt = ("PT", h % 2)
            for mc in range(2):
                pn, ps = PS.get()
                for dc in range(2):
                    S.mm(ps[:, :], kT_bf[:, h * 2 + dc, mc * 128:(mc + 1) * 128], qT[:, h * 2 + dc, :], dc == 0, dc == 1, ["kT", "qT"], [pn])
                S.act(pt[:, mc, :], ps[:, :], AF.Exp, [pn], [rpt], scale=1.0 / 16.0)
            pn, ps = PS.get()
            for mc in range(2):
                S.mm(ps[:, :], X.ones128[:, :], pt[:, mc, :], mc == 0, mc == 1, ["ones128", rpt], [pn])
            S.gen("dve", "reciprocal", (rs[:, :], ps[:, :]), [pn], ["rs"])
            for dc in range(2):
                pn, ps = PS.get()
                for mc in range(2):
                    S.mm(ps[:, :], V_bf[:, mc, h * 256 + dc * 128:h * 256 + (dc + 1) * 128], pt[:, mc, :], mc == 0, mc == 1, ["V", rpt], [pn])
                S.tt("dve", oT[:, h * 2 + dc, :], ps[:, :], rs[:, :], ALU.mult, [pn, "rs"], ["oT"])
        for tt in range(4):
            i = tt % 2
            ry = ("y", i)
            for hf in range(2):
                pn, ps = PS.get()
                for kc in range(8):
                    S.mm(ps[:, :], oT[:, kc, tt * 128:(tt + 1) * 128], w_bf["xa_wo"][:, kc, hf * 512:(hf + 1) * 512], kc == 0, kc == 7, ["oT", "xa_wo"], [pn])
                S.stt(yb[i][:, hf * 512:(hf + 1) * 512], xb[:, tt, hf * 512:(hf + 1) * 512], ALPHA, ps[:, :], ALU.mult, ALU.add, [pn, rx], [ry])
            ln_tail(S, X, yb[i], stats[i], mv[i], rstd[i], g_bc, b_bc, ob[i][:, :], (ry, ("st", i), ("o", i)))
            tok0 = q * 512 + tt * 128
            S.dma("sp", y_d[tok0:tok0 + 128, :], ob[i][:, :], [("o", i)], [], key=("yst", i))


def phase_3(nc, S, X, sb, l, x_d, y_d, xg_d, yg_d, D):
    PS = X.PS
    PST = X.PST
    evac = make_evac(S)
    wgu_d = D["exp_w_gu"][l]
    wd_d = D["exp_w_down"][l]
    bd_d = D["exp_b_down"][l]
    X.phase_base = sb.off
    wgu = [sb(f"wgu{i}", [128, 8, 2048], BF16) for i in range(2)]
    wd = [sb(f"wd{i}", [128, 8, 1024], BF16) for i in range(2)]
    X.p3_weights_end = sb.off
    bd = [sb(f"bd{i}", [1, 1024], BF16) for i in range(2)]
    bgu = sb("bgu", [128, NE * 16], F32)
    rw = sb("rw", [128, 8, NE], F32)
    rb = sb("rb", [1, NE], F32)
    ident32 = sb("ident32", [128, 128], F32)
    su = sb("su_bf", [128, 128], BF16)
    ones32 = sb("ones32", [1, 128], F32)
    ones4 = sb("ones4", [128, 4], F32)
    eoff = sb("eoff_sb", [128, NE], F32)
    g_bc = sb("g_bc", [128, 1024], F32)
    b_bc = sb("b_bc", [128, 1024], F32)
    xt = sb("xt", [128, 1024], F32)
    xbf = [sb(f"xbf{i}", [128, 1024], BF16) for i in range(2)]
    xT32 = sb("xT32", [128, 8, 128], F32)
    lg = sb("lg", [128, NE], F32)
    work = sb("work", [128, NE], F32)
    tmx = sb("tmx", [128, 32], F32)
    mk = sb("mk", [128, 4], F32)
    ohs = sb("ohs", [128, 4, NE], F32)
    num4 = sb("num4", [128, 4], F32)
    negm = sb("negm", [128, 1], F32)
    ex = sb("ex", [128, NE], F32)
    den = sb("den", [128, 2], F32)
    maskf = sb("maskf", [128, NE], F32)
    maskb = sb("maskb", [128, NT, NE], BF16)
    destf = sb("destf", [128, NE], F32)
    junk = sb("junk", [128, NE], F32)
    d4f = sb("d4f", [128, NT, 4], F32)
    d4i = sb("d4i", [128, NT, 4], I32)
    g4 = sb("g4", [128, NT, 4], F32)
    xgs = sb("xgs", [128, 3, 1024], BF16)
    xgT = [sb(f"xgT{i}", [128, 8, CB], BF16) for i in range(2)]
    aT = [sb(f"aT{i}", [128, 8, CB], BF16) for i in range(2)]
    tg = sb("tg", [128, CB], F32)
    tsg = sb("tsg", [128, CB], F32)
    tl0 = sb("tl0", [128, CB], F32)
    tl1 = sb("tl1", [128, CB], F32)
    tgs = sb("tgs", [128, CB], F32)
    ysb = [sb(f"ysb{i}", [128, 1024], F32) for i in range(2)]
    stats = sb("stats", [128, 2, 6], F32)
    mv = sb("mv", [128, 2], F32)
    rstd = sb("rstd", [128, 2], F32)

    S.dma("sp", ident32[:, :], D["ident"], [], ["id32"], key="cst")
    S.dma("pool", su[:, :], D["su"], [], ["su"], key="cst")
    S.dma("sp", eoff[:, :], D["eoff"], [], ["eoff"], key="cst")
    S.dma("sp", rw[:, :, :], D["router_w"][l].rearrange("(k p) n -> p k n", p=128), [], ["rw"], key="cst")
    S.dma("sp", rb[:, :], D["router_b"][l], [], ["rb"], key="cst")
    S.dma("sp", bgu[:, :], D["b_guT"][l], [], ["bgu"], key="cst")
    S.dma("sp", g_bc[:, :], D["ln3_g"][l].partition_broadcast(128), [], ["gb"], key="cst")
    S.dma("sp", b_bc[:, :], D["ln3_b"][l].partition_broadcast(128), [], ["gb2"], key="cst")
    S.memset("dve", ones32[:, :], 1.0, ["ones32"])
    S.memset("dve", ones4[:, :], 1.0, ["ones4"])

    def load_expert(e):
        i = e % 2
        S.dma("pool", wgu[i][:, :, :], wgu_d[e].rearrange("(k p) n -> p k n", p=128), [], [("wgu", i, 0), ("wgu", i, 1)], key=("wgu", i))
        S.dma("pool", wd[i][:, :, :], wd_d[e].rearrange("(k p) n -> p k n", p=128), [], [("wd", i)], key=("wd", i))
        S.dma("pool", bd[i][:, :], bd_d[e:e + 1, :], [], [("bd", i)], key=("bd", i))

    load_expert(0)
    load_expert(1)

    def stt_acc(out, in0, in1, accum, reads, writes):
        S.op("dve", (lambda a, b, c, d: (lambda e: e.scalar_tensor_tensor(a, b, 1.0, c, ALU.mult, ALU.mult, accum_out=d)))(out, in0, in1, accum), reads, writes)

    for j in range(NT):
        S.dma("sp", xt[:, :], x_d[j * 128:(j + 1) * 128, :], [], ["xt"], key="xt")
        xb = xbf[j % 2]
        rxb = ("xbf", j % 2)
        S.copy("pool", xb[:, :], xt[:, :], ["xt"], [rxb])
        for fc in range(8):
            pn, ps = PS.get()
            S.tr(ps[:, 0:128], xt[:, fc * 128:(fc + 1) * 128], ident32[:, :], ["xt", "id32"], [pn])
            evac(xT32[:, fc, :], ps[:, 0:128], [pn], ["xT32"])
        pn, ps = PS.get()
        for kc in range(8):
            S.mm(ps[:, 0:NE], xT32[:, kc, :], rw[:, kc, :], kc == 0, False, ["xT32", "rw"], [pn])
        S.mm(ps[:, 0:NE], ones32[0:1, :], rb[0:1, :], False, True, ["ones32", "rb"], [pn])
        S.copy("dve", lg[:, :], ps[:, 0:NE], [pn], ["lg"])
        S.copy("dve", work[:, :], lg[:, :], ["lg"], ["work"])
        for k in range(4):
            S.tt("dve", tmx[:, 0:16], work[:, 0:16], work[:, 16:32], ALU.max, ["work"], ["tmx"])
            S.tt("dve", tmx[:, 16:24], tmx[:, 0:8], tmx[:, 8:16], ALU.max, ["tmx"], ["tmx"])
            S.tt("dve", tmx[:, 24:28], tmx[:, 16:20], tmx[:, 20:24], ALU.max, ["tmx"], ["tmx"])
            S.tt("dve", tmx[:, 28:30], tmx[:, 24:26], tmx[:, 26:28], ALU.max, ["tmx"], ["tmx"])
            S.tt("dve", mk[:, k:k + 1], tmx[:, 28:29], tmx[:, 29:30], ALU.max, ["tmx"], ["mk"])
            S.ts("dve", ohs[:, k, :], work[:, :], mk[:, k:k + 1], None, ALU.is_equal, None, ["work", "mk"], ["ohs"])
            S.stt(work[:, :], ohs[:, k, :], -1e30, work[:, :], ALU.mult, ALU.add, ["ohs", "work"], ["work"])
        S.ts("dve", maskf[:, :], lg[:, :], mk[:, 3:4], None, ALU.is_ge, None, ["lg", "mk"], ["maskf"])
        S.copy("dve", maskb[:, j, :], maskf[:, :], ["maskf"], [("maskb", j)])
        S.ts("dve", negm[:, :], mk[:, 0:1], -1.0, None, ALU.mult, None, ["mk"], ["negm"])
        S.act(ex[:, :], lg[:, :], AF.Exp, ["lg", "negm"], ["ex"], bias=negm[:, 0:1], scale=1.0)
        pn, ps = PS.get()
        for i in range(j):
            S.mm(ps[:, 0:NE], X.ones128[:, :], maskb[:, i, :], i == 0, False, ["ones128", ("maskb", i)], [pn])
        S.mm(ps[:, 0:NE], su[:, :], maskb[:, j, :], j == 0, True, ["su", ("maskb", j)], [pn])
        S.ts("dve", destf[:, :], ps[:, 0:NE], float(CAP - 1), None, ALU.min, None, [pn], ["destf"])
        S.tt("dve", destf[:, :], destf[:, :], eoff[:, :], ALU.add, ["destf", "eoff"], ["destf"])
        for k in range(4):
            stt_acc(junk[:, :], ohs[:, k, :], destf[:, :], d4f[:, j, k:k + 1], ["ohs", "destf"], ["junk", ("d4f", j)])
            stt_acc(junk[:, :], ohs[:, k, :], ex[:, :], num4[:, k:k + 1], ["ohs", "ex"], ["junk", "num4"])
        stt_acc(junk[:, 0:4], num4[:, :], ones4[:, :], den[:, 0:1], ["num4", "ones4"], ["junk", "den0"])
        S.gen("dve", "reciprocal", (den[:, 1:2], den[:, 0:1]), ["den0"], ["den1"])
        S.ts("dve", g4[:, j, :], num4[:, :], den[:, 1:2], None, ALU.mult, None, ["num4", "den1"], [("g4", j)])
        S.copy("dve", d4i[:, j, :], d4f[:, j, :], [("d4f", j)], [("d4i", j)])
        for k in range(4):
            S.op("pool", (lambda o_, i_, idx: (lambda e: e.indirect_dma_start(o_, bass.IndirectOffsetOnAxis(ap=idx, axis=0), i_, None)))(xg_d[:, :], xb[:, :], d4i[:, j, k:k + 1]),
                 [rxb, ("d4i", j)], [("xg", j, k)], key="xgsc")

    XG_ALL = [("xg", j, k) for j in range(NT) for k in range(4)]
    YG_ALL = []
    for e in range(NE):
        i = e % 2
        for bi, (t0, t1) in enumerate(BLOCKS):
            nst = t1 - t0
            cb = nst * 128
            bb = (e * len(BLOCKS) + bi) % 2
            for st in range(nst):
                r0 = e * CAP + (t0 + st) * 128
                S.dma("sp", xgs[:, st, :], xg_d[r0:r0 + 128, :], XG_ALL, [("xgs", st)], key=("xgs", st))
            for fc in range(8):
                pn, pst = PST[fc % 2]
                for st in range(nst):
                    S.tr(pst[:, st * 128:(st + 1) * 128], xgs[:, st, fc * 128:(fc + 1) * 128], X.ident[:, :], [("xgs", st), "ident"], [pn])
                evac(xgT[bb][:, fc, 0:cb], pst[:, 0:cb], [pn], [("xgT", bb)])
            for jj in range(8):
                png, psg = PS.get()
                for kc in range(8):
                    S.mm(psg[:, 0:cb], wgu[i][:, kc, jj * 128:(jj + 1) * 128], xgT[bb][:, kc, 0:cb], kc == 0, kc == 7, [("wgu", i, 0), ("xgT", bb)], [png])
                pnl, psl = PS.get()
                for kc in range(8):
                    S.mm(psl[:, 0:cb], wgu[i][:, kc, 1024 + jj * 128:1024 + (jj + 1) * 128], xgT[bb][:, kc, 0:cb], kc == 0, kc == 7, [("wgu", i, 1), ("xgT", bb)], [pnl])
                cg = e * 16 + jj
                cl = e * 16 + 8 + jj
                S.ts("dve", tg[:, 0:cb], psg[:, 0:cb], bgu[:, cg:cg + 1], 7.0, ALU.add, ALU.min, [png, "bgu"], ["tg"])
                S.act(tsg[:, 0:cb], tg[:, 0:cb], AF.Sigmoid, ["tg"], ["tsg"], scale=1.702)
                S.act(tl0[:, 0:cb], psl[:, 0:cb], AF.Identity, [pnl, "bgu"], ["tl0"], bias=bgu[:, cl:cl + 1], scale=1.0)
                S.ts("dve", tl1[:, 0:cb], tl0[:, 0:cb], -7.0, 7.0, ALU.max, ALU.min, ["tl0"], ["tl1"])
                S.tt("dve", tgs[:, 0:cb], tg[:, 0:cb], tsg[:, 0:cb], ALU.mult, ["tg", "tsg"], ["tgs"])
                S.stt(aT[bb][:, jj, 0:cb], tl1[:, 0:cb], 1.0, tgs[:, 0:cb], ALU.add, ALU.mult, ["tl1", "tgs"], [("aT", bb)])
            for st in range(nst):
                yi = st % 2
                for hf in range(2):
                    pn, ps = PS.get()
                    for kc in range(8):
                        S.mm(ps[:, :], aT[bb][:, kc, st * 128:(st + 1) * 128], wd[i][:, kc, hf * 512:(hf + 1) * 512], kc == 0, False, [("aT", bb), ("wd", i)], [pn])
                    S.mm(ps[:, :], X.onesb[0:1, :], bd[i][0:1, hf * 512:(hf + 1) * 512], False, True, ["onesb", ("bd", i)], [pn])
                    evac(ysb[yi][:, hf * 512:(hf + 1) * 512], ps[:, :], [pn], [("ysb", yi)])
                r0 = e * CAP + (t0 + st) * 128
                S.dma("sp", yg_d[r0:r0 + 128, :], ysb[yi][:, :], [("ysb", yi)], [("yg", e, t0 + st)], key="ygst")
                YG_ALL.append(("yg", e, t0 + st))
        if e + 2 < NE:
            load_expert(e + 2)

    S.barrier()
    keep = sb.off
    sb.reset(X.phase_base)
    yk = [[sb(f"yk{b}{i}", [128, 1024], F32) for i in range(4)] for b in range(2)]
    acc2 = [sb(f"acc{b}", [128, 1024], F32) for b in range(2)]
    ob2 = [sb(f"ob{b}", [128, 1024], F32) for b in range(2)]
    xt2 = [sb(f"xt{b}", [128, 1024], F32) for b in range(2)]
    st2 = [sb(f"stc{b}", [128, 2, 6], F32) for b in range(2)]
    mv2 = [sb(f"mvc{b}", [128, 2], F32) for b in range(2)]
    rs2 = [sb(f"rsc{b}", [128, 2], F32) for b in range(2)]
    assert sb.off <= X.p3_weights_end
    sb.reset(keep)
    for j in range(NT):
        b = j % 2
        S.dma("sp", xt2[b][:, :], x_d[j * 128:(j + 1) * 128, :], [], [("xt", b)], key=("xtc", b))
        for k in range(4):
            S.op("pool", (lambda o_, i_, idx: (lambda e: e.indirect_dma_start(o_, None, i_, bass.IndirectOffsetOnAxis(ap=idx, axis=0))))(yk[b][k][:, :], yg_d[:, :], d4i[:, j, k:k + 1]),
                 [], [("yk", b, k)], key=("ykg", b, k))
        S.ts("dve", acc2[b][:, :], yk[b][0][:, :], g4[:, j, 0:1], None, ALU.mult, None, [("yk", b, 0)], [("acc", b)])
        for k in range(1, 4):
            S.stt(acc2[b][:, :], yk[b][k][:, :], g4[:, j, k:k + 1], acc2[b][:, :], ALU.mult, ALU.add, [("yk", b, k), ("acc", b)], [("acc", b)])
        S.stt(acc2[b][:, :], xt2[b][:, :], ALPHA, acc2[b][:, :], ALU.mult, ALU.add, [("xt", b), ("acc", b)], [("acc", b)])
        ln_tail(S, X, acc2[b], st2[b], mv2[b], rs2[b], g_bc, b_bc, ob2[b][:, :], (("acc", b), ("stc", b), ("ob", b)))
        S.dma("sp", y_d[j * 128:(j + 1) * 128, :], ob2[b][:, :], [("ob", b)], [], key=("yst3", b))


def consts():
    s = np.arange(128)[:, None]; t = np.arange(128)[None, :]
    same = (s // 64) == (t // 64)
    c = {}
    c['ident'] = np.eye(128, dtype=np.float32)
    c['tri_i'] = np.where((s <= t) & same, -1.0 / 16.0, 0.0).astype(np.float32)
    c['tri_a'] = np.where((s > t) & same, -1.0 / 16.0, 0.0).astype(np.float32)
    c['cmask'] = ((s <= t) & same).astype(np.float32)
    c['cmfull'] = (s <= t).astype(np.float32)
    c['su'] = (s < t).astype(np.float32)
    return c

def invc_for(first_half):
    out = np.zeros((128, 2, 16), np.float32)
    tt = np.arange(16)
    for gi, w in enumerate((2, 4, 8, 16)):
        cc, p0 = gi // 2, (gi % 2) * 64
        cnt = np.minimum(tt + 1, w) if first_half else np.full(16, w)
        out[p0:p0 + 64, cc, :] = (1.0 / cnt).astype(np.float32)[None, :]
    return out

def mixer_a_inputs(d, l):
    b_in = d['b_in'][l]
    m = {}
    m['w_in'] = d['w_in'][l]
    m['bqk'] = np.ascontiguousarray(np.concatenate([b_in[0:256].reshape(4, 64).T, b_in[256:512].reshape(4, 64).T], axis=1))
    m['bglow'] = np.ascontiguousarray(b_in[1536:1552].reshape(16, 1))
    m['brow'] = np.ascontiguousarray(np.concatenate([b_in[256:512], b_in[512:1024], b_in[1024:1536], b_in[1552:2064]]).reshape(1, 1792))
    m['bxc'] = np.ascontiguousarray(b_in[2064:2320].reshape(2, 128).T)
    m['wg2'] = d['gla_wg2'][l]
    m['bg'] = np.ascontiguousarray(d['gla_bg'][l].reshape(1, 256))
    m['gnorm'] = d['gla_norm_g'][l]
    m['sgu_g'] = d['sgu_ln_g'][l]
    m['sgu_b'] = d['sgu_ln_b'][l]
    m['wsT'] = np.ascontiguousarray(d['sgu_ws'][l].transpose(2, 0, 1))
    m['bsT'] = np.ascontiguousarray(d['sgu_bs'][l].T)
    pw = d['pool_w'][l]
    bd = np.zeros((128, 2, 128), np.float32)
    for gi in range(4):
        cc, p0 = gi // 2, (gi % 2) * 64
        bd[p0:p0 + 64, cc, p0:p0 + 64] = pw[gi]
    m['pwbd'] = bd
    m['pscT'] = np.ascontiguousarray(d['pool_scale'][l].reshape(2, 128).T)
    return m

def mixer_b_inputs(d, l):
    b_in = d['b_in'][l]
    m = {}
    m['w_in'] = d['w_in'][l]
    m['b_gateT'] = np.ascontiguousarray(b_in[2320:5392].reshape(24, 128).T)
    m['w_up'] = np.ascontiguousarray(np.concatenate([d['w_up_a'][l], d['w_up_b'][l], d['w_up_c'][l]], axis=0))
    m['w_o'] = d['w_o'][l]
    m['ln_g'] = d['ln1_g'][l]
    m['ln_b'] = d['ln1_b'][l]
    return m


N_CORES = 8


def build_program():
    nc = bass.Bass("TRN2", target_bir_lowering=False)
    D = {}

    def din(name, shape):
        D[name] = nc.dram_tensor(name, shape, F32, kind="ExternalInput").ap()
        return D[name]

    x_in = din("x", [T, 1024])
    din("memT", [1024, 256])
    din("w_in", [2, 1024, 5392])
    din("bqk", [2, 64, 8]); din("bglow", [2, 16, 1]); din("brow", [2, 1, 1792]); din("bxc", [2, 128, 2])
    din("wg2", [2, 16, 256]); din("bg", [2, 1, 256]); din("gnorm", [2, 512]); din("sgu_g", [2, 256]); din("sgu_b", [2, 256])
    din("wsT", [2, 128, 4, 128]); din("bsT", [2, 128, 4]); din("pwbd", [2, 128, 2, 128]); din("pscT", [2, 128, 2])
    din("b_gateT", [2, 128, 24])
    din("w_up_a", [2, 512, 1024]); din("w_up_b", [2, 256, 1024]); din("w_up_c", [2, 256, 1024]); din("w_o", [2, 1024, 1024])
    for n in ("ln1_g", "ln1_b", "ln2_g", "ln2_b", "ln3_g", "ln3_b"):
        din(n, [2, 1024])
    for n in ("xa_wq", "xa_wk", "xa_wv", "xa_wo"):
        din(n, [2, 1024, 1024])
    din("router_w", [2, 1024, NE]); din("router_b", [2, 1, NE])
    din("exp_w_gu", [2, NE, 1024, 2048]); din("exp_w_down", [2, NE, 1024, 1024])
    din("b_guT", [2, 128, NE * 16]); din("exp_b_down", [2, NE, 1024])
    for n in ("ident", "tri_i", "tri_a", "cmask", "cmfull", "su"):
        din(n, [128, 128])
    din("eoff", [128, NE]); din("invc", [128, 2, 16])
    y_out = nc.dram_tensor("y", [T, 1024], F32, kind="ExternalOutput").ap()
    xa = nc.dram_tensor("xa_s", [T, 1024], F32, kind="Internal").ap()
    xb = nc.dram_tensor("xb_s", [T, 1024], F32, kind="Internal").ap()
    xc = nc.dram_tensor("xc_s", [T, 1024], F32, kind="Internal").ap()
    yT = nc.dram_tensor("yT_s", [1024, T], F32, kind="Internal").ap()
    xg = nc.dram_tensor("xg_s", [NE * CAP, 1024], BF16, kind="Internal").ap()
    yg = nc.dram_tensor("yg_s", [NE * CAP, 1024], F32, kind="Internal").ap()

    S = Sched(nc)
    sb = Arena(nc)
    X = Ctx()
    X.PS = PsumPool(nc, [f"ps{i}" for i in range(6)])
    X.PST = [(f"pst{i}", nc.alloc_psum_tensor(f"pst{i}", [128, 1024], BF16)) for i in range(2)]
    X.ident = sb("ident_bf", [128, 128], BF16)
    X.onesb = sb("onesb", [1, 128], BF16)
    X.ones128 = sb("ones128", [128, 128], BF16)
    X.eps = sb("eps_t", [128, 1], F32)
    base = sb.off
    S.dma("pool", X.ident[:, :], D["ident"], [], ["ident"], key="cst")
    S.memset("dve", X.onesb[:, :], 1.0, ["onesb"])
    S.memset("dve", X.ones128[:, :], 1.0, ["ones128"])
    S.memset("dve", X.eps[:, :], LN_EPS, ["eps"])
    S.barrier()
    src = x_in
    for l in range(2):
        dst = y_out if l == 1 else xc
        sb.reset(base); phase_1a(nc, S, X, sb, l, src, yT, D); S.barrier()
        sb.reset(base); phase_1b(nc, S, X, sb, l, src, yT, xa, D); S.barrier()
        sb.reset(base); phase_2(nc, S, X, sb, l, xa, xb, D); S.barrier()
        sb.reset(base); phase_3(nc, S, X, sb, l, xb, dst, xg, yg, D); S.barrier()
        src = xc
    counts = S.emit()
    print("instr counts", counts, "sems", S.nsem)
    return nc


def host_inputs(d):
    cst = consts()
    L = 2
    a = [mixer_a_inputs(d, l) for l in range(L)]
    m = {}
    m['w_in'] = d['w_in']
    for k in ('bqk', 'bglow', 'brow', 'bxc', 'wg2', 'bg', 'gnorm', 'sgu_g', 'sgu_b', 'wsT', 'bsT', 'pwbd', 'pscT'):
        m[k] = np.ascontiguousarray(np.stack([a[l][k] for l in range(L)]))
    m['b_gateT'] = np.ascontiguousarray(np.stack([d['b_in'][l][2320:5392].reshape(24, 128).T for l in range(L)]))
    for k in ('w_up_a', 'w_up_b', 'w_up_c', 'w_o', 'ln1_g', 'ln1_b', 'ln2_g', 'ln2_b', 'ln3_g', 'ln3_b',
              'xa_wq', 'xa_wk', 'xa_wv', 'xa_wo', 'router_w', 'exp_w_gu', 'exp_w_down', 'exp_b_down'):
        m[k] = d[k]
    m['router_b'] = np.ascontiguousarray(d['router_b'].reshape(L, 1, NE))
    m['b_guT'] = np.ascontiguousarray(np.stack([d['exp_b_gu'][l].reshape(NE, 16, 128).transpose(2, 0, 1).reshape(128, NE * 16) for l in range(L)]))
    for k in ('ident', 'tri_i', 'tri_a', 'cmask', 'cmfull', 'su'):
        m[k] = cst[k]
    m['eoff'] = np.tile((np.arange(NE) * CAP).astype(np.float32)[None, :], (128, 1))
    m['invc'] = invc_for(True)
    return m


def kernel(**inputs):
    d = {k: np.asarray(v) for k, v in inputs.items()}
    X = np.ascontiguousarray(d['x'], dtype=np.float32)
    B = X.shape[0]
    shared = host_inputs(d)
    nc = build_program()
    in_maps = []
    for c in range(N_CORES):
        b = c % B
        m = dict(shared)
        m['x'] = np.ascontiguousarray(X[b])
        m['memT'] = np.ascontiguousarray(d['mem'][b].T)
        in_maps.append(m)
    res = run_bass_kernel_spmd(nc, in_maps, core_ids=list(range(N_CORES)))
    out = np.stack([np.asarray(res.results[b]['y']) for b in range(B)])
    return out.astype(np.float32)
```

```python
import numpy as np
import concourse.bass as bass
import concourse.mybir as mybir
from concourse.bass_utils import run_bass_kernel_spmd


F32 = mybir.dt.float32
BF16 = mybir.dt.bfloat16
I32 = mybir.dt.int32
U32 = mybir.dt.uint32
AF = mybir.ActivationFunctionType
ALU = mybir.AluOpType
AX = mybir.AxisListType

SEM_LIMIT = 4000


class Sched:
    ENGS = ("pe", "act", "dve", "pool", "sp")

    def __init__(self, nc):
        self.nc = nc
        self.ops = []
        self.last_w = {}
        self.readers = {}
        self.pending_barrier = {}
        self.last_op_eng = {}
        self.last_op_key = {}

    def op(self, eng, fn, reads=(), writes=(), key=None):
        idx = len(self.ops)
        deps = set()
        for r in reads:
            if r in self.last_w:
                deps.add(self.last_w[r])
        for w in writes:
            if w in self.last_w:
                deps.add(self.last_w[w])
            for i in self.readers.get(w, {}).values():
                deps.add(i)
        if eng in self.pending_barrier:
            deps |= self.pending_barrier.pop(eng)
        rec = dict(eng=eng, fn=fn, deps=deps, key=key, signal=False)
        self.ops.append(rec)
        if key is None:
            self.last_op_eng[eng] = idx
        else:
            self.last_op_key[key] = idx
        tag = eng if key is None else ("dma", key)
        for r in reads:
            self.readers.setdefault(r, {})[tag] = idx
        for w in writes:
            self.last_w[w] = idx
            self.readers[w] = {}
        return idx

    def barrier(self):
        deps = set(self.last_op_eng.values()) | set(self.last_op_key.values())
        for e in self.ENGS:
            self.pending_barrier[e] = set(deps) | self.pending_barrier.get(e, set())
        self.last_w = {}
        self.readers = {}

    def mm(self, out, lhsT, rhs, start, stop, reads, writes):
        self.op("pe", lambda e: e.matmul(out, lhsT, rhs, start=start, stop=stop), reads, writes)

    def tr(self, out, in_, ident, reads, writes):
        self.op("pe", lambda e: e.transpose(out, in_, ident), reads, writes)

    def dma(self, eng, out, in_, reads, writes, key, **kw):
        self.op(eng, lambda e: e.dma_start(out, in_, **kw), reads, writes, key=key)


    def act(self, out, in_, func, reads, writes, **kw):
        self.op("act", lambda e: e.activation(out, in_, func, **kw), reads, writes)

    def copy(self, eng, out, in_, reads, writes):
        if eng == "act":
            self.op("act", lambda e: e.copy(out, in_), reads, writes)
        else:
            self.op(eng, lambda e: e.tensor_copy(out, in_), reads, writes)

    def tt(self, eng, out, in0, in1, op, reads, writes):
        self.op(eng, lambda e: e.tensor_tensor(out, in0, in1, op), reads, writes)

    def ts(self, eng, out, in0, s1, s2, op0, op1, reads, writes):
        if op1 is None:
            self.op(eng, lambda e: e.tensor_scalar(out, in0, s1, None, op0), reads, writes)
        else:
            self.op(eng, lambda e: e.tensor_scalar(out, in0, s1, s2, op0, op1), reads, writes)

    def stt(self, out, in0, scalar, in1, op0, op1, reads, writes):
        self.op("dve", lambda e: e.scalar_tensor_tensor(out, in0, scalar, in1, op0, op1), reads, writes)

    def memset(self, eng, ap, val, writes):
        self.op(eng, lambda e: e.memset(ap, val), [], writes)

    def gen(self, eng, name, args, reads, writes, **kw):
        self.op(eng, lambda e: getattr(e, name)(*args, **kw), reads, writes)

    def emit(self):
        nc = self.nc
        ops = self.ops
        for o in ops:
            o["deps"] = {d for d in o["deps"] if not (ops[d]["eng"] == "pe" and o["eng"] == "pe" and ops[d]["key"] is None and o["key"] is None)}
            for d in o["deps"]:
                ops[d]["signal"] = True
        eng_sem = {}
        key_sem = {}
        waited = {e: {} for e in self.ENGS}
        per_eng = {e: [] for e in self.ENGS}
        nsem = 0
        for o in ops:
            e = o["eng"]
            waits = []
            for d in sorted(o["deps"]):
                od = ops[d]
                if od["key"] is not None:
                    sem, cnt = key_sem[od["key"]]
                    tk = (sem, cnt)
                else:
                    tk = od["ticket"]
                sem, val = tk
                sid = id(sem)
                if waited[e].get(sid, (None, 0))[1] >= val:
                    continue
                waited[e][sid] = (sem, val)
                waits.append((sem, val))
            m = {}
            for sem, val in waits:
                if id(sem) not in m or m[id(sem)][1] < val:
                    m[id(sem)] = (sem, val)
            waits = list(m.values())
            sig = None
            if o["key"] is not None:
                if o["key"] not in key_sem:
                    key_sem[o["key"]] = [nc.alloc_semaphore(f"k{nsem}"), 0]
                    nsem += 1
                ks = key_sem[o["key"]]
                ks[1] += 16
                sig = (ks[0], 16)
            elif o["signal"]:
                if e not in eng_sem or eng_sem[e][1] >= SEM_LIMIT:
                    eng_sem[e] = [nc.alloc_semaphore(f"e{nsem}"), 0]
                    nsem += 1
                es = eng_sem[e]
                es[1] += 1
                o["ticket"] = (es[0], es[1])
                sig = (es[0], 1)
            per_eng[e].append((waits, o["fn"], sig))
        self.nsem = nsem
        final_waits = [(s, c) for (s, c) in key_sem.values()]

        def run(engine, lst, final=False):
            for waits, fn, sig in lst:
                for sem, val in waits:
                    engine.wait_ge(sem, val)
                inst = fn(engine)
                if sig is not None:
                    inst.then_inc(sig[0], sig[1])
            if final:
                for sem, val in final_waits:
                    engine.wait_ge(sem, val)

        with nc.Block() as block:
            @block.tensor
            def _(eng):
                run(eng, per_eng["pe"])

            @block.scalar
            def _(eng):
                run(eng, per_eng["act"])

            @block.vector
            def _(eng):
                run(eng, per_eng["dve"])

            @block.gpsimd
            def _(eng):
                run(eng, per_eng["pool"])

            @block.sync
            def _(eng):
                run(eng, per_eng["sp"], final=True)
        return {e: len(per_eng[e]) for e in self.ENGS}


class PsumPool:
    def __init__(self, nc, names):
        self.tiles = [(n, nc.alloc_psum_tensor(n, [128, 512], F32)) for n in names]
        self.i = 0

    def get(self):
        n, t = self.tiles[self.i % len(self.tiles)]
        self.i += 1
        return n, t


class Arena:
    BASE = 16512
    TOP = 229376

    def __init__(self, nc):
        self.nc = nc
        self.off = self.BASE
        self.n = 0

    def reset(self, to=None):
        self.off = self.BASE if to is None else to

    def __call__(self, name, shape, dtype):
        n = 1
        for s in shape[1:]:
            n *= s
        nbytes = n * (4 if dtype in (F32, I32, U32) else 2)
        nbytes = (nbytes + 63) // 64 * 64
        assert self.off + nbytes <= self.TOP, (name, self.off, nbytes)
        t = self.nc.alloc_sbuf_tensor_at(f"{name}_{self.n}", shape, dtype, offset=self.off)
        self.n += 1
        self.off += nbytes
        return t


T = 4096
NQ = T // 512
NT = T // 128
ALPHA = 4.0 ** 0.25
LN_EPS = 1e-5
NE = 32
CAP = 768
BLOCKS = ((0, 3), (3, 6))
CB = 384
CQ, CK, CV, CR, CG, CUV, CXC, GATE0 = 0, 256, 512, 1024, 1536, 1552, 2064, 2320


class Ctx:
    pass


def ln_tail(S, X, y, stats, mv, rstd, g_bc, b_bc, o_t, tagp):
    ry, rst, ro = tagp
    for hh in range(2):
        S.gen("dve", "bn_stats", (stats[:, hh, :], y[:, hh * 512:(hh + 1) * 512]), [ry], [rst + ("s", hh)])
    S.gen("dve", "bn_aggr", (mv[:, :], stats[:, :, :].rearrange("p a b -> p (a b)")), [rst + ("s", 0), rst + ("s", 1)], [rst + ("mv",)])
    S.act(rstd[:, 0:1], mv[:, 1:2], AF.Ln, [rst + ("mv",), "eps"], [rst + ("r0",)], bias=X.eps[:, 0:1], scale=1.0)
    S.act(rstd[:, 1:2], rstd[:, 0:1], AF.Exp, [rst + ("r0",)], [rst + ("r1",)], scale=-0.5)
    S.ts("dve", y[:, :], y[:, :], mv[:, 0:1], rstd[:, 1:2], ALU.subtract, ALU.mult, [ry, rst + ("mv",), rst + ("r1",)], [ry])
    S.tt("pool", y[:, :], y[:, :], g_bc[:, :], ALU.mult, [ry, "gb"], [ry])
    S.tt("pool", o_t, y[:, :], b_bc[:, :], ALU.add, [ry, "gb2"], [ro])


def make_evac(S):
    cp_i = [0]

    def evac(out, in_, reads, writes):
        cp_i[0] += 1
        S.copy("act" if cp_i[0] % 2 else "dve", out, in_, reads, writes)
    return evac


def x_front(S, X, sb_x, src_d, q, xq, xbf, xT, evac):
    def ld(qq):
        S.dma("sp", xq[qq % 2][:, :, :], src_d[qq * 512:(qq + 1) * 512, :].rearrange("(t p) d -> p t d", p=128), [], [("x", qq % 2)], key=("xld", qq % 2))
    if q == 0:
        ld(0)
    if q + 1 < NQ:
        ld(q + 1)
    xb = xq[q % 2]
    for tt in range(4):
        S.copy("pool", xbf[:, tt, :], xb[:, tt, :], [("x", q % 2)], ["xbf"])
    for fc in range(8):
        pn, pst = X.PST[fc % 2]
        for tt in range(4):
            S.tr(pst[:, tt * 128:(tt + 1) * 128], xbf[:, tt, fc * 128:(fc + 1) * 128], X.ident[:, :], ["xbf", "ident"], [pn])
        evac(xT[:, fc, :], pst[:, 0:512], [pn], ["xT"])


def phase_1a(nc, S, X, sb, l, x_d, yT_d, D):
    PS = X.PS
    PST = X.PST
    evac = make_evac(S)
    win_d = D["w_in"][l]
    wA = sb("wA", [128, 8, 2320], BF16)
    brow = sb("brow_bf", [1, 1792], BF16)
    bqk = sb("bqk_sb", [64, 8], F32)
    bgl = sb("bgl_sb", [16, 1], F32)
    bxc = sb("bxc_sb", [128, 2], F32)
    wg2 = sb("wg2_bf", [16, 256], BF16)
    bg = sb("bg_bf", [1, 256], BF16)
    gn_bc = sb("gn_bc", [128, 512], F32)
    sg_bc = sb("sg_bc", [128, 256], F32)
    sb_bc = sb("sb_bc", [128, 256], F32)
    wsT32 = sb("wsT32", [128, 4, 128], F32)
    wsT = sb("wsT_bf", [128, 4, 128], BF16)
    bsT = sb("bsT_sb", [128, 4], F32)
    pw = sb("pw_bf", [128, 2, 128], BF16)
    psc = sb("psc_sb", [128, 2], F32)
    tri_i = sb("tri_i_sb", [128, 128], F32)
    tri_a = sb("tri_a_sb", [128, 128], F32)
    cm4 = sb("cm4", [128, 4, 128], F32)
    cmf = sb("cmf", [128, 128], F32)
    invc = sb("invc_sb", [128, 2, 16], F32)
    xq = [sb(f"xq{i}", [128, 4, 1024], F32) for i in range(2)]
    xbf = sb("xbf", [128, 4, 1024], BF16)
    xT = sb("xT", [128, 8, 512], BF16)
    qTs = sb("qTs", [64, 4, 512], F32)
    kTs = sb("kTs", [64, 4, 512], F32)
    glT = sb("glT", [16, 512], BF16)
    yTs = sb("yTs", [128, 8, 512], F32)
    xcb = [sb(f"xcb{c}", [128, 528], F32) for c in range(2)]
    pa = [sb(f"pa{c}", [128, 528], F32) for c in range(2)]
    pb = [sb(f"pb{c}", [128, 528], F32) for c in range(2)]
    ypre = sb("ypre", [128, 2, 512], BF16)
    ptmp = sb("ptmp", [128, 16], F32)
    e1 = sb("e1", [128, 256], F32)
    l32 = sb("l32", [128, 256], F32)
    ebT = sb("ebT", [64, 4, 128], F32)
    enbT = sb("enbT", [64, 4, 128], F32)
    erem = sb("erem", [128, 256], F32)
    ktl = sb("ktl", [128, 256], BF16)
    vbf = sb("vbf", [128, 512], BF16)
    qdA = sb("qdA", [64, 4, 128], BF16)
    qdB = sb("qdB", [64, 4, 128], BF16)
    kdT = sb("kdT", [64, 4, 128], BF16)
    scm = sb("scm", [128, 512], BF16)
    Sa = sb("Sa", [64, 512], F32)
    Sb_ = sb("Sb", [64, 512], F32)
    S0b = sb("S0b", [64, 512], BF16)
    S1b = sb("S1b", [64, 512], BF16)
    sr = sb("sr", [128, 512], F32)
    sgn = sb("sgn", [128, 512], F32)
    junk = sb("junk", [128, 128], F32)
    osb = sb("osb", [128, 512], F32)
    ssq = sb("ssq", [128, 4], F32)
    rs4 = sb("rs4", [128, 8], F32)
    yabf = sb("yabf", [128, 512], BF16)
    zz = sb("zz", [128, 512], F32)
    vn = sb("vn", [128, 256], F32)
    vnb = sb("vnb", [128, 256], BF16)
    ybbf = sb("ybbf", [128, 256], BF16)
    stats = sb("stats", [128, 6], F32)
    mv = sb("mv", [128, 2], F32)
    rstd = sb("rstd", [128, 2], F32)

    for j, (c0, c1) in enumerate(((0, 1024), (1024, 2048), (2048, 2320))):
        S.dma("pool", wA[:, :, c0:c1], win_d[:, c0:c1].rearrange("(k p) n -> p k n", p=128), [], [("wA", j)], key=("wA", j))
    WA = [("wA", 0), ("wA", 1), ("wA", 2)]
    CST = ["cst"]
    S.dma("pool", brow[:, :], D["brow"][l], [], CST, key="cst")
    S.dma("pool", wg2[:, :], D["wg2"][l], [], CST, key="cst")
    S.dma("pool", bg[:, :], D["bg"][l], [], CST, key="cst")
    S.dma("pool", pw[:, :, :], D["pwbd"][l], [], CST, key="cst")
    S.dma("sp", bqk[:, :], D["bqk"][l], [], CST, key="cst")
    S.dma("sp", bgl[:, :], D["bglow"][l], [], CST, key="cst")
    S.dma("sp", bxc[:, :], D["bxc"][l], [], CST, key="cst")
    S.dma("sp", gn_bc[:, :], D["gnorm"][l].partition_broadcast(128), [], CST, key="cst")
    S.dma("sp", sg_bc[:, :], D["sgu_g"][l].partition_broadcast(128), [], CST, key="cst")
    S.dma("sp", sb_bc[:, :], D["sgu_b"][l].partition_broadcast(128), [], CST, key="cst")
    S.dma("sp", wsT32[:, :, :], D["wsT"][l], [], CST, key="cst")
    S.dma("sp", bsT[:, :], D["bsT"][l], [], CST, key="cst")
    S.dma("sp", psc[:, :], D["pscT"][l], [], CST, key="cst")
    S.dma("sp", tri_i[:, :], D["tri_i"], [], CST, key="cst")
    S.dma("sp", tri_a[:, :], D["tri_a"], [], CST, key="cst")
    for h in range(4):
        S.dma("sp", cm4[:, h, :], D["cmask"], [], CST, key="cst")
    S.dma("sp", cmf[:, :], D["cmfull"], [], CST, key="cst")
    S.dma("sp", invc[:, :, :], D["invc"], [], CST, key="cst")
    S.memset("dve", Sa[:, :], 0.0, ["Sa"])
    S.memset("dve", qdA[:, :, :], 0.0, ["qdA"])
    S.memset("dve", qdB[:, :, :], 0.0, ["qdB"])
    for c in range(2):
        S.memset("pool", xcb[c][:, :], 0.0, [("xcb", c)])
    for g in range(4):
        S.tt("dve", wsT[:, g, :], wsT32[:, g, :], cmf[:, :], ALU.mult, CST, ["wsT"])
    S.copy("pool", S0b[:, :], Sa[:, :], ["Sa"], ["S0b"])

    def quad_front(q):
        x_front(S, X, sb, x_d, q, xq, xbf, xT, evac)
        pn, ps = PS.get()
        for kc in range(8):
            S.mm(ps[0:16, :], wA[:, kc, CG:CG + 16], xT[:, kc, :], kc == 0, kc == 7, WA + ["xT"], [pn])
        S.act(glT[:, :], ps[0:16, :], AF.Identity, [pn] + CST, ["glT"], bias=bgl[:, 0:1], scale=1.0)
        for h in range(4):
            pn, ps = PS.get()
            for kc in range(8):
                S.mm(ps[0:64, :], wA[:, kc, CQ + h * 64:CQ + (h + 1) * 64], xT[:, kc, :], kc == 0, kc == 7, WA + ["xT"], [pn])
            S.ts("dve", qTs[:, h, :], ps[0:64, :], bqk[:, h:h + 1], 0.125, ALU.add, ALU.mult, [pn] + CST, ["qTs"])
            pn, ps = PS.get()
            for kc in range(8):
                S.mm(ps[0:64, :], wA[:, kc, CK + h * 64:CK + (h + 1) * 64], xT[:, kc, :], kc == 0, kc == 7, WA + ["xT"], [pn])
            S.act(kTs[:, h, :], ps[0:64, :], AF.Identity, [pn] + CST, ["kTs"], bias=bqk[:, 4 + h:5 + h], scale=1.0)
        for c in range(2):
            rxc = ("xcb", c)
            if q > 0:
                S.copy("pool", xcb[c][:, 0:16], xcb[c][:, 512:528], [rxc], [rxc])
            pn, ps = PS.get()
            for kc in range(8):
                S.mm(ps[:, :], wA[:, kc, CXC + c * 128:CXC + (c + 1) * 128], xT[:, kc, :], kc == 0, kc == 7, WA + ["xT"], [pn])
            S.act(xcb[c][:, 16:528], ps[:, :], AF.Identity, [pn] + CST, [rxc], bias=bxc[:, c:c + 1], scale=1.0)

    def gla_tile(tt):
        tsl = slice(tt * 128, (tt + 1) * 128)
        pnk, psk = PS.get()
        for kc in range(8):
            S.mm(psk[:, 0:256], xT[:, kc, tsl], wA[:, kc, CK:CK + 256], kc == 0, False, WA + ["xT"], [pnk])
        S.mm(psk[:, 0:256], X.onesb[0:1, :], brow[0:1, 0:256], False, True, ["onesb"] + CST, [pnk])
        pnv, psv = PS.get()
        for kc in range(8):
            S.mm(psv[:, :], xT[:, kc, tsl], wA[:, kc, CV:CV + 512], kc == 0, False, WA + ["xT"], [pnv])
        S.mm(psv[:, :], X.onesb[0:1, :], brow[0:1, 256:768], False, True, ["onesb"] + CST, [pnv])
        evac(vbf[:, :], psv[:, :], [pnv], ["vbf"])
        pnz, psz = PS.get()
        S.mm(psz[:, 0:256], glT[0:16, tsl], wg2[0:16, :], True, False, ["glT"] + CST, [pnz])
        S.mm(psz[:, 0:256], X.onesb[0:1, :], bg[0:1, :], False, True, ["onesb"] + CST, [pnz])
        S.act(e1[:, :], psz[:, 0:256], AF.Exp, [pnz], ["e1"], scale=-1.0)
        S.act(l32[:, :], e1[:, :], AF.Ln, ["e1"], ["l32"], bias=1.0, scale=1.0)
        pnb, psb = PS.get()
        for h in range(4):
            S.mm(psb[0:64, h * 128:(h + 1) * 128], l32[:, h * 64:(h + 1) * 64], tri_i[:, :], True, True, ["l32"] + CST, [pnb])
        pnr, psr = PS.get()
        S.mm(psr[:, 0:256], tri_a[:, :], l32[:, :], True, True, ["l32"] + CST, [pnr])
        S.act(ebT[:, :, :].rearrange("p a b -> p (a b)"), psb[0:64, :], AF.Exp, [pnb], ["ebT"], scale=1.0)
        S.act(enbT[:, :, :].rearrange("p a b -> p (a b)"), psb[0:64, :], AF.Exp, [pnb], ["enbT"], scale=-1.0)
        S.act(erem[:, :], psr[:, 0:256], AF.Exp, [pnr], ["erem"], scale=1.0)
        S.tt("dve", ktl[:, :], psk[:, 0:256], erem[:, :], ALU.mult, [pnk, "erem"], ["ktl"])
        kvs = []
        for c in range(2):
            pn, ps = PS.get()
            for h in range(4):
                S.mm(ps[0:64, h * 128:(h + 1) * 128], ktl[c * 64:(c + 1) * 64, h * 64:(h + 1) * 64], vbf[c * 64:(c + 1) * 64, h * 128:(h + 1) * 128], True, True, ["ktl", "vbf"], [pn])
            kvs.append((pn, ps))
        S.tt("dve", qdA[:, :, 0:64], qTs[:, :, tt * 128:tt * 128 + 64], ebT[:, :, 0:64], ALU.mult, ["qTs", "ebT"], ["qdA"])
        S.tt("dve", qdB[:, :, 64:128], qTs[:, :, tt * 128 + 64:tt * 128 + 128], ebT[:, :, 64:128], ALU.mult, ["qTs", "ebT"], ["qdB"])
        S.tt("dve", kdT[:, :, :], kTs[:, :, tsl], enbT[:, :, :], ALU.mult, ["kTs", "enbT"], ["kdT"])
        pns, pss = PS.get()
        for h in range(4):
            S.mm(pss[:, h * 128:h * 128 + 64], kdT[:, h, :], qdA[:, h, 0:64], True, True, ["kdT", "qdA"], [pns])
            S.mm(pss[:, h * 128 + 64:h * 128 + 128], kdT[:, h, :], qdB[:, h, 64:128], True, True, ["kdT", "qdB"], [pns])
        S.tt("dve", scm[:, :], pss[:, :], cm4[:, :, :].rearrange("p a b -> p (a b)"), ALU.mult, [pns] + CST, ["scm"])
        for h in range(4):
            hs = slice(h * 128, (h + 1) * 128)
            S.stt(Sb_[:, hs], Sa[:, hs], ebT[:, h, 63:64], kvs[0][1][0:64, hs], ALU.mult, ALU.add, ["Sa", "ebT", kvs[0][0]], ["Sb"])
        S.copy("pool", S1b[:, :], Sb_[:, :], ["Sb"], ["S1b"])
        pno, pso = PS.get()
        for h in range(4):
            hs = slice(h * 128, (h + 1) * 128)
            S.mm(pso[:, hs], scm[:, hs], vbf[:, hs], True, False, ["scm", "vbf"], [pno])
            S.mm(pso[:, hs], qdA[:, h, :], S0b[:, hs], False, False, ["qdA", "S0b"], [pno])
            S.mm(pso[:, hs], qdB[:, h, :], S1b[:, hs], False, True, ["qdB", "S1b"], [pno])
        for h in range(4):
            hs = slice(h * 128, (h + 1) * 128)
            S.stt(Sa[:, hs], Sb_[:, hs], ebT[:, h, 127:128], kvs[1][1][0:64, hs], ALU.mult, ALU.add, ["Sb", "ebT", kvs[1][0]], ["Sa"])
        S.copy("pool", S0b[:, :], Sa[:, :], ["Sa"], ["S0b"])
        pnr2, psr2 = PS.get()
        for kc in range(8):
            S.mm(psr2[:, :], xT[:, kc, tsl], wA[:, kc, CR:CR + 512], kc == 0, False, WA + ["xT"], [pnr2])
        S.mm(psr2[:, :], X.onesb[0:1, :], brow[0:1, 768:1280], False, True, ["onesb"] + CST, [pnr2])
        S.act(sr[:, :], psr2[:, :], AF.Silu, [pnr2], ["sr"])
        S.tt("pool", sgn[:, :], sr[:, :], gn_bc[:, :], ALU.mult, ["sr"] + CST, ["sgn"])
        S.copy("act", osb[:, :], pso[:, :], [pno], ["osb"])
        for h in range(4):
            hs = slice(h * 128, (h + 1) * 128)
            S.op("dve", (lambda a, b, d: (lambda e: e.scalar_tensor_tensor(a, b, 1.0, b, ALU.mult, ALU.mult, accum_out=d)))(junk[:, :], osb[:, hs], ssq[:, h:h + 1]),
                 ["osb"], ["junk", ("ssq", h)])
        S.act(rs4[:, 0:4], ssq[:, :], AF.Ln, [("ssq", h) for h in range(4)] + ["eps"], ["rs4a"], bias=X.eps[:, 0:1], scale=1.0 / 128.0)
        S.act(rs4[:, 4:8], rs4[:, 0:4], AF.Exp, ["rs4a"], ["rs4b"], scale=-0.5)
        for h in range(4):
            hs = slice(h * 128, (h + 1) * 128)
            S.stt(yabf[:, hs], osb[:, hs], rs4[:, 4 + h:5 + h], sgn[:, hs], ALU.mult, ALU.mult, ["osb", "rs4b", "sgn"], ["yabf"])
        pn, pst = PST[0]
        for h in range(4):
            S.tr(pst[:, h * 128:(h + 1) * 128], yabf[:, h * 128:(h + 1) * 128], X.ident[:, :], ["yabf", "ident"], [pn])
        evac(yTs[:, 0:4, tsl], pst[:, 0:512].rearrange("p (a b) -> p a b", b=128), [pn], [("yTs", "a")])

    def sgu_tile(tt):
        tsl = slice(tt * 128, (tt + 1) * 128)
        pn, ps = PS.get()
        for kc in range(8):
            S.mm(ps[:, :], xT[:, kc, tsl], wA[:, kc, CUV:CUV + 512], kc == 0, False, WA + ["xT"], [pn])
        S.mm(ps[:, :], X.onesb[0:1, :], brow[0:1, 1280:1792], False, True, ["onesb"] + CST, [pn])
        S.act(zz[:, :], ps[:, :], AF.Gelu, [pn], ["zz"])
        S.gen("dve", "bn_stats", (stats[:, :], zz[:, 256:512]), ["zz"], ["stats"])
        S.gen("dve", "bn_aggr", (mv[:, :], stats[:, :]), ["stats"], ["mv"])
        S.act(rstd[:, 0:1], mv[:, 1:2], AF.Ln, ["mv", "eps"], ["rstd0"], bias=X.eps[:, 0:1], scale=1.0)
        S.act(rstd[:, 1:2], rstd[:, 0:1], AF.Exp, ["rstd0"], ["rstd1"], scale=-0.5)
        S.ts("dve", vn[:, :], zz[:, 256:512], mv[:, 0:1], rstd[:, 1:2], ALU.subtract, ALU.mult, ["zz", "mv", "rstd1"], ["vn"])
        S.tt("pool", vn[:, :], vn[:, :], sg_bc[:, :], ALU.mult, ["vn"] + CST, ["vn"])
        S.tt("pool", vnb[:, :], vn[:, :], sb_bc[:, :], ALU.add, ["vn"] + CST, ["vnb"])
        pn2, ps2 = PS.get()
        for g in range(4):
            S.mm(ps2[:, g * 64:(g + 1) * 64], wsT[:, g, :], vnb[:, g * 64:(g + 1) * 64], True, True, ["wsT", "vnb"], [pn2])
        for g in range(4):
            gs = slice(g * 64, (g + 1) * 64)
            S.stt(ybbf[:, gs], ps2[:, gs], bsT[:, g:g + 1], zz[:, gs], ALU.add, ALU.mult, [pn2, "zz"] + CST, ["ybbf"])
        pn, pst = PST[1]
        for c in range(2):
            S.tr(pst[:, c * 128:(c + 1) * 128], ybbf[:, c * 128:(c + 1) * 128], X.ident[:, :], ["ybbf", "ident"], [pn])
        evac(yTs[:, 4:6, tsl], pst[:, 0:256].rearrange("p (a b) -> p a b", b=128), [pn], [("yTs", "b")])

    def pool_quad(q):
        for c in range(2):
            rxc = ("xcb", c)
            Xc = xcb[c]
            A = pa[c]
            B = pb[c]
            S.tt("pool", A[:, 1:528], Xc[:, 1:528], Xc[:, 0:527], ALU.add, [rxc], [("pa", c)])
            S.tt("pool", B[:, 3:528], A[:, 3:528], A[:, 1:526], ALU.add, [("pa", c)], [("pb", c)])
            if c == 0:
                S.stt(ypre[0:64, 0, :], A[0:64, 16:528], 0.5, Xc[0:64, 16:528], ALU.mult, ALU.subtract, [("pa", c), rxc], [("ypre", 0)])
                S.stt(ypre[64:128, 0, :], B[64:128, 16:528], 0.25, Xc[64:128, 16:528], ALU.mult, ALU.subtract, [("pb", c), rxc], [("ypre", 0)])
            else:
                S.tt("pool", A[:, 7:528], B[:, 7:528], B[:, 3:524], ALU.add, [("pb", c)], [("pa", c)])
                S.tt("pool", B[64:128, 15:528], A[64:128, 15:528], A[64:128, 7:520], ALU.add, [("pa", c)], [("pb", c)])
                S.stt(ypre[0:64, 1, :], A[0:64, 16:528], 0.125, Xc[0:64, 16:528], ALU.mult, ALU.subtract, [("pa", c), rxc], [("ypre", 1)])
                S.stt(ypre[64:128, 1, :], B[64:128, 16:528], 0.0625, Xc[64:128, 16:528], ALU.mult, ALU.subtract, [("pb", c), rxc], [("ypre", 1)])
            if q == 0:
                for (Z, p0, p1, nm) in ((A, 0, 64, "pa"), (B, 64, 128, "pb")):
                    S.tt("dve", ptmp[p0:p1, :], Z[p0:p1, 16:32], invc[p0:p1, c, :], ALU.mult, [(nm, c)] + CST, ["ptmp"])
                    S.tt("dve", ypre[p0:p1, c, 0:16], ptmp[p0:p1, :], Xc[p0:p1, 16:32], ALU.subtract, ["ptmp", rxc], [("ypre", c)])
            pn, ps = PS.get()
            S.mm(ps[:, :], pw[:, c, :], ypre[:, c, :], True, True, [("ypre", c)] + CST, [pn])
            S.ts("dve", yTs[:, 6 + c, :], ps[:, :], psc[:, c:c + 1], None, ALU.mult, None, [pn] + CST, [("yTs", "c", c)])

    for q in range(NQ):
        quad_front(q)
        for tt in range(4):
            gla_tile(tt)
            sgu_tile(tt)
        pool_quad(q)
        S.dma("sp", yT_d[:, q * 512:(q + 1) * 512].rearrange("(k p) t -> p k t", p=128), yTs[:, :, :],
              [("yTs", "a"), ("yTs", "b"), ("yTs", "c", 0), ("yTs", "c", 1)], [], key="yst")


def phase_1b(nc, S, X, sb, l, x_d, yT_d, y_d, D):
    PS = X.PS
    evac = make_evac(S)
    win_d = D["w_in"][l]
    wg = sb("wg", [128, 8, 3072], BF16)
    wup = sb("wup", [128, 8, 1024], BF16)
    wo = sb("wo", [128, 8, 1024], BF16)
    bgate = sb("bgate", [128, 24], F32)
    g_bc = sb("g_bc", [128, 1024], F32)
    b_bc = sb("b_bc", [128, 1024], F32)
    xq = [sb(f"xq{i}", [128, 4, 1024], F32) for i in range(2)]
    xbf = sb("xbf", [128, 4, 1024], BF16)
    xT = sb("xT", [128, 8, 512], BF16)
    yTq = sb("yTq", [128, 8, 512], BF16)
    sg = [sb(f"sg{j}", [128, 512], F32) for j in range(3)]
    mm_ = [sb(f"m{j}", [128, 512], F32) for j in range(3)]
    mT = sb("mT", [128, 8, 512], BF16)
    yb = [sb(f"yb{i}", [128, 1024], F32) for i in range(2)]
    ob = [sb(f"ob{i}", [128, 1024], F32) for i in range(2)]
    stats = [sb(f"st{i}", [128, 2, 6], F32) for i in range(2)]
    mv = [sb(f"mv{i}", [128, 2], F32) for i in range(2)]
    rstd = [sb(f"rstd{i}", [128, 2], F32) for i in range(2)]

    for j in range(3):
        S.dma("pool", wg[:, :, j * 1024:(j + 1) * 1024], win_d[:, GATE0 + j * 1024:GATE0 + (j + 1) * 1024].rearrange("(k p) n -> p k n", p=128), [], [("wg", j)], key=("wA", j))
    for r0, r1, nm in ((0, 4, "w_up_a"), (4, 6, "w_up_b"), (6, 8, "w_up_c")):
        S.dma("pool", wup[:, r0:r1, :], D[nm][l].rearrange("(k p) n -> p k n", p=128), [], [("wup", r0)], key="wup")
    WUP = [("wup", 0), ("wup", 4), ("wup", 6)]
    S.dma("pool", wo[:, :, :], D["w_o"][l].rearrange("(k p) n -> p k n", p=128), [], ["wo"], key="wo")
    S.dma("sp", bgate[:, :], D["b_gateT"][l], [], ["bgate"], key="cst")
    S.dma("sp", g_bc[:, :], D["ln1_g"][l].partition_broadcast(128), [], ["gb"], key="cst")
    S.dma("sp", b_bc[:, :], D["ln1_b"][l].partition_broadcast(128), [], ["gb2"], key="cst")

    for q in range(NQ):
        x_front(S, X, sb, x_d, q, xq, xbf, xT, evac)
        S.dma("pool", yTq[:, :, :], yT_d[:, q * 512:(q + 1) * 512].rearrange("(k p) t -> p k t", p=128), [], ["yTq"], key="yld")
        for fo in range(8):
            fsl = slice(fo * 128, (fo + 1) * 128)
            ups = []
            for (k0, k1) in ((0, 4), (4, 6), (6, 8)):
                pn, ps = PS.get()
                for kc in range(k0, k1):
                    S.mm(ps[:, :], wup[:, kc, fsl], yTq[:, kc, :], kc == k0, kc == k1 - 1, WUP + ["yTq"], [pn])
                ups.append((pn, ps))
            for j in range(3):
                pn, ps = PS.get()
                for kc in range(8):
                    S.mm(ps[:, :], wg[:, kc, j * 1024 + fo * 128:j * 1024 + (fo + 1) * 128], xT[:, kc, :], kc == 0, kc == 7, [("wg", j), "xT"], [pn])
                S.act(sg[j][:, :], ps[:, :], AF.Identity, [pn, "bgate"], [("sg", j)], bias=bgate[:, j * 8 + fo:j * 8 + fo + 1], scale=1.0)
                S.act(sg[j][:, :], sg[j][:, :], AF.Sigmoid, [("sg", j)], [("sg", j)], scale=1.0)
                S.tt("dve", mm_[j][:, :], ups[j][1][:, :], sg[j][:, :], ALU.mult, [ups[j][0], ("sg", j)], [("m", j)])
            S.tt("pool", mm_[0][:, :], mm_[0][:, :], mm_[1][:, :], ALU.add, [("m", 0), ("m", 1)], [("m", 0)])
            S.tt("pool", mT[:, fo, :], mm_[0][:, :], mm_[2][:, :], ALU.add, [("m", 0), ("m", 2)], ["mT"])
        for tt in range(4):
            i = tt % 2
            ry = ("y", i)
            for hf in range(2):
                pn, ps = PS.get()
                for kc in range(8):
                    S.mm(ps[:, :], mT[:, kc, tt * 128:(tt + 1) * 128], wo[:, kc, hf * 512:(hf + 1) * 512], kc == 0, kc == 7, ["mT", "wo"], [pn])
                S.stt(yb[i][:, hf * 512:(hf + 1) * 512], xq[q % 2][:, tt, hf * 512:(hf + 1) * 512], ALPHA, ps[:, :], ALU.mult, ALU.add, [pn, ("x", q % 2)], [ry])
            ln_tail(S, X, yb[i], stats[i], mv[i], rstd[i], g_bc, b_bc, ob[i][:, :], (ry, ("st", i), ("o", i)))
            tok0 = q * 512 + tt * 128
            S.dma("sp", y_d[tok0:tok0 + 128, :], ob[i][:, :], [("o", i)], [], key=("yst", i))


def phase_2(nc, S, X, sb, l, x_d, y_d, D):
    PS = X.PS
    evac = make_evac(S)
    w_bf = {n: sb(n + "_bf", [128, 8, 1024], BF16) for n in ("xa_wq", "xa_wk", "xa_wv", "xa_wo")}
    memT_bf = sb("memT_bf", [128, 8, 256], BF16)
    kT_bf = sb("kT_bf", [128, 8, 256], BF16)
    V_bf = sb("V_bf", [128, 2, 1024], BF16)
    g_bc = sb("g_bc", [128, 1024], F32)
    b_bc = sb("b_bc", [128, 1024], F32)
    xq = [sb(f"xq{i}", [128, 4, 1024], F32) for i in range(2)]
    xbf = sb("xbf", [128, 4, 1024], BF16)
    xT = sb("xT", [128, 8, 512], BF16)
    qT = sb("qT", [128, 8, 512], BF16)
    PT = [sb(f"PT{i}", [128, 2, 512], BF16) for i in range(2)]
    rs = sb("rs", [128, 512], F32)
    oT = sb("oT", [128, 8, 512], BF16)
    yb = [sb(f"yb{i}", [128, 1024], F32) for i in range(2)]
    ob = [sb(f"ob{i}", [128, 1024], F32) for i in range(2)]
    stats = [sb(f"st{i}", [128, 2, 6], F32) for i in range(2)]
    mv = [sb(f"mv{i}", [128, 2], F32) for i in range(2)]
    rstd = [sb(f"rstd{i}", [128, 2], F32) for i in range(2)]
    for n in w_bf:
        S.dma("pool", w_bf[n][:, :, :], D[n][l].rearrange("(k p) n -> p k n", p=128), [], [n], key=n)
    S.dma("pool", memT_bf[:, :, :], D["memT"].rearrange("(k p) n -> p k n", p=128), [], ["memT"], key="cst")
    S.dma("sp", g_bc[:, :], D["ln2_g"][l].partition_broadcast(128), [], ["gb"], key="cst")
    S.dma("sp", b_bc[:, :], D["ln2_b"][l].partition_broadcast(128), [], ["gb2"], key="cst")
    for fc in range(8):
        pn, ps = PS.get()
        for kc in range(8):
            S.mm(ps[:, 0:256], w_bf["xa_wk"][:, kc, fc * 128:(fc + 1) * 128], memT_bf[:, kc, :], kc == 0, kc == 7, ["xa_wk", "memT"], [pn])
        evac(kT_bf[:, fc, :], ps[:, 0:256], [pn], ["kT"])
    for mc in range(2):
        for hf in range(2):
            pn, ps = PS.get()
            for kc in range(8):
                S.mm(ps[:, :], memT_bf[:, kc, mc * 128:(mc + 1) * 128], w_bf["xa_wv"][:, kc, hf * 512:(hf + 1) * 512], kc == 0, kc == 7, ["xa_wv", "memT"], [pn])
            evac(V_bf[:, mc, hf * 512:(hf + 1) * 512], ps[:, :], [pn], ["V"])
    for q in range(NQ):
        xb = xq[q % 2]
        rx = ("x", q % 2)
        x_front(S, X, sb, x_d, q, xq, xbf, xT, evac)
        for fc in range(8):
            pn, ps = PS.get()
            for kc in range(8):
                S.mm(ps[:, :], w_bf["xa_wq"][:, kc, fc * 128:(fc + 1) * 128], xT[:, kc, :], kc == 0, kc == 7, ["xa_wq", "xT"], [pn])
            evac(qT[:, fc, :], ps[:, :], [pn], ["qT"])
        for h in range(4):
            pt = PT[h % 2]
            rpt = ("PT", h % 2)
            for mc in range(2):
                pn, ps = PS.get()
                for dc in range(2):
                    S.mm(ps[:, :], kT_bf[:, h * 2 + dc, mc * 128:(mc + 1) * 128], qT[:, h * 2 + dc, :], dc == 0, dc == 1, ["kT", "qT"], [pn])
                S.act(pt[:, mc, :], ps[:, :], AF.Exp, [pn], [rpt], scale=1.0 / 16.0)
            pn, ps = PS.get()
            for mc in range(2):
                S.mm(ps[:, :], X.ones128[:, :], pt[:, mc, :], mc == 0, mc == 1, ["ones128", rpt], [pn])
            S.gen("dve", "reciprocal", (rs[:, :], ps[:, :]), [pn], ["rs"])
            for dc in range(2):
                pn, ps = PS.get()
                for mc in range(2):
                    S.mm(ps[:, :], V_bf[:, mc, h * 256 + dc * 128:h * 256 + (dc + 1) * 128], pt[:, mc, :], mc == 0, mc == 1, ["V", rpt], [pn])
                S.tt("dve", oT[:, h * 2 + dc, :], ps[:, :], rs[:, :], ALU.mult, [pn, "rs"], ["oT"])
        for tt in range(4):
            i = tt % 2
            ry = ("y", i)
            for hf in range(2):
                pn, ps = PS.get()
                for kc in range(8):
                    S.mm(ps[:, :], oT[:, kc, tt * 128:(tt + 1) * 128], w_bf["xa_wo"][:, kc, hf * 512:(hf + 1) * 512], kc == 0, kc == 7, ["oT", "xa_wo"], [pn])
                S.stt(yb[i][:, hf * 512:(hf + 1) * 512], xb[:, tt, hf * 512:(hf + 1) * 512], ALPHA, ps[:, :], ALU.mult, ALU.add, [pn, rx], [ry])
            ln_tail(S, X, yb[i], stats[i], mv[i], rstd[i], g_bc, b_bc, ob[i][:, :], (ry, ("st", i), ("o", i)))
            tok0 = q * 512 + tt * 128
            S.dma("sp", y_d[tok0:tok0 + 128, :], ob[i][:, :], [("o", i)], [], key=("yst", i))


def phase_3(nc, S, X, sb, l, x_d, y_d, xg_d, yg_d, D):
    PS = X.PS
    PST = X.PST
    evac = make_evac(S)
    wgu_d = D["exp_w_gu"][l]
    wd_d = D["exp_w_down"][l]
    bd_d = D["exp_b_down"][l]
    X.phase_base = sb.off
    wgu = [sb(f"wgu{i}", [128, 8, 2048], BF16) for i in range(2)]
    wd = [sb(f"wd{i}", [128, 8, 1024], BF16) for i in range(2)]
    X.p3_weights_end = sb.off
    bd = [sb(f"bd{i}", [1, 1024], BF16) for i in range(2)]
    bgu = sb("bgu", [128, NE * 16], F32)
    rw = sb("rw", [128, 8, NE], F32)
    rb = sb("rb", [1, NE], F32)
    ident32 = sb("ident32", [128, 128], F32)
    su = sb("su_bf", [128, 128], BF16)
    ones32 = sb("ones32", [1, 128], F32)
    ones4 = sb("ones4", [128, 4], F32)
    eoff = sb("eoff_sb", [128, NE], F32)
    g_bc = sb("g_bc", [128, 1024], F32)
    b_bc = sb("b_bc", [128, 1024], F32)
    xt = sb("xt", [128, 1024], F32)
    xbf = [sb(f"xbf{i}", [128, 1024], BF16) for i in range(2)]
    xT32 = sb("xT32", [128, 8, 128], F32)
    lg = sb("lg", [128, NE], F32)
    work = sb("work", [128, NE], F32)
    tmx = sb("tmx", [128, 32], F32)
    mk = sb("mk", [128, 4], F32)
    ohs = sb("ohs", [128, 4, NE], F32)
    num4 = sb("num4", [128, 4], F32)
    negm = sb("negm", [128, 1], F32)
    ex = sb("ex", [128, NE], F32)
    den = sb("den", [128, 2], F32)
    maskf = sb("maskf", [128, NE], F32)
    maskb = sb("maskb", [128, NT, NE], BF16)
    destf = sb("destf", [128, NE], F32)
    junk = sb("junk", [128, NE], F32)
    d4f = sb("d4f", [128, NT, 4], F32)
    d4i = sb("d4i", [128, NT, 4], I32)
    g4 = sb("g4", [128, NT, 4], F32)
    xgs = [sb(f"xgs{i}", [128, 3, 1024], BF16) for i in range(2)]
    xgT = [sb(f"xgT{i}", [128, 8, CB], BF16) for i in range(2)]
    aT = [sb(f"aT{i}", [128, 8, CB], BF16) for i in range(2)]
    tg = [sb(f"tg{i}", [128, CB], F32) for i in range(2)]
    tsg = [sb(f"tsg{i}", [128, CB], F32) for i in range(2)]
    tl0 = [sb(f"tl0{i}", [128, CB], F32) for i in range(2)]
    tl1 = sb("tl1", [128, CB], F32)
    tgs = sb("tgs", [128, CB], F32)
    ysb = [sb(f"ysb{i}", [128, 1024], F32) for i in range(2)]
    stats = sb("stats", [128, 2, 6], F32)
    mv = sb("mv", [128, 2], F32)
    rstd = sb("rstd", [128, 2], F32)

    S.dma("sp", ident32[:, :], D["ident"], [], ["id32"], key="cst")
    S.dma("pool", su[:, :], D["su"], [], ["su"], key="cst")
    S.dma("sp", eoff[:, :], D["eoff"], [], ["eoff"], key="cst")
    S.dma("sp", rw[:, :, :], D["router_w"][l].rearrange("(k p) n -> p k n", p=128), [], ["rw"], key="cst")
    S.dma("sp", rb[:, :], D["router_b"][l], [], ["rb"], key="cst")
    S.dma("sp", bgu[:, :], D["b_guT"][l], [], ["bgu"], key="cst")
    S.dma("sp", g_bc[:, :], D["ln3_g"][l].partition_broadcast(128), [], ["gb"], key="cst")
    S.dma("sp", b_bc[:, :], D["ln3_b"][l].partition_broadcast(128), [], ["gb2"], key="cst")
    S.memset("dve", ones32[:, :], 1.0, ["ones32"])
    S.memset("dve", ones4[:, :], 1.0, ["ones4"])

    def load_expert(e):
        i = e % 2
        S.dma("pool", wgu[i][:, :, :], wgu_d[e].rearrange("(k p) n -> p k n", p=128), [], [("wgu", i, 0), ("wgu", i, 1)], key=("wgu", i))
        S.dma("pool", wd[i][:, :, :], wd_d[e].rearrange("(k p) n -> p k n", p=128), [], [("wd", i)], key=("wd", i))
        S.dma("pool", bd[i][:, :], bd_d[e:e + 1, :], [], [("bd", i)], key=("bd", i))

    load_expert(0)
    load_expert(1)

    def stt_acc(out, in0, in1, accum, reads, writes):
        S.op("dve", (lambda a, b, c, d: (lambda e: e.scalar_tensor_tensor(a, b, 1.0, c, ALU.mult, ALU.mult, accum_out=d)))(out, in0, in1, accum), reads, writes)

    for j in range(NT):
        S.dma("sp", xt[:, :], x_d[j * 128:(j + 1) * 128, :], [], ["xt"], key="xt")
        xb = xbf[j % 2]
        rxb = ("xbf", j % 2)
        S.copy("pool", xb[:, :], xt[:, :], ["xt"], [rxb])
        for fc in range(8):
            pn, ps = PS.get()
            S.tr(ps[:, 0:128], xt[:, fc * 128:(fc + 1) * 128], ident32[:, :], ["xt", "id32"], [pn])
            evac(xT32[:, fc, :], ps[:, 0:128], [pn], ["xT32"])
        pn, ps = PS.get()
        for kc in range(8):
            S.mm(ps[:, 0:NE], xT32[:, kc, :], rw[:, kc, :], kc == 0, False, ["xT32", "rw"], [pn])
        S.mm(ps[:, 0:NE], ones32[0:1, :], rb[0:1, :], False, True, ["ones32", "rb"], [pn])
        S.copy("dve", lg[:, :], ps[:, 0:NE], [pn], ["lg"])
        S.copy("dve", work[:, :], lg[:, :], ["lg"], ["work"])
        for k in range(4):
            S.tt("dve", tmx[:, 0:16], work[:, 0:16], work[:, 16:32], ALU.max, ["work"], ["tmx"])
            S.tt("dve", tmx[:, 16:24], tmx[:, 0:8], tmx[:, 8:16], ALU.max, ["tmx"], ["tmx"])
            S.tt("dve", tmx[:, 24:28], tmx[:, 16:20], tmx[:, 20:24], ALU.max, ["tmx"], ["tmx"])
            S.tt("dve", tmx[:, 28:30], tmx[:, 24:26], tmx[:, 26:28], ALU.max, ["tmx"], ["tmx"])
            S.tt("dve", mk[:, k:k + 1], tmx[:, 28:29], tmx[:, 29:30], ALU.max, ["tmx"], ["mk"])
            S.ts("dve", ohs[:, k, :], work[:, :], mk[:, k:k + 1], None, ALU.is_equal, None, ["work", "mk"], ["ohs"])
            S.stt(work[:, :], ohs[:, k, :], -1e30, work[:, :], ALU.mult, ALU.add, ["ohs", "work"], ["work"])
        S.ts("dve", maskf[:, :], lg[:, :], mk[:, 3:4], None, ALU.is_ge, None, ["lg", "mk"], ["maskf"])
        S.copy("dve", maskb[:, j, :], maskf[:, :], ["maskf"], [("maskb", j)])
        S.ts("dve", negm[:, :], mk[:, 0:1], -1.0, None, ALU.mult, None, ["mk"], ["negm"])
        S.act(ex[:, :], lg[:, :], AF.Exp, ["lg", "negm"], ["ex"], bias=negm[:, 0:1], scale=1.0)
        pn, ps = PS.get()
        for i in range(j):
            S.mm(ps[:, 0:NE], X.ones128[:, :], maskb[:, i, :], i == 0, False, ["ones128", ("maskb", i)], [pn])
        S.mm(ps[:, 0:NE], su[:, :], maskb[:, j, :], j == 0, True, ["su", ("maskb", j)], [pn])
        S.ts("dve", destf[:, :], ps[:, 0:NE], float(CAP - 1), None, ALU.min, None, [pn], ["destf"])
        S.tt("dve", destf[:, :], destf[:, :], eoff[:, :], ALU.add, ["destf", "eoff"], ["destf"])
        for k in range(4):
            stt_acc(junk[:, :], ohs[:, k, :], destf[:, :], d4f[:, j, k:k + 1], ["ohs", "destf"], ["junk", ("d4f", j)])
            stt_acc(junk[:, :], ohs[:, k, :], ex[:, :], num4[:, k:k + 1], ["ohs", "ex"], ["junk", "num4"])
        stt_acc(junk[:, 0:4], num4[:, :], ones4[:, :], den[:, 0:1], ["num4", "ones4"], ["junk", "den0"])
        S.gen("dve", "reciprocal", (den[:, 1:2], den[:, 0:1]), ["den0"], ["den1"])
        S.ts("dve", g4[:, j, :], num4[:, :], den[:, 1:2], None, ALU.mult, None, ["num4", "den1"], [("g4", j)])
        S.copy("dve", d4i[:, j, :], d4f[:, j, :], [("d4f", j)], [("d4i", j)])
        for k in range(4):
            S.op("pool", (lambda o_, i_, idx: (lambda e: e.indirect_dma_start(o_, bass.IndirectOffsetOnAxis(ap=idx, axis=0), i_, None)))(xg_d[:, :], xb[:, :], d4i[:, j, k:k + 1]),
                 [rxb, ("d4i", j)], [("xg", j, k)], key="xgsc")

    XG_ALL = [("xg", j, k) for j in range(NT) for k in range(4)]

    def load_xg(e_, bi_):
        t0_, t1_ = BLOCKS[bi_]
        b_ = (e_ * len(BLOCKS) + bi_) % 2
        for st in range(t1_ - t0_):
            r0 = e_ * CAP + (t0_ + st) * 128
            S.dma("sp", xgs[b_][:, st, :], xg_d[r0:r0 + 128, :], XG_ALL, [("xgs", b_, st)], key=("xgs", b_, st))
    YG_ALL = []
    for e in range(NE):
        i = e % 2
        for bi, (t0, t1) in enumerate(BLOCKS):
            nst = t1 - t0
            cb = nst * 128
            bb = (e * len(BLOCKS) + bi) % 2
            if e == 0 and bi == 0:
                load_xg(0, 0)
            nb = e * len(BLOCKS) + bi + 1
            if nb < NE * len(BLOCKS):
                load_xg(nb // len(BLOCKS), nb % len(BLOCKS))
            for fc in range(8):
                pn, pst = PST[fc % 2]
                for st in range(nst):
                    S.tr(pst[:, st * 128:(st + 1) * 128], xgs[bb][:, st, fc * 128:(fc + 1) * 128], X.ident[:, :], [("xgs", bb, st), "ident"], [pn])
                evac(xgT[bb][:, fc, 0:cb], pst[:, 0:cb], [pn], [("xgT", bb)])
            def stage_a(jj):
                d2 = jj % 2
                png, psg = PS.get()
                for kc in range(8):
                    S.mm(psg[:, 0:cb], wgu[i][:, kc, jj * 128:(jj + 1) * 128], xgT[bb][:, kc, 0:cb], kc == 0, kc == 7, [("wgu", i, 0), ("xgT", bb)], [png])
                pnl, psl = PS.get()
                for kc in range(8):
                    S.mm(psl[:, 0:cb], wgu[i][:, kc, 1024 + jj * 128:1024 + (jj + 1) * 128], xgT[bb][:, kc, 0:cb], kc == 0, kc == 7, [("wgu", i, 1), ("xgT", bb)], [pnl])
                cg = e * 16 + jj
                cl = e * 16 + 8 + jj
                S.ts("dve", tg[d2][:, 0:cb], psg[:, 0:cb], bgu[:, cg:cg + 1], 7.0, ALU.add, ALU.min, [png, "bgu"], [("tg", d2)])
                S.act(tsg[d2][:, 0:cb], tg[d2][:, 0:cb], AF.Sigmoid, [("tg", d2)], [("tsg", d2)], scale=1.702)
                S.act(tl0[d2][:, 0:cb], psl[:, 0:cb], AF.Identity, [pnl, "bgu"], [("tl0", d2)], bias=bgu[:, cl:cl + 1], scale=1.0)

            def stage_b(jj):
                d2 = jj % 2
                S.ts("dve", tl1[:, 0:cb], tl0[d2][:, 0:cb], -7.0, 7.0, ALU.max, ALU.min, [("tl0", d2)], ["tl1"])
                S.tt("dve", tgs[:, 0:cb], tg[d2][:, 0:cb], tsg[d2][:, 0:cb], ALU.mult, [("tg", d2), ("tsg", d2)], ["tgs"])
                S.stt(aT[bb][:, jj, 0:cb], tl1[:, 0:cb], 1.0, tgs[:, 0:cb], ALU.add, ALU.mult, ["tl1", "tgs"], [("aT", bb)])

            stage_a(0)
            for jj in range(1, 8):
                stage_a(jj)
                stage_b(jj - 1)
            stage_b(7)
            for st in range(nst):
                yi = st % 2
                for hf in range(2):
                    pn, ps = PS.get()
                    for kc in range(8):
                        S.mm(ps[:, :], aT[bb][:, kc, st * 128:(st + 1) * 128], wd[i][:, kc, hf * 512:(hf + 1) * 512], kc == 0, False, [("aT", bb), ("wd", i)], [pn])
                    S.mm(ps[:, :], X.onesb[0:1, :], bd[i][0:1, hf * 512:(hf + 1) * 512], False, True, ["onesb", ("bd", i)], [pn])
                    evac(ysb[yi][:, hf * 512:(hf + 1) * 512], ps[:, :], [pn], [("ysb", yi)])
                r0 = e * CAP + (t0 + st) * 128
                S.dma("sp", yg_d[r0:r0 + 128, :], ysb[yi][:, :], [("ysb", yi)], [("yg", e, t0 + st)], key="ygst")
                YG_ALL.append(("yg", e, t0 + st))
        if e + 2 < NE:
            load_expert(e + 2)

    S.barrier()
    keep = sb.off
    sb.reset(X.phase_base)
    yk = [[sb(f"yk{b}{i}", [128, 1024], F32) for i in range(4)] for b in range(2)]
    acc2 = [sb(f"acc{b}", [128, 1024], F32) for b in range(2)]
    ob2 = [sb(f"ob{b}", [128, 1024], F32) for b in range(2)]
    xt2 = [sb(f"xt{b}", [128, 1024], F32) for b in range(2)]
    st2 = [sb(f"stc{b}", [128, 2, 6], F32) for b in range(2)]
    mv2 = [sb(f"mvc{b}", [128, 2], F32) for b in range(2)]
    rs2 = [sb(f"rsc{b}", [128, 2], F32) for b in range(2)]
    assert sb.off <= X.p3_weights_end
    sb.reset(keep)
    for j in range(NT):
        b = j % 2
        S.dma("sp", xt2[b][:, :], x_d[j * 128:(j + 1) * 128, :], [], [("xt", b)], key=("xtc", b))
        for k in range(4):
            S.op("pool", (lambda o_, i_, idx: (lambda e: e.indirect_dma_start(o_, None, i_, bass.IndirectOffsetOnAxis(ap=idx, axis=0))))(yk[b][k][:, :], yg_d[:, :], d4i[:, j, k:k + 1]),
                 [], [("yk", b, k)], key=("ykg", b, k))
        S.ts("dve", acc2[b][:, :], yk[b][0][:, :], g4[:, j, 0:1], None, ALU.mult, None, [("yk", b, 0)], [("acc", b)])
        for k in range(1, 4):
            S.stt(acc2[b][:, :], yk[b][k][:, :], g4[:, j, k:k + 1], acc2[b][:, :], ALU.mult, ALU.add, [("yk", b, k), ("acc", b)], [("acc", b)])
        S.stt(acc2[b][:, :], xt2[b][:, :], ALPHA, acc2[b][:, :], ALU.mult, ALU.add, [("xt", b), ("acc", b)], [("acc", b)])
        ln_tail(S, X, acc2[b], st2[b], mv2[b], rs2[b], g_bc, b_bc, ob2[b][:, :], (("acc", b), ("stc", b), ("ob", b)))
        S.dma("sp", y_d[j * 128:(j + 1) * 128, :], ob2[b][:, :], [("ob", b)], [], key=("yst3", b))


def consts():
    s = np.arange(128)[:, None]; t = np.arange(128)[None, :]
    same = (s // 64) == (t // 64)
    c = {}
    c['ident'] = np.eye(128, dtype=np.float32)
    c['tri_i'] = np.where((s <= t) & same, -1.0 / 16.0, 0.0).astype(np.float32)
    c['tri_a'] = np.where((s > t) & same, -1.0 / 16.0, 0.0).astype(np.float32)
    c['cmask'] = ((s <= t) & same).astype(np.float32)
    c['cmfull'] = (s <= t).astype(np.float32)
    c['su'] = (s < t).astype(np.float32)
    return c

def invc_for(first_half):
    out = np.zeros((128, 2, 16), np.float32)
    tt = np.arange(16)
    for gi, w in enumerate((2, 4, 8, 16)):
        cc, p0 = gi // 2, (gi % 2) * 64
        cnt = np.minimum(tt + 1, w) if first_half else np.full(16, w)
        out[p0:p0 + 64, cc, :] = (1.0 / cnt).astype(np.float32)[None, :]
    return out

def mixer_a_inputs(d, l):
    b_in = d['b_in'][l]
    m = {}
    m['w_in'] = d['w_in'][l]
    m['bqk'] = np.ascontiguousarray(np.concatenate([b_in[0:256].reshape(4, 64).T, b_in[256:512].reshape(4, 64).T], axis=1))
    m['bglow'] = np.ascontiguousarray(b_in[1536:1552].reshape(16, 1))
    m['brow'] = np.ascontiguousarray(np.concatenate([b_in[256:512], b_in[512:1024], b_in[1024:1536], b_in[1552:2064]]).reshape(1, 1792))
    m['bxc'] = np.ascontiguousarray(b_in[2064:2320].reshape(2, 128).T)
    m['wg2'] = d['gla_wg2'][l]
    m['bg'] = np.ascontiguousarray(d['gla_bg'][l].reshape(1, 256))
    m['gnorm'] = d['gla_norm_g'][l]
    m['sgu_g'] = d['sgu_ln_g'][l]
    m['sgu_b'] = d['sgu_ln_b'][l]
    m['wsT'] = np.ascontiguousarray(d['sgu_ws'][l].transpose(2, 0, 1))
    m['bsT'] = np.ascontiguousarray(d['sgu_bs'][l].T)
    pw = d['pool_w'][l]
    bd = np.zeros((128, 2, 128), np.float32)
    for gi in range(4):
        cc, p0 = gi // 2, (gi % 2) * 64
        bd[p0:p0 + 64, cc, p0:p0 + 64] = pw[gi]
    m['pwbd'] = bd
    m['pscT'] = np.ascontiguousarray(d['pool_scale'][l].reshape(2, 128).T)
    return m

def mixer_b_inputs(d, l):
    b_in = d['b_in'][l]
    m = {}
    m['w_in'] = d['w_in'][l]
    m['b_gateT'] = np.ascontiguousarray(b_in[2320:5392].reshape(24, 128).T)
    m['w_up'] = np.ascontiguousarray(np.concatenate([d['w_up_a'][l], d['w_up_b'][l], d['w_up_c'][l]], axis=0))
    m['w_o'] = d['w_o'][l]
    m['ln_g'] = d['ln1_g'][l]
    m['ln_b'] = d['ln1_b'][l]
    return m


N_CORES = 8


def build_program():
    nc = bass.Bass("TRN2", target_bir_lowering=False)
    D = {}

    def din(name, shape):
        D[name] = nc.dram_tensor(name, shape, F32, kind="ExternalInput").ap()
        return D[name]

    x_in = din("x", [T, 1024])
    din("memT", [1024, 256])
    din("w_in", [2, 1024, 5392])
    din("bqk", [2, 64, 8]); din("bglow", [2, 16, 1]); din("brow", [2, 1, 1792]); din("bxc", [2, 128, 2])
    din("wg2", [2, 16, 256]); din("bg", [2, 1, 256]); din("gnorm", [2, 512]); din("sgu_g", [2, 256]); din("sgu_b", [2, 256])
    din("wsT", [2, 128, 4, 128]); din("bsT", [2, 128, 4]); din("pwbd", [2, 128, 2, 128]); din("pscT", [2, 128, 2])
    din("b_gateT", [2, 128, 24])
    din("w_up_a", [2, 512, 1024]); din("w_up_b", [2, 256, 1024]); din("w_up_c", [2, 256, 1024]); din("w_o", [2, 1024, 1024])
    for n in ("ln1_g", "ln1_b", "ln2_g", "ln2_b", "ln3_g", "ln3_b"):
        din(n, [2, 1024])
    for n in ("xa_wq", "xa_wk", "xa_wv", "xa_wo"):
        din(n, [2, 1024, 1024])
    din("router_w", [2, 1024, NE]); din("router_b", [2, 1, NE])
    din("exp_w_gu", [2, NE, 1024, 2048]); din("exp_w_down", [2, NE, 1024, 1024])
    din("b_guT", [2, 128, NE * 16]); din("exp_b_down", [2, NE, 1024])
    for n in ("ident", "tri_i", "tri_a", "cmask", "cmfull", "su"):
        din(n, [128, 128])
    din("eoff", [128, NE]); din("invc", [128, 2, 16])
    y_out = nc.dram_tensor("y", [T, 1024], F32, kind="ExternalOutput").ap()
    xa = nc.dram_tensor("xa_s", [T, 1024], F32, kind="Internal").ap()
    xb = nc.dram_tensor("xb_s", [T, 1024], F32, kind="Internal").ap()
    xc = nc.dram_tensor("xc_s", [T, 1024], F32, kind="Internal").ap()
    yT = nc.dram_tensor("yT_s", [1024, T], F32, kind="Internal").ap()
    xg = nc.dram_tensor("xg_s", [NE * CAP, 1024], BF16, kind="Internal").ap()
    yg = nc.dram_tensor("yg_s", [NE * CAP, 1024], F32, kind="Internal").ap()

    S = Sched(nc)
    sb = Arena(nc)
    X = Ctx()
    X.PS = PsumPool(nc, [f"ps{i}" for i in range(6)])
    X.PST = [(f"pst{i}", nc.alloc_psum_tensor(f"pst{i}", [128, 1024], BF16)) for i in range(2)]
    X.ident = sb("ident_bf", [128, 128], BF16)
    X.onesb = sb("onesb", [1, 128], BF16)
    X.ones128 = sb("ones128", [128, 128], BF16)
    X.eps = sb("eps_t", [128, 1], F32)
    base = sb.off
    S.dma("pool", X.ident[:, :], D["ident"], [], ["ident"], key="cst")
    S.memset("dve", X.onesb[:, :], 1.0, ["onesb"])
    S.memset("dve", X.ones128[:, :], 1.0, ["ones128"])
    S.memset("dve", X.eps[:, :], LN_EPS, ["eps"])
    S.barrier()
    src = x_in
    for l in range(2):
        dst = y_out if l == 1 else xc
        sb.reset(base); phase_1a(nc, S, X, sb, l, src, yT, D); S.barrier(); print("sbuf 1a", sb.off)
        sb.reset(base); phase_1b(nc, S, X, sb, l, src, yT, xa, D); S.barrier(); print("sbuf 1b", sb.off)
        sb.reset(base); phase_2(nc, S, X, sb, l, xa, xb, D); S.barrier(); print("sbuf 2", sb.off)
        sb.reset(base); phase_3(nc, S, X, sb, l, xb, dst, xg, yg, D); S.barrier(); print("sbuf 3", sb.off)
        src = xc
    counts = S.emit()
    print("instr counts", counts, "sems", S.nsem)
    return nc


def host_inputs(d):
    cst = consts()
    L = 2
    a = [mixer_a_inputs(d, l) for l in range(L)]
    m = {}
    m['w_in'] = d['w_in']
    for k in ('bqk', 'bglow', 'brow', 'bxc', 'wg2', 'bg', 'gnorm', 'sgu_g', 'sgu_b', 'wsT', 'bsT', 'pwbd', 'pscT'):
        m[k] = np.ascontiguousarray(np.stack([a[l][k] for l in range(L)]))
    m['b_gateT'] = np.ascontiguousarray(np.stack([d['b_in'][l][2320:5392].reshape(24, 128).T for l in range(L)]))
    for k in ('w_up_a', 'w_up_b', 'w_up_c', 'w_o', 'ln1_g', 'ln1_b', 'ln2_g', 'ln2_b', 'ln3_g', 'ln3_b',
              'xa_wq', 'xa_wk', 'xa_wv', 'xa_wo', 'router_w', 'exp_w_gu', 'exp_w_down', 'exp_b_down'):
        m[k] = d[k]
    m['router_b'] = np.ascontiguousarray(d['router_b'].reshape(L, 1, NE))
    m['b_guT'] = np.ascontiguousarray(np.stack([d['exp_b_gu'][l].reshape(NE, 16, 128).transpose(2, 0, 1).reshape(128, NE * 16) for l in range(L)]))
    for k in ('ident', 'tri_i', 'tri_a', 'cmask', 'cmfull', 'su'):
        m[k] = cst[k]
    m['eoff'] = np.tile((np.arange(NE) * CAP).astype(np.float32)[None, :], (128, 1))
    m['invc'] = invc_for(True)
    return m


def kernel(**inputs):
    d = {k: np.asarray(v) for k, v in inputs.items()}
    X = np.ascontiguousarray(d['x'], dtype=np.float32)
    B = X.shape[0]
    shared = host_inputs(d)
    nc = build_program()
    owner = {0: 0, 1: 1, 4: 2, 5: 3}
    zeros = {k: np.zeros_like(v) for k, v in shared.items()}
    zx = np.zeros_like(X[0])
    zm = np.zeros((X.shape[2], d['mem'].shape[1]), np.float32)
    in_maps = []
    for c in range(N_CORES):
        if c in owner:
            b = owner[c]
            m = dict(shared)
            m['x'] = np.ascontiguousarray(X[b])
            m['memT'] = np.ascontiguousarray(d['mem'][b].T)
        else:
            m = dict(zeros)
            m['x'] = zx
            m['memT'] = zm
        in_maps.append(m)
    res = run_bass_kernel_spmd(nc, in_maps, core_ids=list(range(N_CORES)))
    out = np.stack([np.asarray(res.results[c]['y']) for c in (0, 1, 4, 5)])
    return out.astype(np.float32)
```

```python
import numpy as np
import concourse.bass as bass
import concourse.mybir as mybir
from concourse.bass_utils import run_bass_kernel_spmd


F32 = mybir.dt.float32
BF16 = mybir.dt.bfloat16
I32 = mybir.dt.int32
U32 = mybir.dt.uint32
AF = mybir.ActivationFunctionType
ALU = mybir.AluOpType
AX = mybir.AxisListType

SEM_LIMIT = 4000


class Sched:
    ENGS = ("pe", "act", "dve", "pool", "sp")

    def __init__(self, nc):
        self.nc = nc
        self.ops = []
        self.last_w = {}
        self.readers = {}
        self.pending_barrier = {}
        self.last_op_eng = {}
        self.last_op_key = {}

    def op(self, eng, fn, reads=(), writes=(), key=None):
        idx = len(self.ops)
        deps = set()
        for r in reads:
            if r in self.last_w:
                deps.add(self.last_w[r])
        for w in writes:
            if w in self.last_w:
                deps.add(self.last_w[w])
            for i in self.readers.get(w, {}).values():
                deps.add(i)
        if eng in self.pending_barrier:
            deps |= self.pending_barrier.pop(eng)
        rec = dict(eng=eng, fn=fn, deps=deps, key=key, signal=False)
        self.ops.append(rec)
        if key is None:
            self.last_op_eng[eng] = idx
        else:
            self.last_op_key[key] = idx
        tag = eng if key is None else ("dma", key)
        for r in reads:
            self.readers.setdefault(r, {})[tag] = idx
        for w in writes:
            self.last_w[w] = idx
            self.readers[w] = {}
        return idx

    def barrier(self):
        deps = set(self.last_op_eng.values()) | set(self.last_op_key.values())
        for e in self.ENGS:
            self.pending_barrier[e] = set(deps) | self.pending_barrier.get(e, set())
        self.last_w = {}
        self.readers = {}

    def mm(self, out, lhsT, rhs, start, stop, reads, writes):
        self.op("pe", lambda e: e.matmul(out, lhsT, rhs, start=start, stop=stop), reads, writes)

    def tr(self, out, in_, ident, reads, writes):
        self.op("pe", lambda e: e.transpose(out, in_, ident), reads, writes)

    def dma(self, eng, out, in_, reads, writes, key, **kw):
        self.op(eng, lambda e: e.dma_start(out, in_, **kw), reads, writes, key=key)


    def act(self, out, in_, func, reads, writes, **kw):
        self.op("act", lambda e: e.activation(out, in_, func, **kw), reads, writes)

    def copy(self, eng, out, in_, reads, writes):
        if eng == "act":
            self.op("act", lambda e: e.copy(out, in_), reads, writes)
        else:
            self.op(eng, lambda e: e.tensor_copy(out, in_), reads, writes)

    def tt(self, eng, out, in0, in1, op, reads, writes):
        self.op(eng, lambda e: e.tensor_tensor(out, in0, in1, op), reads, writes)

    def ts(self, eng, out, in0, s1, s2, op0, op1, reads, writes):
        if op1 is None:
            self.op(eng, lambda e: e.tensor_scalar(out, in0, s1, None, op0), reads, writes)
        else:
            self.op(eng, lambda e: e.tensor_scalar(out, in0, s1, s2, op0, op1), reads, writes)

    def stt(self, out, in0, scalar, in1, op0, op1, reads, writes):
        self.op("dve", lambda e: e.scalar_tensor_tensor(out, in0, scalar, in1, op0, op1), reads, writes)

    def memset(self, eng, ap, val, writes):
        self.op(eng, lambda e: e.memset(ap, val), [], writes)

    def gen(self, eng, name, args, reads, writes, **kw):
        self.op(eng, lambda e: getattr(e, name)(*args, **kw), reads, writes)

    def emit(self):
        nc = self.nc
        ops = self.ops
        for o in ops:
            o["deps"] = {d for d in o["deps"] if not (ops[d]["eng"] == "pe" and o["eng"] == "pe" and ops[d]["key"] is None and o["key"] is None)}
            for d in o["deps"]:
                ops[d]["signal"] = True
        eng_sem = {}
        key_sem = {}
        waited = {e: {} for e in self.ENGS}
        per_eng = {e: [] for e in self.ENGS}
        nsem = 0
        for o in ops:
            e = o["eng"]
            waits = []
            for d in sorted(o["deps"]):
                od = ops[d]
                if od["key"] is not None:
                    sem, cnt = key_sem[od["key"]]
                    tk = (sem, cnt)
                else:
                    tk = od["ticket"]
                sem, val = tk
                sid = id(sem)
                if waited[e].get(sid, (None, 0))[1] >= val:
                    continue
                waited[e][sid] = (sem, val)
                waits.append((sem, val))
            m = {}
            for sem, val in waits:
                if id(sem) not in m or m[id(sem)][1] < val:
                    m[id(sem)] = (sem, val)
            waits = list(m.values())
            sig = None
            if o["key"] is not None:
                if o["key"] not in key_sem:
                    key_sem[o["key"]] = [nc.alloc_semaphore(f"k{nsem}"), 0]
                    nsem += 1
                ks = key_sem[o["key"]]
                ks[1] += 16
                sig = (ks[0], 16)
            elif o["signal"]:
                if e not in eng_sem or eng_sem[e][1] >= SEM_LIMIT:
                    eng_sem[e] = [nc.alloc_semaphore(f"e{nsem}"), 0]
                    nsem += 1
                es = eng_sem[e]
                es[1] += 1
                o["ticket"] = (es[0], es[1])
                sig = (es[0], 1)
            per_eng[e].append((waits, o["fn"], sig))
        self.nsem = nsem
        final_waits = [(s, c) for (s, c) in key_sem.values()]

        def run(engine, lst, final=False):
            for waits, fn, sig in lst:
                for sem, val in waits:
                    engine.wait_ge(sem, val)
                inst = fn(engine)
                if sig is not None:
                    inst.then_inc(sig[0], sig[1])
            if final:
                for sem, val in final_waits:
                    engine.wait_ge(sem, val)

        with nc.Block() as block:
            @block.tensor
            def _(eng):
                run(eng, per_eng["pe"])

            @block.scalar
            def _(eng):
                run(eng, per_eng["act"])

            @block.vector
            def _(eng):
                run(eng, per_eng["dve"])

            @block.gpsimd
            def _(eng):
                run(eng, per_eng["pool"])

            @block.sync
            def _(eng):
                run(eng, per_eng["sp"], final=True)
        return {e: len(per_eng[e]) for e in self.ENGS}


class PsumPool:
    def __init__(self, nc, names):
        self.tiles = [(n, nc.alloc_psum_tensor(n, [128, 512], F32)) for n in names]
        self.i = 0

    def get(self):
        n, t = self.tiles[self.i % len(self.tiles)]
        self.i += 1
        return n, t


class Arena:
    BASE = 16512
    TOP = 229376

    def __init__(self, nc):
        self.nc = nc
        self.off = self.BASE
        self.n = 0

    def reset(self, to=None):
        self.off = self.BASE if to is None else to

    def __call__(self, name, shape, dtype):
        n = 1
        for s in shape[1:]:
            n *= s
        nbytes = n * (4 if dtype in (F32, I32, U32) else 2)
        nbytes = (nbytes + 63) // 64 * 64
        assert self.off + nbytes <= self.TOP, (name, self.off, nbytes)
        t = self.nc.alloc_sbuf_tensor_at(f"{name}_{self.n}", shape, dtype, offset=self.off)
        self.n += 1
        self.off += nbytes
        return t


T = 4096
NQ = T // 512
NT = T // 128
ALPHA = 4.0 ** 0.25
LN_EPS = 1e-5
NE = 32
CAP = 768
BLOCKS = ((0, 3), (3, 6))
CB = 384
CQ, CK, CV, CR, CG, CUV, CXC, GATE0 = 0, 256, 512, 1024, 1536, 1552, 2064, 2320


class Ctx:
    pass


def ln_tail(S, X, y, stats, mv, rstd, g_bc, b_bc, o_t, tagp, eng_g="pool"):
    ry, rst, ro = tagp
    for hh in range(2):
        S.gen("dve", "bn_stats", (stats[:, hh, :], y[:, hh * 512:(hh + 1) * 512]), [ry], [rst + ("s", hh)])
    S.gen("dve", "bn_aggr", (mv[:, :], stats[:, :, :].rearrange("p a b -> p (a b)")), [rst + ("s", 0), rst + ("s", 1)], [rst + ("mv",)])
    S.act(rstd[:, 0:1], mv[:, 1:2], AF.Ln, [rst + ("mv",), "eps"], [rst + ("r0",)], bias=X.eps[:, 0:1], scale=1.0)
    S.act(rstd[:, 1:2], rstd[:, 0:1], AF.Exp, [rst + ("r0",)], [rst + ("r1",)], scale=-0.5)
    S.ts("dve", y[:, :], y[:, :], mv[:, 0:1], rstd[:, 1:2], ALU.subtract, ALU.mult, [ry, rst + ("mv",), rst + ("r1",)], [ry])
    S.tt(eng_g, y[:, :], y[:, :], g_bc[:, :], ALU.mult, [ry, "gb"], [ry])
    S.tt("pool", o_t, y[:, :], b_bc[:, :], ALU.add, [ry, "gb2"], [ro])


def make_evac(S):
    cp_i = [0]

    def evac(out, in_, reads, writes):
        cp_i[0] += 1
        S.copy("act" if cp_i[0] % 2 else "dve", out, in_, reads, writes)
    return evac


def x_front(S, X, sb_x, src_d, q, xq, xbf, xT, evac):
    def ld(qq):
        S.dma("sp", xq[qq % 2][:, :, :], src_d[qq * 512:(qq + 1) * 512, :].rearrange("(t p) d -> p t d", p=128), [], [("x", qq % 2)], key=("xld", qq % 2))
    if q == 0:
        ld(0)
    if q + 1 < NQ:
        ld(q + 1)
    xb = xq[q % 2]
    for tt in range(4):
        S.copy("pool", xbf[:, tt, :], xb[:, tt, :], [("x", q % 2)], ["xbf"])
    for fc in range(8):
        pn, pst = X.PST[fc % 2]
        for tt in range(4):
            S.tr(pst[:, tt * 128:(tt + 1) * 128], xbf[:, tt, fc * 128:(fc + 1) * 128], X.ident[:, :], ["xbf", "ident"], [pn])
        evac(xT[:, fc, :], pst[:, 0:512], [pn], ["xT"])


def phase_1a(nc, S, X, sb, l, x_d, yT_d, D):
    PS = X.PS
    PST = X.PST
    evac = make_evac(S)
    win_d = D["w_in"][l]
    wA = sb("wA", [128, 8, 2320], BF16)
    brow = sb("brow_bf", [1, 1792], BF16)
    bqk = sb("bqk_sb", [64, 8], F32)
    bgl = sb("bgl_sb", [16, 1], F32)
    bxc = sb("bxc_sb", [128, 2], F32)
    wg2 = sb("wg2_bf", [16, 256], BF16)
    bg = sb("bg_bf", [1, 256], BF16)
    gn_bc = sb("gn_bc", [128, 512], F32)
    sg_bc = sb("sg_bc", [128, 256], F32)
    sb_bc = sb("sb_bc", [128, 256], F32)
    wsT32 = sb("wsT32", [128, 4, 128], F32)
    wsT = sb("wsT_bf", [128, 4, 128], BF16)
    bsT = sb("bsT_sb", [128, 4], F32)
    pw = sb("pw_bf", [128, 2, 128], BF16)
    psc = sb("psc_sb", [128, 2], F32)
    tri_i = sb("tri_i_sb", [128, 128], F32)
    tri_a = sb("tri_a_sb", [128, 128], F32)
    cm4 = sb("cm4", [128, 4, 128], F32)
    cmf = sb("cmf", [128, 128], F32)
    invc = sb("invc_sb", [128, 2, 16], F32)
    xq = [sb(f"xq{i}", [128, 4, 1024], F32) for i in range(2)]
    xbf = sb("xbf", [128, 4, 1024], BF16)
    xT = sb("xT", [128, 8, 512], BF16)
    qTs = sb("qTs", [64, 4, 512], F32)
    kTs = sb("kTs", [64, 4, 512], F32)
    glT = sb("glT", [16, 512], BF16)
    yTs = sb("yTs", [128, 8, 512], F32)
    xcb = [sb(f"xcb{c}", [128, 528], F32) for c in range(2)]
    pa = [sb(f"pa{c}", [128, 528], F32) for c in range(2)]
    pb = [sb(f"pb{c}", [128, 528], F32) for c in range(2)]
    ypre = sb("ypre", [128, 2, 512], BF16)
    ptmp = sb("ptmp", [128, 16], F32)
    e1 = sb("e1", [128, 256], F32)
    l32 = sb("l32", [128, 256], F32)
    ebT = sb("ebT", [64, 4, 128], F32)
    enbT = sb("enbT", [64, 4, 128], F32)
    erem = sb("erem", [128, 256], F32)
    ktl = sb("ktl", [128, 256], BF16)
    vbf = sb("vbf", [128, 512], BF16)
    qdA = sb("qdA", [64, 4, 128], BF16)
    qdB = sb("qdB", [64, 4, 128], BF16)
    kdT = sb("kdT", [64, 4, 128], BF16)
    scm = sb("scm", [128, 512], BF16)
    Sa = sb("Sa", [64, 512], F32)
    Sb_ = sb("Sb", [64, 512], F32)
    S0b = sb("S0b", [64, 512], BF16)
    S1b = sb("S1b", [64, 512], BF16)
    sr = sb("sr", [128, 512], F32)
    sgn = sb("sgn", [128, 512], F32)
    junk = sb("junk", [128, 128], F32)
    osb = sb("osb", [128, 512], F32)
    ssq = sb("ssq", [128, 4], F32)
    rs4 = sb("rs4", [128, 8], F32)
    yabf = sb("yabf", [128, 512], BF16)
    zz = sb("zz", [128, 512], F32)
    vn = sb("vn", [128, 256], F32)
    vnb = sb("vnb", [128, 256], BF16)
    ybbf = sb("ybbf", [128, 256], BF16)
    stats = sb("stats", [128, 6], F32)
    mv = sb("mv", [128, 2], F32)
    rstd = sb("rstd", [128, 2], F32)

    for j, (c0, c1) in enumerate(((0, 1024), (1024, 2048), (2048, 2320))):
        S.dma("pool", wA[:, :, c0:c1], win_d[:, c0:c1].rearrange("(k p) n -> p k n", p=128), [], [("wA", j)], key=("wA", j))
    WA = [("wA", 0), ("wA", 1), ("wA", 2)]
    CST = ["cst"]
    S.dma("pool", brow[:, :], D["brow"][l], [], CST, key="cst")
    S.dma("pool", wg2[:, :], D["wg2"][l], [], CST, key="cst")
    S.dma("pool", bg[:, :], D["bg"][l], [], CST, key="cst")
    S.dma("pool", pw[:, :, :], D["pwbd"][l], [], CST, key="cst")
    S.dma("sp", bqk[:, :], D["bqk"][l], [], CST, key="cst")
    S.dma("sp", bgl[:, :], D["bglow"][l], [], CST, key="cst")
    S.dma("sp", bxc[:, :], D["bxc"][l], [], CST, key="cst")
    S.dma("sp", gn_bc[:, :], D["gnorm"][l].partition_broadcast(128), [], CST, key="cst")
    S.dma("sp", sg_bc[:, :], D["sgu_g"][l].partition_broadcast(128), [], CST, key="cst")
    S.dma("sp", sb_bc[:, :], D["sgu_b"][l].partition_broadcast(128), [], CST, key="cst")
    S.dma("sp", wsT32[:, :, :], D["wsT"][l], [], CST, key="cst")
    S.dma("sp", bsT[:, :], D["bsT"][l], [], CST, key="cst")
    S.dma("sp", psc[:, :], D["pscT"][l], [], CST, key="cst")
    S.dma("sp", tri_i[:, :], D["tri_i"], [], CST, key="cst")
    S.dma("sp", tri_a[:, :], D["tri_a"], [], CST, key="cst")
    for h in range(4):
        S.dma("sp", cm4[:, h, :], D["cmask"], [], CST, key="cst")
    S.dma("sp", cmf[:, :], D["cmfull"], [], CST, key="cst")
    S.dma("sp", invc[:, :, :], D["invc"], [], CST, key="cst")
    S.memset("dve", Sa[:, :], 0.0, ["Sa"])
    S.memset("dve", qdA[:, :, :], 0.0, ["qdA"])
    S.memset("dve", qdB[:, :, :], 0.0, ["qdB"])
    for c in range(2):
        S.memset("pool", xcb[c][:, :], 0.0, [("xcb", c)])
    for g in range(4):
        S.tt("dve", wsT[:, g, :], wsT32[:, g, :], cmf[:, :], ALU.mult, CST, ["wsT"])
    S.copy("pool", S0b[:, :], Sa[:, :], ["Sa"], ["S0b"])

    def quad_front(q):
        x_front(S, X, sb, x_d, q, xq, xbf, xT, evac)
        pn, ps = PS.get()
        for kc in range(8):
            S.mm(ps[0:16, :], wA[:, kc, CG:CG + 16], xT[:, kc, :], kc == 0, kc == 7, WA + ["xT"], [pn])
        S.act(glT[:, :], ps[0:16, :], AF.Identity, [pn] + CST, ["glT"], bias=bgl[:, 0:1], scale=1.0)
        for h in range(4):
            pn, ps = PS.get()
            for kc in range(8):
                S.mm(ps[0:64, :], wA[:, kc, CQ + h * 64:CQ + (h + 1) * 64], xT[:, kc, :], kc == 0, kc == 7, WA + ["xT"], [pn])
            S.ts("dve", qTs[:, h, :], ps[0:64, :], bqk[:, h:h + 1], 0.125, ALU.add, ALU.mult, [pn] + CST, ["qTs"])
            pn, ps = PS.get()
            for kc in range(8):
                S.mm(ps[0:64, :], wA[:, kc, CK + h * 64:CK + (h + 1) * 64], xT[:, kc, :], kc == 0, kc == 7, WA + ["xT"], [pn])
            S.act(kTs[:, h, :], ps[0:64, :], AF.Identity, [pn] + CST, ["kTs"], bias=bqk[:, 4 + h:5 + h], scale=1.0)
        for c in range(2):
            rxc = ("xcb", c)
            if q > 0:
                S.copy("pool", xcb[c][:, 0:16], xcb[c][:, 512:528], [rxc], [rxc])
            pn, ps = PS.get()
            for kc in range(8):
                S.mm(ps[:, :], wA[:, kc, CXC + c * 128:CXC + (c + 1) * 128], xT[:, kc, :], kc == 0, kc == 7, WA + ["xT"], [pn])
            S.act(xcb[c][:, 16:528], ps[:, :], AF.Identity, [pn] + CST, [rxc], bias=bxc[:, c:c + 1], scale=1.0)

    def gla_tile(tt):
        tsl = slice(tt * 128, (tt + 1) * 128)
        pnk, psk = PS.get()
        for kc in range(8):
            S.mm(psk[:, 0:256], xT[:, kc, tsl], wA[:, kc, CK:CK + 256], kc == 0, False, WA + ["xT"], [pnk])
        S.mm(psk[:, 0:256], X.onesb[0:1, :], brow[0:1, 0:256], False, True, ["onesb"] + CST, [pnk])
        pnv, psv = PS.get()
        for kc in range(8):
            S.mm(psv[:, :], xT[:, kc, tsl], wA[:, kc, CV:CV + 512], kc == 0, False, WA + ["xT"], [pnv])
        S.mm(psv[:, :], X.onesb[0:1, :], brow[0:1, 256:768], False, True, ["onesb"] + CST, [pnv])
        evac(vbf[:, :], psv[:, :], [pnv], ["vbf"])
        pnz, psz = PS.get()
        S.mm(psz[:, 0:256], glT[0:16, tsl], wg2[0:16, :], True, False, ["glT"] + CST, [pnz])
        S.mm(psz[:, 0:256], X.onesb[0:1, :], bg[0:1, :], False, True, ["onesb"] + CST, [pnz])
        S.act(e1[:, :], psz[:, 0:256], AF.Exp, [pnz], ["e1"], scale=-1.0)
        S.act(l32[:, :], e1[:, :], AF.Ln, ["e1"], ["l32"], bias=1.0, scale=1.0)
        pnb, psb = PS.get()
        for h in range(4):
            S.mm(psb[0:64, h * 128:(h + 1) * 128], l32[:, h * 64:(h + 1) * 64], tri_i[:, :], True, True, ["l32"] + CST, [pnb])
        pnr, psr = PS.get()
        S.mm(psr[:, 0:256], tri_a[:, :], l32[:, :], True, True, ["l32"] + CST, [pnr])
        S.act(ebT[:, :, :].rearrange("p a b -> p (a b)"), psb[0:64, :], AF.Exp, [pnb], ["ebT"], scale=1.0)
        S.act(enbT[:, :, :].rearrange("p a b -> p (a b)"), psb[0:64, :], AF.Exp, [pnb], ["enbT"], scale=-1.0)
        S.act(erem[:, :], psr[:, 0:256], AF.Exp, [pnr], ["erem"], scale=1.0)
        S.tt("dve", ktl[:, :], psk[:, 0:256], erem[:, :], ALU.mult, [pnk, "erem"], ["ktl"])
        kvs = []
        for c in range(2):
            pn, ps = PS.get()
            for h in range(4):
                S.mm(ps[0:64, h * 128:(h + 1) * 128], ktl[c * 64:(c + 1) * 64, h * 64:(h + 1) * 64], vbf[c * 64:(c + 1) * 64, h * 128:(h + 1) * 128], True, True, ["ktl", "vbf"], [pn])
            kvs.append((pn, ps))
        S.tt("dve", qdA[:, :, 0:64], qTs[:, :, tt * 128:tt * 128 + 64], ebT[:, :, 0:64], ALU.mult, ["qTs", "ebT"], ["qdA"])
        S.tt("dve", qdB[:, :, 64:128], qTs[:, :, tt * 128 + 64:tt * 128 + 128], ebT[:, :, 64:128], ALU.mult, ["qTs", "ebT"], ["qdB"])
        S.tt("dve", kdT[:, :, :], kTs[:, :, tsl], enbT[:, :, :], ALU.mult, ["kTs", "enbT"], ["kdT"])
        pns, pss = PS.get()
        for h in range(4):
            S.mm(pss[:, h * 128:h * 128 + 64], kdT[:, h, :], qdA[:, h, 0:64], True, True, ["kdT", "qdA"], [pns])
            S.mm(pss[:, h * 128 + 64:h * 128 + 128], kdT[:, h, :], qdB[:, h, 64:128], True, True, ["kdT", "qdB"], [pns])
        S.tt("dve", scm[:, :], pss[:, :], cm4[:, :, :].rearrange("p a b -> p (a b)"), ALU.mult, [pns] + CST, ["scm"])
        for h in range(4):
            hs = slice(h * 128, (h + 1) * 128)
            S.stt(Sb_[:, hs], Sa[:, hs], ebT[:, h, 63:64], kvs[0][1][0:64, hs], ALU.mult, ALU.add, ["Sa", "ebT", kvs[0][0]], ["Sb"])
        S.copy("pool", S1b[:, :], Sb_[:, :], ["Sb"], ["S1b"])
        pno, pso = PS.get()
        for h in range(4):
            hs = slice(h * 128, (h + 1) * 128)
            S.mm(pso[:, hs], scm[:, hs], vbf[:, hs], True, False, ["scm", "vbf"], [pno])
            S.mm(pso[:, hs], qdA[:, h, :], S0b[:, hs], False, False, ["qdA", "S0b"], [pno])
            S.mm(pso[:, hs], qdB[:, h, :], S1b[:, hs], False, True, ["qdB", "S1b"], [pno])
        for h in range(4):
            hs = slice(h * 128, (h + 1) * 128)
            S.stt(Sa[:, hs], Sb_[:, hs], ebT[:, h, 127:128], kvs[1][1][0:64, hs], ALU.mult, ALU.add, ["Sb", "ebT", kvs[1][0]], ["Sa"])
        S.copy("pool", S0b[:, :], Sa[:, :], ["Sa"], ["S0b"])
        pnr2, psr2 = PS.get()
        for kc in range(8):
            S.mm(psr2[:, :], xT[:, kc, tsl], wA[:, kc, CR:CR + 512], kc == 0, False, WA + ["xT"], [pnr2])
        S.mm(psr2[:, :], X.onesb[0:1, :], brow[0:1, 768:1280], False, True, ["onesb"] + CST, [pnr2])
        S.act(sr[:, :], psr2[:, :], AF.Silu, [pnr2], ["sr"])
        S.tt("pool", sgn[:, :], sr[:, :], gn_bc[:, :], ALU.mult, ["sr"] + CST, ["sgn"])
        S.copy("act", osb[:, :], pso[:, :], [pno], ["osb"])
        for h in range(4):
            hs = slice(h * 128, (h + 1) * 128)
            S.op("dve", (lambda a, b, d: (lambda e: e.scalar_tensor_tensor(a, b, 1.0, b, ALU.mult, ALU.mult, accum_out=d)))(junk[:, :], osb[:, hs], ssq[:, h:h + 1]),
                 ["osb"], ["junk", ("ssq", h)])
        S.act(rs4[:, 0:4], ssq[:, :], AF.Ln, [("ssq", h) for h in range(4)] + ["eps"], ["rs4a"], bias=X.eps[:, 0:1], scale=1.0 / 128.0)
        S.act(rs4[:, 4:8], rs4[:, 0:4], AF.Exp, ["rs4a"], ["rs4b"], scale=-0.5)
        for h in range(4):
            hs = slice(h * 128, (h + 1) * 128)
            S.stt(yabf[:, hs], osb[:, hs], rs4[:, 4 + h:5 + h], sgn[:, hs], ALU.mult, ALU.mult, ["osb", "rs4b", "sgn"], ["yabf"])
        pn, pst = PST[0]
        for h in range(4):
            S.tr(pst[:, h * 128:(h + 1) * 128], yabf[:, h * 128:(h + 1) * 128], X.ident[:, :], ["yabf", "ident"], [pn])
        evac(yTs[:, 0:4, tsl], pst[:, 0:512].rearrange("p (a b) -> p a b", b=128), [pn], [("yTs", "a")])

    def sgu_tile(tt):
        tsl = slice(tt * 128, (tt + 1) * 128)
        pn, ps = PS.get()
        for kc in range(8):
            S.mm(ps[:, :], xT[:, kc, tsl], wA[:, kc, CUV:CUV + 512], kc == 0, False, WA + ["xT"], [pn])
        S.mm(ps[:, :], X.onesb[0:1, :], brow[0:1, 1280:1792], False, True, ["onesb"] + CST, [pn])
        S.act(zz[:, :], ps[:, :], AF.Gelu, [pn], ["zz"])
        S.gen("dve", "bn_stats", (stats[:, :], zz[:, 256:512]), ["zz"], ["stats"])
        S.gen("dve", "bn_aggr", (mv[:, :], stats[:, :]), ["stats"], ["mv"])
        S.act(rstd[:, 0:1], mv[:, 1:2], AF.Ln, ["mv", "eps"], ["rstd0"], bias=X.eps[:, 0:1], scale=1.0)
        S.act(rstd[:, 1:2], rstd[:, 0:1], AF.Exp, ["rstd0"], ["rstd1"], scale=-0.5)
        S.ts("dve", vn[:, :], zz[:, 256:512], mv[:, 0:1], rstd[:, 1:2], ALU.subtract, ALU.mult, ["zz", "mv", "rstd1"], ["vn"])
        S.tt("pool", vn[:, :], vn[:, :], sg_bc[:, :], ALU.mult, ["vn"] + CST, ["vn"])
        S.tt("pool", vnb[:, :], vn[:, :], sb_bc[:, :], ALU.add, ["vn"] + CST, ["vnb"])
        pn2, ps2 = PS.get()
        for g in range(4):
            S.mm(ps2[:, g * 64:(g + 1) * 64], wsT[:, g, :], vnb[:, g * 64:(g + 1) * 64], True, True, ["wsT", "vnb"], [pn2])
        for g in range(4):
            gs = slice(g * 64, (g + 1) * 64)
            S.stt(ybbf[:, gs], ps2[:, gs], bsT[:, g:g + 1], zz[:, gs], ALU.add, ALU.mult, [pn2, "zz"] + CST, ["ybbf"])
        pn, pst = PST[1]
        for c in range(2):
            S.tr(pst[:, c * 128:(c + 1) * 128], ybbf[:, c * 128:(c + 1) * 128], X.ident[:, :], ["ybbf", "ident"], [pn])
        evac(yTs[:, 4:6, tsl], pst[:, 0:256].rearrange("p (a b) -> p a b", b=128), [pn], [("yTs", "b")])

    def pool_quad(q):
        for c in range(2):
            rxc = ("xcb", c)
            Xc = xcb[c]
            A = pa[c]
            B = pb[c]
            S.tt("pool", A[:, 1:528], Xc[:, 1:528], Xc[:, 0:527], ALU.add, [rxc], [("pa", c)])
            S.tt("pool", B[:, 3:528], A[:, 3:528], A[:, 1:526], ALU.add, [("pa", c)], [("pb", c)])
            if c == 0:
                S.stt(ypre[0:64, 0, :], A[0:64, 16:528], 0.5, Xc[0:64, 16:528], ALU.mult, ALU.subtract, [("pa", c), rxc], [("ypre", 0)])
                S.stt(ypre[64:128, 0, :], B[64:128, 16:528], 0.25, Xc[64:128, 16:528], ALU.mult, ALU.subtract, [("pb", c), rxc], [("ypre", 0)])
            else:
                S.tt("pool", A[:, 7:528], B[:, 7:528], B[:, 3:524], ALU.add, [("pb", c)], [("pa", c)])
                S.tt("pool", B[64:128, 15:528], A[64:128, 15:528], A[64:128, 7:520], ALU.add, [("pa", c)], [("pb", c)])
                S.stt(ypre[0:64, 1, :], A[0:64, 16:528], 0.125, Xc[0:64, 16:528], ALU.mult, ALU.subtract, [("pa", c), rxc], [("ypre", 1)])
                S.stt(ypre[64:128, 1, :], B[64:128, 16:528], 0.0625, Xc[64:128, 16:528], ALU.mult, ALU.subtract, [("pb", c), rxc], [("ypre", 1)])
            if q == 0:
                for (Z, p0, p1, nm) in ((A, 0, 64, "pa"), (B, 64, 128, "pb")):
                    S.tt("dve", ptmp[p0:p1, :], Z[p0:p1, 16:32], invc[p0:p1, c, :], ALU.mult, [(nm, c)] + CST, ["ptmp"])
                    S.tt("dve", ypre[p0:p1, c, 0:16], ptmp[p0:p1, :], Xc[p0:p1, 16:32], ALU.subtract, ["ptmp", rxc], [("ypre", c)])
            pn, ps = PS.get()
            S.mm(ps[:, :], pw[:, c, :], ypre[:, c, :], True, True, [("ypre", c)] + CST, [pn])
            S.ts("dve", yTs[:, 6 + c, :], ps[:, :], psc[:, c:c + 1], None, ALU.mult, None, [pn] + CST, [("yTs", "c", c)])

    for q in range(NQ):
        quad_front(q)
        for tt in range(4):
            gla_tile(tt)
            sgu_tile(tt)
        pool_quad(q)
        S.dma("sp", yT_d[:, q * 512:(q + 1) * 512].rearrange("(k p) t -> p k t", p=128), yTs[:, :, :],
              [("yTs", "a"), ("yTs", "b"), ("yTs", "c", 0), ("yTs", "c", 1)], [], key="yst")


def phase_1b(nc, S, X, sb, l, x_d, yT_d, y_d, D):
    PS = X.PS
    evac = make_evac(S)
    win_d = D["w_in"][l]
    wg = sb("wg", [128, 8, 3072], BF16)
    wup = sb("wup", [128, 8, 1024], BF16)
    wo = sb("wo", [128, 8, 1024], BF16)
    bgate = sb("bgate", [128, 24], F32)
    g_bc = sb("g_bc", [128, 1024], F32)
    b_bc = sb("b_bc", [128, 1024], F32)
    xq = [sb(f"xq{i}", [128, 4, 1024], F32) for i in range(2)]
    xbf = sb("xbf", [128, 4, 1024], BF16)
    xT = sb("xT", [128, 8, 512], BF16)
    yTq = sb("yTq", [128, 8, 512], BF16)
    sg = [sb(f"sg{j}", [128, 512], F32) for j in range(3)]
    mm_ = [sb(f"m{j}", [128, 512], F32) for j in range(3)]
    mT = sb("mT", [128, 8, 512], BF16)
    yb = [sb(f"yb{i}", [128, 1024], F32) for i in range(2)]
    ob = [sb(f"ob{i}", [128, 1024], F32) for i in range(2)]
    stats = [sb(f"st{i}", [128, 2, 6], F32) for i in range(2)]
    mv = [sb(f"mv{i}", [128, 2], F32) for i in range(2)]
    rstd = [sb(f"rstd{i}", [128, 2], F32) for i in range(2)]

    for j in range(3):
        S.dma("pool", wg[:, :, j * 1024:(j + 1) * 1024], win_d[:, GATE0 + j * 1024:GATE0 + (j + 1) * 1024].rearrange("(k p) n -> p k n", p=128), [], [("wg", j)], key=("wA", j))
    for r0, r1, nm in ((0, 4, "w_up_a"), (4, 6, "w_up_b"), (6, 8, "w_up_c")):
        S.dma("pool", wup[:, r0:r1, :], D[nm][l].rearrange("(k p) n -> p k n", p=128), [], [("wup", r0)], key="wup")
    WUP = [("wup", 0), ("wup", 4), ("wup", 6)]
    S.dma("pool", wo[:, :, :], D["w_o"][l].rearrange("(k p) n -> p k n", p=128), [], ["wo"], key="wo")
    S.dma("sp", bgate[:, :], D["b_gateT"][l], [], ["bgate"], key="cst")
    S.dma("sp", g_bc[:, :], D["ln1_g"][l].partition_broadcast(128), [], ["gb"], key="cst")
    S.dma("sp", b_bc[:, :], D["ln1_b"][l].partition_broadcast(128), [], ["gb2"], key="cst")

    for q in range(NQ):
        x_front(S, X, sb, x_d, q, xq, xbf, xT, evac)
        S.dma("pool", yTq[:, :, :], yT_d[:, q * 512:(q + 1) * 512].rearrange("(k p) t -> p k t", p=128), [], ["yTq"], key="yld")
        for fo in range(8):
            fsl = slice(fo * 128, (fo + 1) * 128)
            ups = []
            for (k0, k1) in ((0, 4), (4, 6), (6, 8)):
                pn, ps = PS.get()
                for kc in range(k0, k1):
                    S.mm(ps[:, :], wup[:, kc, fsl], yTq[:, kc, :], kc == k0, kc == k1 - 1, WUP + ["yTq"], [pn])
                ups.append((pn, ps))
            for j in range(3):
                pn, ps = PS.get()
                for kc in range(8):
                    S.mm(ps[:, :], wg[:, kc, j * 1024 + fo * 128:j * 1024 + (fo + 1) * 128], xT[:, kc, :], kc == 0, kc == 7, [("wg", j), "xT"], [pn])
                S.act(sg[j][:, :], ps[:, :], AF.Identity, [pn, "bgate"], [("sg", j)], bias=bgate[:, j * 8 + fo:j * 8 + fo + 1], scale=1.0)
                S.act(sg[j][:, :], sg[j][:, :], AF.Sigmoid, [("sg", j)], [("sg", j)], scale=1.0)
                S.tt("dve", mm_[j][:, :], ups[j][1][:, :], sg[j][:, :], ALU.mult, [ups[j][0], ("sg", j)], [("m", j)])
            S.tt("pool", mm_[0][:, :], mm_[0][:, :], mm_[1][:, :], ALU.add, [("m", 0), ("m", 1)], [("m", 0)])
            S.tt("pool", mT[:, fo, :], mm_[0][:, :], mm_[2][:, :], ALU.add, [("m", 0), ("m", 2)], ["mT"])
        for tt in range(4):
            i = tt % 2
            ry = ("y", i)
            for hf in range(2):
                pn, ps = PS.get()
                for kc in range(8):
                    S.mm(ps[:, :], mT[:, kc, tt * 128:(tt + 1) * 128], wo[:, kc, hf * 512:(hf + 1) * 512], kc == 0, kc == 7, ["mT", "wo"], [pn])
                S.stt(yb[i][:, hf * 512:(hf + 1) * 512], xq[q % 2][:, tt, hf * 512:(hf + 1) * 512], ALPHA, ps[:, :], ALU.mult, ALU.add, [pn, ("x", q % 2)], [ry])
            ln_tail(S, X, yb[i], stats[i], mv[i], rstd[i], g_bc, b_bc, ob[i][:, :], (ry, ("st", i), ("o", i)))
            tok0 = q * 512 + tt * 128
            S.dma("sp", y_d[tok0:tok0 + 128, :], ob[i][:, :], [("o", i)], [], key=("yst", i))


def phase_2(nc, S, X, sb, l, x_d, y_d, D):
    PS = X.PS
    evac = make_evac(S)
    w_bf = {n: sb(n + "_bf", [128, 8, 1024], BF16) for n in ("xa_wq", "xa_wk", "xa_wv", "xa_wo")}
    memT_bf = sb("memT_bf", [128, 8, 256], BF16)
    kT_bf = sb("kT_bf", [128, 8, 256], BF16)
    V_bf = sb("V_bf", [128, 2, 1024], BF16)
    g_bc = sb("g_bc", [128, 1024], F32)
    b_bc = sb("b_bc", [128, 1024], F32)
    xq = [sb(f"xq{i}", [128, 4, 1024], F32) for i in range(2)]
    xbf = sb("xbf", [128, 4, 1024], BF16)
    xT = sb("xT", [128, 8, 512], BF16)
    qT = sb("qT", [128, 8, 512], BF16)
    PT = [sb(f"PT{i}", [128, 2, 512], BF16) for i in range(2)]
    rs = sb("rs", [128, 512], F32)
    oT = sb("oT", [128, 8, 512], BF16)
    yb = [sb(f"yb{i}", [128, 1024], F32) for i in range(2)]
    ob = [sb(f"ob{i}", [128, 1024], F32) for i in range(2)]
    stats = [sb(f"st{i}", [128, 2, 6], F32) for i in range(2)]
    mv = [sb(f"mv{i}", [128, 2], F32) for i in range(2)]
    rstd = [sb(f"rstd{i}", [128, 2], F32) for i in range(2)]
    for n in w_bf:
        S.dma("pool", w_bf[n][:, :, :], D[n][l].rearrange("(k p) n -> p k n", p=128), [], [n], key=n)
    S.dma("pool", memT_bf[:, :, :], D["memT"].rearrange("(k p) n -> p k n", p=128), [], ["memT"], key="cst")
    S.dma("sp", g_bc[:, :], D["ln2_g"][l].partition_broadcast(128), [], ["gb"], key="cst")
    S.dma("sp", b_bc[:, :], D["ln2_b"][l].partition_broadcast(128), [], ["gb2"], key="cst")
    for fc in range(8):
        pn, ps = PS.get()
        for kc in range(8):
            S.mm(ps[:, 0:256], w_bf["xa_wk"][:, kc, fc * 128:(fc + 1) * 128], memT_bf[:, kc, :], kc == 0, kc == 7, ["xa_wk", "memT"], [pn])
        evac(kT_bf[:, fc, :], ps[:, 0:256], [pn], ["kT"])
    for mc in range(2):
        for hf in range(2):
            pn, ps = PS.get()
            for kc in range(8):
                S.mm(ps[:, :], memT_bf[:, kc, mc * 128:(mc + 1) * 128], w_bf["xa_wv"][:, kc, hf * 512:(hf + 1) * 512], kc == 0, kc == 7, ["xa_wv", "memT"], [pn])
            evac(V_bf[:, mc, hf * 512:(hf + 1) * 512], ps[:, :], [pn], ["V"])
    for q in range(NQ):
        xb = xq[q % 2]
        rx = ("x", q % 2)
        x_front(S, X, sb, x_d, q, xq, xbf, xT, evac)
        for fc in range(8):
            pn, ps = PS.get()
            for kc in range(8):
                S.mm(ps[:, :], w_bf["xa_wq"][:, kc, fc * 128:(fc + 1) * 128], xT[:, kc, :], kc == 0, kc == 7, ["xa_wq", "xT"], [pn])
            evac(qT[:, fc, :], ps[:, :], [pn], ["qT"])
        for h in range(4):
            pt = PT[h % 2]
            rpt = ("PT", h % 2)
            for mc in range(2):
                pn, ps = PS.get()
                for dc in range(2):
                    S.mm(ps[:, :], kT_bf[:, h * 2 + dc, mc * 128:(mc + 1) * 128], qT[:, h * 2 + dc, :], dc == 0, dc == 1, ["kT", "qT"], [pn])
                S.act(pt[:, mc, :], ps[:, :], AF.Exp, [pn], [rpt], scale=1.0 / 16.0)
            pn, ps = PS.get()
            for mc in range(2):
                S.mm(ps[:, :], X.ones128[:, :], pt[:, mc, :], mc == 0, mc == 1, ["ones128", rpt], [pn])
            S.gen("dve", "reciprocal", (rs[:, :], ps[:, :]), [pn], ["rs"])
            for dc in range(2):
                pn, ps = PS.get()
                for mc in range(2):
                    S.mm(ps[:, :], V_bf[:, mc, h * 256 + dc * 128:h * 256 + (dc + 1) * 128], pt[:, mc, :], mc == 0, mc == 1, ["V", rpt], [pn])
                S.tt("dve", oT[:, h * 2 + dc, :], ps[:, :], rs[:, :], ALU.mult, [pn, "rs"], ["oT"])
        for tt in range(4):
            i = tt % 2
            ry = ("y", i)
            for hf in range(2):
                pn, ps = PS.get()
                for kc in range(8):
                    S.mm(ps[:, :], oT[:, kc, tt * 128:(tt + 1) * 128], w_bf["xa_wo"][:, kc, hf * 512:(hf + 1) * 512], kc == 0, kc == 7, ["oT", "xa_wo"], [pn])
                S.stt(yb[i][:, hf * 512:(hf + 1) * 512], xb[:, tt, hf * 512:(hf + 1) * 512], ALPHA, ps[:, :], ALU.mult, ALU.add, [pn, rx], [ry])
            ln_tail(S, X, yb[i], stats[i], mv[i], rstd[i], g_bc, b_bc, ob[i][:, :], (ry, ("st", i), ("o", i)))
            tok0 = q * 512 + tt * 128
            S.dma("sp", y_d[tok0:tok0 + 128, :], ob[i][:, :], [("o", i)], [], key=("yst", i))


def phase_3(nc, S, X, sb, l, x_d, y_d, xg_d, yg_d, D):
    PS = X.PS
    PST = X.PST
    evac = make_evac(S)
    wgu_d = D["exp_w_gu"][l]
    wd_d = D["exp_w_down"][l]
    bd_d = D["exp_b_down"][l]
    X.phase_base = sb.off
    wgu = [sb(f"wgu{i}", [128, 8, 2048], BF16) for i in range(2)]
    wd = [sb(f"wd{i}", [128, 8, 1024], BF16) for i in range(2)]
    X.p3_weights_end = sb.off
    bd = [sb(f"bd{i}", [1, 1024], BF16) for i in range(2)]
    bgu = sb("bgu", [128, NE * 16], F32)
    rw = sb("rw", [128, 8, NE], F32)
    rb = sb("rb", [1, NE], F32)
    ident32 = sb("ident32", [128, 128], F32)
    su = sb("su_bf", [128, 128], BF16)
    ones32 = sb("ones32", [1, 128], F32)
    ones4 = sb("ones4", [128, 4], F32)
    eoff = sb("eoff_sb", [128, NE], F32)
    g_bc = sb("g_bc", [128, 1024], F32)
    b_bc = sb("b_bc", [128, 1024], F32)
    xt = sb("xt", [128, 1024], F32)
    xbf = [sb(f"xbf{i}", [128, 1024], BF16) for i in range(2)]
    xT32 = sb("xT32", [128, 8, 128], F32)
    lg = sb("lg", [128, NE], F32)
    work = sb("work", [128, NE], F32)
    tmx = sb("tmx", [128, 32], F32)
    mk = sb("mk", [128, 4], F32)
    ohs = sb("ohs", [128, 4, NE], F32)
    num4 = sb("num4", [128, 4], F32)
    negm = sb("negm", [128, 1], F32)
    ex = sb("ex", [128, NE], F32)
    den = sb("den", [128, 2], F32)
    maskf = sb("maskf", [128, NE], F32)
    maskb = sb("maskb", [128, NT, NE], BF16)
    destf = sb("destf", [128, NE], F32)
    junk = sb("junk", [128, NE], F32)
    d4f = sb("d4f", [128, NT, 4], F32)
    d4i = sb("d4i", [128, NT, 4], I32)
    g4 = sb("g4", [128, NT, 4], F32)
    xgs = [sb(f"xgs{i}", [128, 3, 1024], BF16) for i in range(2)]
    xgT = [sb(f"xgT{i}", [128, 8, CB], BF16) for i in range(2)]
    aT = [sb(f"aT{i}", [128, 8, CB], BF16) for i in range(2)]
    tg = [sb(f"tg{i}", [128, CB], F32) for i in range(2)]
    tsg = [sb(f"tsg{i}", [128, CB], F32) for i in range(2)]
    tl0 = [sb(f"tl0{i}", [128, CB], F32) for i in range(2)]
    tl1 = sb("tl1", [128, CB], F32)
    tgs = sb("tgs", [128, CB], F32)
    ysb = [sb(f"ysb{i}", [128, 1024], F32) for i in range(2)]
    stats = sb("stats", [128, 2, 6], F32)
    mv = sb("mv", [128, 2], F32)
    rstd = sb("rstd", [128, 2], F32)

    S.dma("sp", ident32[:, :], D["ident"], [], ["id32"], key="cst")
    S.dma("pool", su[:, :], D["su"], [], ["su"], key="cst")
    S.dma("sp", eoff[:, :], D["eoff"], [], ["eoff"], key="cst")
    S.dma("sp", rw[:, :, :], D["router_w"][l].rearrange("(k p) n -> p k n", p=128), [], ["rw"], key="cst")
    S.dma("sp", rb[:, :], D["router_b"][l], [], ["rb"], key="cst")
    S.dma("sp", bgu[:, :], D["b_guT"][l], [], ["bgu"], key="cst")
    S.dma("sp", g_bc[:, :], D["ln3_g"][l].partition_broadcast(128), [], ["gb"], key="cst")
    S.dma("sp", b_bc[:, :], D["ln3_b"][l].partition_broadcast(128), [], ["gb2"], key="cst")
    S.memset("dve", ones32[:, :], 1.0, ["ones32"])
    S.memset("dve", ones4[:, :], 1.0, ["ones4"])

    def load_expert(e):
        i = e % 2
        S.dma("pool", wgu[i][:, :, :], wgu_d[e].rearrange("(k p) n -> p k n", p=128), [], [("wgu", i, 0), ("wgu", i, 1)], key=("wgu", i))
        S.dma("pool", wd[i][:, :, :], wd_d[e].rearrange("(k p) n -> p k n", p=128), [], [("wd", i)], key=("wd", i))
        S.dma("pool", bd[i][:, :], bd_d[e:e + 1, :], [], [("bd", i)], key=("bd", i))

    load_expert(0)
    load_expert(1)

    def stt_acc(out, in0, in1, accum, reads, writes):
        S.op("dve", (lambda a, b, c, d: (lambda e: e.scalar_tensor_tensor(a, b, 1.0, c, ALU.mult, ALU.mult, accum_out=d)))(out, in0, in1, accum), reads, writes)

    for j in range(NT):
        S.dma("sp", xt[:, :], x_d[j * 128:(j + 1) * 128, :], [], ["xt"], key="xt")
        xb = xbf[j % 2]
        rxb = ("xbf", j % 2)
        S.copy("pool", xb[:, :], xt[:, :], ["xt"], [rxb])
        for fc in range(8):
            pn, ps = PS.get()
            S.tr(ps[:, 0:128], xt[:, fc * 128:(fc + 1) * 128], ident32[:, :], ["xt", "id32"], [pn])
            evac(xT32[:, fc, :], ps[:, 0:128], [pn], ["xT32"])
        pn, ps = PS.get()
        for kc in range(8):
            S.mm(ps[:, 0:NE], xT32[:, kc, :], rw[:, kc, :], kc == 0, False, ["xT32", "rw"], [pn])
        S.mm(ps[:, 0:NE], ones32[0:1, :], rb[0:1, :], False, True, ["ones32", "rb"], [pn])
        S.copy("dve", lg[:, :], ps[:, 0:NE], [pn], ["lg"])
        S.copy("dve", work[:, :], lg[:, :], ["lg"], ["work"])
        for k in range(4):
            S.tt("dve", tmx[:, 0:16], work[:, 0:16], work[:, 16:32], ALU.max, ["work"], ["tmx"])
            S.tt("dve", tmx[:, 16:24], tmx[:, 0:8], tmx[:, 8:16], ALU.max, ["tmx"], ["tmx"])
            S.tt("dve", tmx[:, 24:28], tmx[:, 16:20], tmx[:, 20:24], ALU.max, ["tmx"], ["tmx"])
            S.tt("dve", tmx[:, 28:30], tmx[:, 24:26], tmx[:, 26:28], ALU.max, ["tmx"], ["tmx"])
            S.tt("dve", mk[:, k:k + 1], tmx[:, 28:29], tmx[:, 29:30], ALU.max, ["tmx"], ["mk"])
            S.ts("dve", ohs[:, k, :], work[:, :], mk[:, k:k + 1], None, ALU.is_equal, None, ["work", "mk"], ["ohs"])
            S.stt(work[:, :], ohs[:, k, :], -1e30, work[:, :], ALU.mult, ALU.add, ["ohs", "work"], ["work"])
        S.ts("dve", maskf[:, :], lg[:, :], mk[:, 3:4], None, ALU.is_ge, None, ["lg", "mk"], ["maskf"])
        S.copy("dve", maskb[:, j, :], maskf[:, :], ["maskf"], [("maskb", j)])
        S.ts("dve", negm[:, :], mk[:, 0:1], -1.0, None, ALU.mult, None, ["mk"], ["negm"])
        S.act(ex[:, :], lg[:, :], AF.Exp, ["lg", "negm"], ["ex"], bias=negm[:, 0:1], scale=1.0)
        pn, ps = PS.get()
        for i in range(j):
            S.mm(ps[:, 0:NE], X.ones128[:, :], maskb[:, i, :], i == 0, False, ["ones128", ("maskb", i)], [pn])
        S.mm(ps[:, 0:NE], su[:, :], maskb[:, j, :], j == 0, True, ["su", ("maskb", j)], [pn])
        S.ts("dve", destf[:, :], ps[:, 0:NE], float(CAP - 1), None, ALU.min, None, [pn], ["destf"])
        S.tt("dve", destf[:, :], destf[:, :], eoff[:, :], ALU.add, ["destf", "eoff"], ["destf"])
        for k in range(4):
            stt_acc(junk[:, :], ohs[:, k, :], destf[:, :], d4f[:, j, k:k + 1], ["ohs", "destf"], ["junk", ("d4f", j)])
            stt_acc(junk[:, :], ohs[:, k, :], ex[:, :], num4[:, k:k + 1], ["ohs", "ex"], ["junk", "num4"])
        stt_acc(junk[:, 0:4], num4[:, :], ones4[:, :], den[:, 0:1], ["num4", "ones4"], ["junk", "den0"])
        S.gen("dve", "reciprocal", (den[:, 1:2], den[:, 0:1]), ["den0"], ["den1"])
        S.ts("dve", g4[:, j, :], num4[:, :], den[:, 1:2], None, ALU.mult, None, ["num4", "den1"], [("g4", j)])
        S.copy("dve", d4i[:, j, :], d4f[:, j, :], [("d4f", j)], [("d4i", j)])
        for k in range(4):
            S.op("pool", (lambda o_, i_, idx: (lambda e: e.indirect_dma_start(o_, bass.IndirectOffsetOnAxis(ap=idx, axis=0), i_, None)))(xg_d[:, :], xb[:, :], d4i[:, j, k:k + 1]),
                 [rxb, ("d4i", j)], [("xg", j, k)], key="xgsc")

    XG_ALL = [("xg", j, k) for j in range(NT) for k in range(4)]

    def load_xg(e_, bi_):
        t0_, t1_ = BLOCKS[bi_]
        b_ = (e_ * len(BLOCKS) + bi_) % 2
        for st in range(t1_ - t0_):
            r0 = e_ * CAP + (t0_ + st) * 128
            S.dma("sp", xgs[b_][:, st, :], xg_d[r0:r0 + 128, :], XG_ALL, [("xgs", b_, st)], key=("xgs", b_, st))
    YG_ALL = []
    for e in range(NE):
        i = e % 2
        for bi, (t0, t1) in enumerate(BLOCKS):
            nst = t1 - t0
            cb = nst * 128
            bb = (e * len(BLOCKS) + bi) % 2
            if e == 0 and bi == 0:
                load_xg(0, 0)
            nb = e * len(BLOCKS) + bi + 1
            if nb < NE * len(BLOCKS):
                load_xg(nb // len(BLOCKS), nb % len(BLOCKS))
            for fc in range(8):
                pn, pst = PST[fc % 2]
                for st in range(nst):
                    S.tr(pst[:, st * 128:(st + 1) * 128], xgs[bb][:, st, fc * 128:(fc + 1) * 128], X.ident[:, :], [("xgs", bb, st), "ident"], [pn])
                evac(xgT[bb][:, fc, 0:cb], pst[:, 0:cb], [pn], [("xgT", bb)])
            def stage_a(jj):
                d2 = jj % 2
                png, psg = PS.get()
                for kc in range(8):
                    S.mm(psg[:, 0:cb], wgu[i][:, kc, jj * 128:(jj + 1) * 128], xgT[bb][:, kc, 0:cb], kc == 0, kc == 7, [("wgu", i, 0), ("xgT", bb)], [png])
                pnl, psl = PS.get()
                for kc in range(8):
                    S.mm(psl[:, 0:cb], wgu[i][:, kc, 1024 + jj * 128:1024 + (jj + 1) * 128], xgT[bb][:, kc, 0:cb], kc == 0, kc == 7, [("wgu", i, 1), ("xgT", bb)], [pnl])
                cg = e * 16 + jj
                cl = e * 16 + 8 + jj
                S.ts("dve", tg[d2][:, 0:cb], psg[:, 0:cb], bgu[:, cg:cg + 1], 7.0, ALU.add, ALU.min, [png, "bgu"], [("tg", d2)])
                S.act(tsg[d2][:, 0:cb], tg[d2][:, 0:cb], AF.Sigmoid, [("tg", d2)], [("tsg", d2)], scale=1.702)
                S.act(tl0[d2][:, 0:cb], psl[:, 0:cb], AF.Identity, [pnl, "bgu"], [("tl0", d2)], bias=bgu[:, cl:cl + 1], scale=1.0)

            def stage_b(jj):
                d2 = jj % 2
                S.ts("dve", tl1[:, 0:cb], tl0[d2][:, 0:cb], -7.0, 7.0, ALU.max, ALU.min, [("tl0", d2)], ["tl1"])
                S.tt("dve", tgs[:, 0:cb], tg[d2][:, 0:cb], tsg[d2][:, 0:cb], ALU.mult, [("tg", d2), ("tsg", d2)], ["tgs"])
                S.stt(aT[bb][:, jj, 0:cb], tl1[:, 0:cb], 1.0, tgs[:, 0:cb], ALU.add, ALU.mult, ["tl1", "tgs"], [("aT", bb)])

            stage_a(0)
            for jj in range(1, 8):
                stage_a(jj)
                stage_b(jj - 1)
            stage_b(7)
            for st in range(nst):
                yi = st % 2
                for hf in range(2):
                    pn, ps = PS.get()
                    for kc in range(8):
                        S.mm(ps[:, :], aT[bb][:, kc, st * 128:(st + 1) * 128], wd[i][:, kc, hf * 512:(hf + 1) * 512], kc == 0, False, [("aT", bb), ("wd", i)], [pn])
                    S.mm(ps[:, :], X.onesb[0:1, :], bd[i][0:1, hf * 512:(hf + 1) * 512], False, True, ["onesb", ("bd", i)], [pn])
                    evac(ysb[yi][:, hf * 512:(hf + 1) * 512], ps[:, :], [pn], [("ysb", yi)])
                r0 = e * CAP + (t0 + st) * 128
                S.dma("sp", yg_d[r0:r0 + 128, :], ysb[yi][:, :], [("ysb", yi)], [("yg", e, t0 + st)], key="ygst")
                YG_ALL.append(("yg", e, t0 + st))
        if e + 2 < NE:
            load_expert(e + 2)

    S.barrier()
    keep = sb.off
    sb.reset(X.phase_base)
    yk = [[sb(f"yk{b}{i}", [128, 1024], F32) for i in range(4)] for b in range(2)]
    acc2 = [sb(f"acc{b}", [128, 1024], F32) for b in range(2)]
    ob2 = [sb(f"ob{b}", [128, 1024], F32) for b in range(2)]
    xt2 = [sb(f"xt{b}", [128, 1024], F32) for b in range(2)]
    st2 = [sb(f"stc{b}", [128, 2, 6], F32) for b in range(2)]
    mv2 = [sb(f"mvc{b}", [128, 2], F32) for b in range(2)]
    rs2 = [sb(f"rsc{b}", [128, 2], F32) for b in range(2)]
    assert sb.off <= X.p3_weights_end
    sb.reset(keep)
    def fetch(j):
        b = j % 2
        S.dma("sp", xt2[b][:, :], x_d[j * 128:(j + 1) * 128, :], [], [("xt", b)], key=("xtc", b))
        for k in range(4):
            S.op("pool", (lambda o_, i_, idx: (lambda e: e.indirect_dma_start(o_, None, i_, bass.IndirectOffsetOnAxis(ap=idx, axis=0))))(yk[b][k][:, :], yg_d[:, :], d4i[:, j, k:k + 1]),
                 [], [("yk", b, k)], key=("ykg", b, k))

    fetch(0)
    for j in range(NT):
        b = j % 2
        if j + 1 < NT:
            fetch(j + 1)
        S.ts("dve", acc2[b][:, :], yk[b][0][:, :], g4[:, j, 0:1], None, ALU.mult, None, [("yk", b, 0)], [("acc", b)])
        for k in range(1, 4):
            S.stt(acc2[b][:, :], yk[b][k][:, :], g4[:, j, k:k + 1], acc2[b][:, :], ALU.mult, ALU.add, [("yk", b, k), ("acc", b)], [("acc", b)])
        S.stt(acc2[b][:, :], xt2[b][:, :], ALPHA, acc2[b][:, :], ALU.mult, ALU.add, [("xt", b), ("acc", b)], [("acc", b)])
        ln_tail(S, X, acc2[b], st2[b], mv2[b], rs2[b], g_bc, b_bc, ob2[b][:, :], (("acc", b), ("stc", b), ("ob", b)), eng_g="dve")
        S.dma("sp", y_d[j * 128:(j + 1) * 128, :], ob2[b][:, :], [("ob", b)], [], key=("yst3", b))


def consts():
    s = np.arange(128)[:, None]; t = np.arange(128)[None, :]
    same = (s // 64) == (t // 64)
    c = {}
    c['ident'] = np.eye(128, dtype=np.float32)
    c['tri_i'] = np.where((s <= t) & same, -1.0 / 16.0, 0.0).astype(np.float32)
    c['tri_a'] = np.where((s > t) & same, -1.0 / 16.0, 0.0).astype(np.float32)
    c['cmask'] = ((s <= t) & same).astype(np.float32)
    c['cmfull'] = (s <= t).astype(np.float32)
    c['su'] = (s < t).astype(np.float32)
    return c

def invc_for(first_half):
    out = np.zeros((128, 2, 16), np.float32)
    tt = np.arange(16)
    for gi, w in enumerate((2, 4, 8, 16)):
        cc, p0 = gi // 2, (gi % 2) * 64
        cnt = np.minimum(tt + 1, w) if first_half else np.full(16, w)
        out[p0:p0 + 64, cc, :] = (1.0 / cnt).astype(np.float32)[None, :]
    return out

def mixer_a_inputs(d, l):
    b_in = d['b_in'][l]
    m = {}
    m['w_in'] = d['w_in'][l]
    m['bqk'] = np.ascontiguousarray(np.concatenate([b_in[0:256].reshape(4, 64).T, b_in[256:512].reshape(4, 64).T], axis=1))
    m['bglow'] = np.ascontiguousarray(b_in[1536:1552].reshape(16, 1))
    m['brow'] = np.ascontiguousarray(np.concatenate([b_in[256:512], b_in[512:1024], b_in[1024:1536], b_in[1552:2064]]).reshape(1, 1792))
    m['bxc'] = np.ascontiguousarray(b_in[2064:2320].reshape(2, 128).T)
    m['wg2'] = d['gla_wg2'][l]
    m['bg'] = np.ascontiguousarray(d['gla_bg'][l].reshape(1, 256))
    m['gnorm'] = d['gla_norm_g'][l]
    m['sgu_g'] = d['sgu_ln_g'][l]
    m['sgu_b'] = d['sgu_ln_b'][l]
    m['wsT'] = np.ascontiguousarray(d['sgu_ws'][l].transpose(2, 0, 1))
    m['bsT'] = np.ascontiguousarray(d['sgu_bs'][l].T)
    pw = d['pool_w'][l]
    bd = np.zeros((128, 2, 128), np.float32)
    for gi in range(4):
        cc, p0 = gi // 2, (gi % 2) * 64
        bd[p0:p0 + 64, cc, p0:p0 + 64] = pw[gi]
    m['pwbd'] = bd
    m['pscT'] = np.ascontiguousarray(d['pool_scale'][l].reshape(2, 128).T)
    return m

def mixer_b_inputs(d, l):
    b_in = d['b_in'][l]
    m = {}
    m['w_in'] = d['w_in'][l]
    m['b_gateT'] = np.ascontiguousarray(b_in[2320:5392].reshape(24, 128).T)
    m['w_up'] = np.ascontiguousarray(np.concatenate([d['w_up_a'][l], d['w_up_b'][l], d['w_up_c'][l]], axis=0))
    m['w_o'] = d['w_o'][l]
    m['ln_g'] = d['ln1_g'][l]
    m['ln_b'] = d['ln1_b'][l]
    return m


N_CORES = 8


def build_program():
    nc = bass.Bass("TRN2", target_bir_lowering=False)
    D = {}

    def din(name, shape):
        D[name] = nc.dram_tensor(name, shape, F32, kind="ExternalInput").ap()
        return D[name]

    x_in = din("x", [T, 1024])
    din("memT", [1024, 256])
    din("w_in", [2, 1024, 5392])
    din("bqk", [2, 64, 8]); din("bglow", [2, 16, 1]); din("brow", [2, 1, 1792]); din("bxc", [2, 128, 2])
    din("wg2", [2, 16, 256]); din("bg", [2, 1, 256]); din("gnorm", [2, 512]); din("sgu_g", [2, 256]); din("sgu_b", [2, 256])
    din("wsT", [2, 128, 4, 128]); din("bsT", [2, 128, 4]); din("pwbd", [2, 128, 2, 128]); din("pscT", [2, 128, 2])
    din("b_gateT", [2, 128, 24])
    din("w_up_a", [2, 512, 1024]); din("w_up_b", [2, 256, 1024]); din("w_up_c", [2, 256, 1024]); din("w_o", [2, 1024, 1024])
    for n in ("ln1_g", "ln1_b", "ln2_g", "ln2_b", "ln3_g", "ln3_b"):
        din(n, [2, 1024])
    for n in ("xa_wq", "xa_wk", "xa_wv", "xa_wo"):
        din(n, [2, 1024, 1024])
    din("router_w", [2, 1024, NE]); din("router_b", [2, 1, NE])
    din("exp_w_gu", [2, NE, 1024, 2048]); din("exp_w_down", [2, NE, 1024, 1024])
    din("b_guT", [2, 128, NE * 16]); din("exp_b_down", [2, NE, 1024])
    for n in ("ident", "tri_i", "tri_a", "cmask", "cmfull", "su"):
        din(n, [128, 128])
    din("eoff", [128, NE]); din("invc", [128, 2, 16])
    y_out = nc.dram_tensor("y", [T, 1024], F32, kind="ExternalOutput").ap()
    xa = nc.dram_tensor("xa_s", [T, 1024], F32, kind="Internal").ap()
    xb = nc.dram_tensor("xb_s", [T, 1024], F32, kind="Internal").ap()
    xc = nc.dram_tensor("xc_s", [T, 1024], F32, kind="Internal").ap()
    yT = nc.dram_tensor("yT_s", [1024, T], F32, kind="Internal").ap()
    xg = nc.dram_tensor("xg_s", [NE * CAP, 1024], BF16, kind="Internal").ap()
    yg = nc.dram_tensor("yg_s", [NE * CAP, 1024], F32, kind="Internal").ap()

    S = Sched(nc)
    sb = Arena(nc)
    X = Ctx()
    X.PS = PsumPool(nc, [f"ps{i}" for i in range(6)])
    X.PST = [(f"pst{i}", nc.alloc_psum_tensor(f"pst{i}", [128, 1024], BF16)) for i in range(2)]
    X.ident = sb("ident_bf", [128, 128], BF16)
    X.onesb = sb("onesb", [1, 128], BF16)
    X.ones128 = sb("ones128", [128, 128], BF16)
    X.eps = sb("eps_t", [128, 1], F32)
    base = sb.off
    S.dma("pool", X.ident[:, :], D["ident"], [], ["ident"], key="cst")
    S.memset("dve", X.onesb[:, :], 1.0, ["onesb"])
    S.memset("dve", X.ones128[:, :], 1.0, ["ones128"])
    S.memset("dve", X.eps[:, :], LN_EPS, ["eps"])
    S.barrier()
    src = x_in
    for l in range(2):
        dst = y_out if l == 1 else xc
        sb.reset(base); phase_1a(nc, S, X, sb, l, src, yT, D); S.barrier(); print("sbuf 1a", sb.off)
        sb.reset(base); phase_1b(nc, S, X, sb, l, src, yT, xa, D); S.barrier(); print("sbuf 1b", sb.off)
        sb.reset(base); phase_2(nc, S, X, sb, l, xa, xb, D); S.barrier(); print("sbuf 2", sb.off)
        sb.reset(base); phase_3(nc, S, X, sb, l, xb, dst, xg, yg, D); S.barrier(); print("sbuf 3", sb.off)
        src = xc
    counts = S.emit()
    print("instr counts", counts, "sems", S.nsem)
    return nc


def host_inputs(d):
    cst = consts()
    L = 2
    a = [mixer_a_inputs(d, l) for l in range(L)]
    m = {}
    m['w_in'] = d['w_in']
    for k in ('bqk', 'bglow', 'brow', 'bxc', 'wg2', 'bg', 'gnorm', 'sgu_g', 'sgu_b', 'wsT', 'bsT', 'pwbd', 'pscT'):
        m[k] = np.ascontiguousarray(np.stack([a[l][k] for l in range(L)]))
    m['b_gateT'] = np.ascontiguousarray(np.stack([d['b_in'][l][2320:5392].reshape(24, 128).T for l in range(L)]))
    for k in ('w_up_a', 'w_up_b', 'w_up_c', 'w_o', 'ln1_g', 'ln1_b', 'ln2_g', 'ln2_b', 'ln3_g', 'ln3_b',
              'xa_wq', 'xa_wk', 'xa_wv', 'xa_wo', 'router_w', 'exp_w_gu', 'exp_w_down', 'exp_b_down'):
        m[k] = d[k]
    m['router_b'] = np.ascontiguousarray(d['router_b'].reshape(L, 1, NE))
    m['b_guT'] = np.ascontiguousarray(np.stack([d['exp_b_gu'][l].reshape(NE, 16, 128).transpose(2, 0, 1).reshape(128, NE * 16) for l in range(L)]))
    for k in ('ident', 'tri_i', 'tri_a', 'cmask', 'cmfull', 'su'):
        m[k] = cst[k]
    m['eoff'] = np.tile((np.arange(NE) * CAP).astype(np.float32)[None, :], (128, 1))
    m['invc'] = invc_for(True)
    return m


def kernel(**inputs):
    d = {k: np.asarray(v) for k, v in inputs.items()}
    X = np.ascontiguousarray(d['x'], dtype=np.float32)
    B = X.shape[0]
    shared = host_inputs(d)
    nc = build_program()
    owner = {0: 0, 1: 1, 4: 2, 5: 3, 2: 0, 3: 1, 6: 2, 7: 3}
    in_maps = []
    for c in range(N_CORES):
        b = owner[c]
        m = dict(shared)
        m['x'] = np.ascontiguousarray(X[b])
        m['memT'] = np.ascontiguousarray(d['mem'][b].T)
        in_maps.append(m)
    res = run_bass_kernel_spmd(nc, in_maps, core_ids=list(range(N_CORES)))
    out = np.stack([np.asarray(res.results[c]['y']) for c in (0, 1, 4, 5)])
    return out.astype(np.float32)
```

```python
import numpy as np
import concourse.bass as bass
import concourse.mybir as mybir
from concourse.bass_utils import run_bass_kernel_spmd


F32 = mybir.dt.float32
BF16 = mybir.dt.bfloat16
I32 = mybir.dt.int32
U32 = mybir.dt.uint32
AF = mybir.ActivationFunctionType
ALU = mybir.AluOpType
AX = mybir.AxisListType

SEM_LIMIT = 4000


class Sched:
    ENGS = ("pe", "act", "dve", "pool", "sp")

    def __init__(self, nc):
        self.nc = nc
        self.ops = []
        self.last_w = {}
        self.readers = {}
        self.pending_barrier = {}
        self.last_op_eng = {}
        self.last_op_key = {}

    def op(self, eng, fn, reads=(), writes=(), key=None):
        idx = len(self.ops)
        deps = set()
        for r in reads:
            if r in self.last_w:
                deps.add(self.last_w[r])
        for w in writes:
            if w in self.last_w:
                deps.add(self.last_w[w])
            for i in self.readers.get(w, {}).values():
                deps.add(i)
        if eng in self.pending_barrier:
            deps |= self.pending_barrier.pop(eng)
        rec = dict(eng=eng, fn=fn, deps=deps, key=key, signal=False)
        self.ops.append(rec)
        if key is None:
            self.last_op_eng[eng] = idx
        else:
            self.last_op_key[key] = idx
        tag = eng if key is None else ("dma", key)
        for r in reads:
            self.readers.setdefault(r, {})[tag] = idx
        for w in writes:
            self.last_w[w] = idx
            self.readers[w] = {}
        return idx

    def barrier(self):
        deps = set(self.last_op_eng.values()) | set(self.last_op_key.values())
        for e in self.ENGS:
            self.pending_barrier[e] = set(deps) | self.pending_barrier.get(e, set())
        self.last_w = {}
        self.readers = {}

    def mm(self, out, lhsT, rhs, start, stop, reads, writes):
        self.op("pe", lambda e: e.matmul(out, lhsT, rhs, start=start, stop=stop), reads, writes)

    def tr(self, out, in_, ident, reads, writes):
        self.op("pe", lambda e: e.transpose(out, in_, ident), reads, writes)

    def dma(self, eng, out, in_, reads, writes, key, **kw):
        self.op(eng, lambda e: e.dma_start(out, in_, **kw), reads, writes, key=key)


    def act(self, out, in_, func, reads, writes, **kw):
        self.op("act", lambda e: e.activation(out, in_, func, **kw), reads, writes)

    def copy(self, eng, out, in_, reads, writes):
        if eng == "act":
            self.op("act", lambda e: e.copy(out, in_), reads, writes)
        else:
            self.op(eng, lambda e: e.tensor_copy(out, in_), reads, writes)

    def tt(self, eng, out, in0, in1, op, reads, writes):
        self.op(eng, lambda e: e.tensor_tensor(out, in0, in1, op), reads, writes)

    def ts(self, eng, out, in0, s1, s2, op0, op1, reads, writes):
        if op1 is None:
            self.op(eng, lambda e: e.tensor_scalar(out, in0, s1, None, op0), reads, writes)
        else:
            self.op(eng, lambda e: e.tensor_scalar(out, in0, s1, s2, op0, op1), reads, writes)

    def stt(self, out, in0, scalar, in1, op0, op1, reads, writes):
        self.op("dve", lambda e: e.scalar_tensor_tensor(out, in0, scalar, in1, op0, op1), reads, writes)

    def memset(self, eng, ap, val, writes):
        self.op(eng, lambda e: e.memset(ap, val), [], writes)

    def gen(self, eng, name, args, reads, writes, **kw):
        self.op(eng, lambda e: getattr(e, name)(*args, **kw), reads, writes)

    def emit(self):
        nc = self.nc
        ops = self.ops
        for o in ops:
            o["deps"] = {d for d in o["deps"] if not (ops[d]["eng"] == "pe" and o["eng"] == "pe" and ops[d]["key"] is None and o["key"] is None)}
            for d in o["deps"]:
                ops[d]["signal"] = True
        eng_sem = {}
        key_sem = {}
        waited = {e: {} for e in self.ENGS}
        per_eng = {e: [] for e in self.ENGS}
        nsem = 0
        for o in ops:
            e = o["eng"]
            waits = []
            for d in sorted(o["deps"]):
                od = ops[d]
                if od["key"] is not None:
                    sem, cnt = key_sem[od["key"]]
                    tk = (sem, cnt)
                else:
                    tk = od["ticket"]
                sem, val = tk
                sid = id(sem)
                if waited[e].get(sid, (None, 0))[1] >= val:
                    continue
                waited[e][sid] = (sem, val)
                waits.append((sem, val))
            m = {}
            for sem, val in waits:
                if id(sem) not in m or m[id(sem)][1] < val:
                    m[id(sem)] = (sem, val)
            waits = list(m.values())
            sig = None
            if o["key"] is not None:
                if o["key"] not in key_sem:
                    key_sem[o["key"]] = [nc.alloc_semaphore(f"k{nsem}"), 0]
                    nsem += 1
                ks = key_sem[o["key"]]
                ks[1] += 16
                sig = (ks[0], 16)
            elif o["signal"]:
                if e not in eng_sem or eng_sem[e][1] >= SEM_LIMIT:
                    eng_sem[e] = [nc.alloc_semaphore(f"e{nsem}"), 0]
                    nsem += 1
                es = eng_sem[e]
                es[1] += 1
                o["ticket"] = (es[0], es[1])
                sig = (es[0], 1)
            per_eng[e].append((waits, o["fn"], sig))
        self.nsem = nsem
        final_waits = [(s, c) for (s, c) in key_sem.values()]

        def run(engine, lst, final=False):
            for waits, fn, sig in lst:
                for sem, val in waits:
                    engine.wait_ge(sem, val)
                inst = fn(engine)
                if sig is not None:
                    inst.then_inc(sig[0], sig[1])
            if final:
                for sem, val in final_waits:
                    engine.wait_ge(sem, val)

        with nc.Block() as block:
            @block.tensor
            def _(eng):
                run(eng, per_eng["pe"])

            @block.scalar
            def _(eng):
                run(eng, per_eng["act"])

            @block.vector
            def _(eng):
                run(eng, per_eng["dve"])

            @block.gpsimd
            def _(eng):
                run(eng, per_eng["pool"])

            @block.sync
            def _(eng):
                run(eng, per_eng["sp"], final=True)
        return {e: len(per_eng[e]) for e in self.ENGS}


class PsumPool:
    def __init__(self, nc, names):
        self.tiles = [(n, nc.alloc_psum_tensor(n, [128, 512], F32)) for n in names]
        self.i = 0

    def get(self):
        n, t = self.tiles[self.i % len(self.tiles)]
        self.i += 1
        return n, t


class Arena:
    BASE = 16512
    TOP = 229376

    def __init__(self, nc):
        self.nc = nc
        self.off = self.BASE
        self.n = 0

    def reset(self, to=None):
        self.off = self.BASE if to is None else to

    def __call__(self, name, shape, dtype):
        n = 1
        for s in shape[1:]:
            n *= s
        nbytes = n * (4 if dtype in (F32, I32, U32) else 2)
        nbytes = (nbytes + 63) // 64 * 64
        assert self.off + nbytes <= self.TOP, (name, self.off, nbytes)
        t = self.nc.alloc_sbuf_tensor_at(f"{name}_{self.n}", shape, dtype, offset=self.off)
        self.n += 1
        self.off += nbytes
        return t


T = 4096
NQ = T // 512
NT = T // 128
ALPHA = 4.0 ** 0.25
LN_EPS = 1e-5
NE = 32
CAP = 768
BLOCKS = ((0, 3), (3, 6))
CB = 384
CQ, CK, CV, CR, CG, CUV, CXC, GATE0 = 0, 256, 512, 1024, 1536, 1552, 2064, 2320


class Ctx:
    pass


def interleave(gens):
    gens = list(gens)
    while gens:
        for g in list(gens):
            try:
                next(g)
            except StopIteration:
                gens.remove(g)


def ln_tail(S, X, y, stats, mv, rstd, g_bc, b_bc, o_t, tagp, eng_g="pool"):
    ry, rst, ro = tagp
    for hh in range(2):
        S.gen("dve", "bn_stats", (stats[:, hh, :], y[:, hh * 512:(hh + 1) * 512]), [ry], [rst + ("s", hh)])
    S.gen("dve", "bn_aggr", (mv[:, :], stats[:, :, :].rearrange("p a b -> p (a b)")), [rst + ("s", 0), rst + ("s", 1)], [rst + ("mv",)])
    S.act(rstd[:, 0:1], mv[:, 1:2], AF.Ln, [rst + ("mv",), "eps"], [rst + ("r0",)], bias=X.eps[:, 0:1], scale=1.0)
    S.act(rstd[:, 1:2], rstd[:, 0:1], AF.Exp, [rst + ("r0",)], [rst + ("r1",)], scale=-0.5)
    S.ts("dve", y[:, :], y[:, :], mv[:, 0:1], rstd[:, 1:2], ALU.subtract, ALU.mult, [ry, rst + ("mv",), rst + ("r1",)], [ry])
    S.tt(eng_g, y[:, :], y[:, :], g_bc[:, :], ALU.mult, [ry, "gb"], [ry])
    S.tt("pool", o_t, y[:, :], b_bc[:, :], ALU.add, [ry, "gb2"], [ro])


def make_evac(S):
    cp_i = [0]

    def evac(out, in_, reads, writes):
        cp_i[0] += 1
        S.copy("act" if cp_i[0] % 2 else "dve", out, in_, reads, writes)
    return evac


def x_front(S, X, sb_x, src_d, q, xq, xbf, xT, evac):
    def ld(qq):
        S.dma("sp", xq[qq % 2][:, :, :], src_d[qq * 512:(qq + 1) * 512, :].rearrange("(t p) d -> p t d", p=128), [], [("x", qq % 2)], key=("xld", qq % 2))
    if q == 0:
        ld(0)
    if q + 1 < NQ:
        ld(q + 1)
    xb = xq[q % 2]
    for tt in range(4):
        S.copy("pool", xbf[:, tt, :], xb[:, tt, :], [("x", q % 2)], ["xbf"])
    for fc in range(8):
        pn, pst = X.PST[fc % 2]
        for tt in range(4):
            S.tr(pst[:, tt * 128:(tt + 1) * 128], xbf[:, tt, fc * 128:(fc + 1) * 128], X.ident[:, :], ["xbf", "ident"], [pn])
        evac(xT[:, fc, :], pst[:, 0:512], [pn], ["xT"])


def phase_1a(nc, S, X, sb, l, x_d, yT_d, D):
    PS = X.PS
    PST = X.PST
    evac = make_evac(S)
    win_d = D["w_in"][l]
    wA = sb("wA", [128, 8, 2320], BF16)
    brow = sb("brow_bf", [1, 1792], BF16)
    bqk = sb("bqk_sb", [64, 8], F32)
    bgl = sb("bgl_sb", [16, 1], F32)
    bxc = sb("bxc_sb", [128, 2], F32)
    wg2 = sb("wg2_bf", [16, 256], BF16)
    bg = sb("bg_bf", [1, 256], BF16)
    gn_bc = sb("gn_bc", [128, 512], F32)
    sg_bc = sb("sg_bc", [128, 256], F32)
    sb_bc = sb("sb_bc", [128, 256], F32)
    wsT32 = sb("wsT32", [128, 4, 128], F32)
    wsT = sb("wsT_bf", [128, 4, 128], BF16)
    bsT = sb("bsT_sb", [128, 4], F32)
    pw = sb("pw_bf", [128, 2, 128], BF16)
    psc = sb("psc_sb", [128, 2], F32)
    tri_i = sb("tri_i_sb", [128, 128], F32)
    tri_a = sb("tri_a_sb", [128, 128], F32)
    cm4 = sb("cm4", [128, 4, 128], F32)
    cmf = sb("cmf", [128, 128], F32)
    invc = sb("invc_sb", [128, 2, 16], F32)
    xq = [sb(f"xq{i}", [128, 4, 1024], F32) for i in range(2)]
    xbf = sb("xbf", [128, 4, 1024], BF16)
    xT = sb("xT", [128, 8, 512], BF16)
    qTs = sb("qTs", [64, 4, 512], F32)
    kTs = sb("kTs", [64, 4, 512], F32)
    glT = sb("glT", [16, 512], BF16)
    yTs = sb("yTs", [128, 8, 512], F32)
    xcb = [sb(f"xcb{c}", [128, 528], F32) for c in range(2)]
    pa = [sb(f"pa{c}", [128, 528], F32) for c in range(2)]
    pb = [sb(f"pb{c}", [128, 528], F32) for c in range(2)]
    ypre = sb("ypre", [128, 2, 512], BF16)
    ptmp = sb("ptmp", [128, 16], F32)
    e1 = sb("e1", [128, 256], F32)
    l32 = sb("l32", [128, 256], F32)
    ebT = sb("ebT", [64, 4, 128], F32)
    enbT = sb("enbT", [64, 4, 128], F32)
    erem = sb("erem", [128, 256], F32)
    ktl = sb("ktl", [128, 256], BF16)
    vbf = sb("vbf", [128, 512], BF16)
    qdA = sb("qdA", [64, 4, 128], BF16)
    qdB = sb("qdB", [64, 4, 128], BF16)
    kdT = sb("kdT", [64, 4, 128], BF16)
    scm = sb("scm", [128, 512], BF16)
    Sa = sb("Sa", [64, 512], F32)
    Sb_ = sb("Sb", [64, 512], F32)
    S0b = sb("S0b", [64, 512], BF16)
    S1b = sb("S1b", [64, 512], BF16)
    sr = sb("sr", [128, 512], F32)
    sgn = sb("sgn", [128, 512], F32)
    junk = sb("junk", [128, 128], F32)
    osb = sb("osb", [128, 512], F32)
    ssq = sb("ssq", [128, 4], F32)
    rs4 = sb("rs4", [128, 8], F32)
    yabf = sb("yabf", [128, 512], BF16)
    zz = sb("zz", [128, 512], F32)
    vn = sb("vn", [128, 256], F32)
    vnb = sb("vnb", [128, 256], BF16)
    ybbf = sb("ybbf", [128, 256], BF16)
    stats = sb("stats", [128, 6], F32)
    mv = sb("mv", [128, 2], F32)
    rstd = sb("rstd", [128, 2], F32)

    for j, (c0, c1) in enumerate(((0, 1024), (1024, 2048), (2048, 2320))):
        S.dma("pool", wA[:, :, c0:c1], win_d[:, c0:c1].rearrange("(k p) n -> p k n", p=128), [], [("wA", j)], key=("wA", j))
    WA = [("wA", 0), ("wA", 1), ("wA", 2)]
    CST = ["cst"]
    S.dma("pool", brow[:, :], D["brow"][l], [], CST, key="cst")
    S.dma("pool", wg2[:, :], D["wg2"][l], [], CST, key="cst")
    S.dma("pool", bg[:, :], D["bg"][l], [], CST, key="cst")
    S.dma("pool", pw[:, :, :], D["pwbd"][l], [], CST, key="cst")
    S.dma("sp", bqk[:, :], D["bqk"][l], [], CST, key="cst")
    S.dma("sp", bgl[:, :], D["bglow"][l], [], CST, key="cst")
    S.dma("sp", bxc[:, :], D["bxc"][l], [], CST, key="cst")
    S.dma("sp", gn_bc[:, :], D["gnorm"][l].partition_broadcast(128), [], CST, key="cst")
    S.dma("sp", sg_bc[:, :], D["sgu_g"][l].partition_broadcast(128), [], CST, key="cst")
    S.dma("sp", sb_bc[:, :], D["sgu_b"][l].partition_broadcast(128), [], CST, key="cst")
    S.dma("sp", wsT32[:, :, :], D["wsT"][l], [], CST, key="cst")
    S.dma("sp", bsT[:, :], D["bsT"][l], [], CST, key="cst")
    S.dma("sp", psc[:, :], D["pscT"][l], [], CST, key="cst")
    S.dma("sp", tri_i[:, :], D["tri_i"], [], CST, key="cst")
    S.dma("sp", tri_a[:, :], D["tri_a"], [], CST, key="cst")
    for h in range(4):
        S.dma("sp", cm4[:, h, :], D["cmask"], [], CST, key="cst")
    S.dma("sp", cmf[:, :], D["cmfull"], [], CST, key="cst")
    S.dma("sp", invc[:, :, :], D["invc"], [], CST, key="cst")
    S.memset("dve", Sa[:, :], 0.0, ["Sa"])
    S.memset("dve", qdA[:, :, :], 0.0, ["qdA"])
    S.memset("dve", qdB[:, :, :], 0.0, ["qdB"])
    for c in range(2):
        S.memset("pool", xcb[c][:, :], 0.0, [("xcb", c)])
    for g in range(4):
        S.tt("dve", wsT[:, g, :], wsT32[:, g, :], cmf[:, :], ALU.mult, CST, ["wsT"])
    S.copy("pool", S0b[:, :], Sa[:, :], ["Sa"], ["S0b"])

    def quad_front(q):
        x_front(S, X, sb, x_d, q, xq, xbf, xT, evac)
        pn, ps = PS.get()
        for kc in range(8):
            S.mm(ps[0:16, :], wA[:, kc, CG:CG + 16], xT[:, kc, :], kc == 0, kc == 7, WA + ["xT"], [pn])
        S.act(glT[:, :], ps[0:16, :], AF.Identity, [pn] + CST, ["glT"], bias=bgl[:, 0:1], scale=1.0)
        for h in range(4):
            pn, ps = PS.get()
            for kc in range(8):
                S.mm(ps[0:64, :], wA[:, kc, CQ + h * 64:CQ + (h + 1) * 64], xT[:, kc, :], kc == 0, kc == 7, WA + ["xT"], [pn])
            S.ts("dve", qTs[:, h, :], ps[0:64, :], bqk[:, h:h + 1], 0.125, ALU.add, ALU.mult, [pn] + CST, ["qTs"])
            pn, ps = PS.get()
            for kc in range(8):
                S.mm(ps[0:64, :], wA[:, kc, CK + h * 64:CK + (h + 1) * 64], xT[:, kc, :], kc == 0, kc == 7, WA + ["xT"], [pn])
            S.act(kTs[:, h, :], ps[0:64, :], AF.Identity, [pn] + CST, ["kTs"], bias=bqk[:, 4 + h:5 + h], scale=1.0)
        for c in range(2):
            rxc = ("xcb", c)
            if q > 0:
                S.copy("pool", xcb[c][:, 0:16], xcb[c][:, 512:528], [rxc], [rxc])
            pn, ps = PS.get()
            for kc in range(8):
                S.mm(ps[:, :], wA[:, kc, CXC + c * 128:CXC + (c + 1) * 128], xT[:, kc, :], kc == 0, kc == 7, WA + ["xT"], [pn])
            S.act(xcb[c][:, 16:528], ps[:, :], AF.Identity, [pn] + CST, [rxc], bias=bxc[:, c:c + 1], scale=1.0)

    def gla_tile(tt):
        tsl = slice(tt * 128, (tt + 1) * 128)
        pnk, psk = PS.get()
        for kc in range(8):
            S.mm(psk[:, 0:256], xT[:, kc, tsl], wA[:, kc, CK:CK + 256], kc == 0, False, WA + ["xT"], [pnk])
        S.mm(psk[:, 0:256], X.onesb[0:1, :], brow[0:1, 0:256], False, True, ["onesb"] + CST, [pnk])
        pnv, psv = PS.get()
        for kc in range(8):
            S.mm(psv[:, :], xT[:, kc, tsl], wA[:, kc, CV:CV + 512], kc == 0, False, WA + ["xT"], [pnv])
        S.mm(psv[:, :], X.onesb[0:1, :], brow[0:1, 256:768], False, True, ["onesb"] + CST, [pnv])
        evac(vbf[:, :], psv[:, :], [pnv], ["vbf"])
        yield
        pnz, psz = PS.get()
        S.mm(psz[:, 0:256], glT[0:16, tsl], wg2[0:16, :], True, False, ["glT"] + CST, [pnz])
        S.mm(psz[:, 0:256], X.onesb[0:1, :], bg[0:1, :], False, True, ["onesb"] + CST, [pnz])
        S.act(e1[:, :], psz[:, 0:256], AF.Exp, [pnz], ["e1"], scale=-1.0)
        S.act(l32[:, :], e1[:, :], AF.Ln, ["e1"], ["l32"], bias=1.0, scale=1.0)
        yield
        pnb, psb = PS.get()
        for h in range(4):
            S.mm(psb[0:64, h * 128:(h + 1) * 128], l32[:, h * 64:(h + 1) * 64], tri_i[:, :], True, True, ["l32"] + CST, [pnb])
        pnr, psr = PS.get()
        S.mm(psr[:, 0:256], tri_a[:, :], l32[:, :], True, True, ["l32"] + CST, [pnr])
        yield
        S.act(ebT[:, :, :].rearrange("p a b -> p (a b)"), psb[0:64, :], AF.Exp, [pnb], ["ebT"], scale=1.0)
        S.act(enbT[:, :, :].rearrange("p a b -> p (a b)"), psb[0:64, :], AF.Exp, [pnb], ["enbT"], scale=-1.0)
        S.act(erem[:, :], psr[:, 0:256], AF.Exp, [pnr], ["erem"], scale=1.0)
        S.tt("dve", ktl[:, :], psk[:, 0:256], erem[:, :], ALU.mult, [pnk, "erem"], ["ktl"])
        yield
        kvs = []
        for c in range(2):
            pn, ps = PS.get()
            for h in range(4):
                S.mm(ps[0:64, h * 128:(h + 1) * 128], ktl[c * 64:(c + 1) * 64, h * 64:(h + 1) * 64], vbf[c * 64:(c + 1) * 64, h * 128:(h + 1) * 128], True, True, ["ktl", "vbf"], [pn])
            kvs.append((pn, ps))
        yield
        S.tt("dve", qdA[:, :, 0:64], qTs[:, :, tt * 128:tt * 128 + 64], ebT[:, :, 0:64], ALU.mult, ["qTs", "ebT"], ["qdA"])
        S.tt("dve", qdB[:, :, 64:128], qTs[:, :, tt * 128 + 64:tt * 128 + 128], ebT[:, :, 64:128], ALU.mult, ["qTs", "ebT"], ["qdB"])
        S.tt("dve", kdT[:, :, :], kTs[:, :, tsl], enbT[:, :, :], ALU.mult, ["kTs", "enbT"], ["kdT"])
        yield
        pns, pss = PS.get()
        for h in range(4):
            S.mm(pss[:, h * 128:h * 128 + 64], kdT[:, h, :], qdA[:, h, 0:64], True, True, ["kdT", "qdA"], [pns])
            S.mm(pss[:, h * 128 + 64:h * 128 + 128], kdT[:, h, :], qdB[:, h, 64:128], True, True, ["kdT", "qdB"], [pns])
        S.tt("dve", scm[:, :], pss[:, :], cm4[:, :, :].rearrange("p a b -> p (a b)"), ALU.mult, [pns] + CST, ["scm"])
        yield
        for h in range(4):
            hs = slice(h * 128, (h + 1) * 128)
            S.stt(Sb_[:, hs], Sa[:, hs], ebT[:, h, 63:64], kvs[0][1][0:64, hs], ALU.mult, ALU.add, ["Sa", "ebT", kvs[0][0]], ["Sb"])
        S.copy("pool", S1b[:, :], Sb_[:, :], ["Sb"], ["S1b"])
        yield
        pno, pso = PS.get()
        for h in range(4):
            hs = slice(h * 128, (h + 1) * 128)
            S.mm(pso[:, hs], scm[:, hs], vbf[:, hs], True, False, ["scm", "vbf"], [pno])
            S.mm(pso[:, hs], qdA[:, h, :], S0b[:, hs], False, False, ["qdA", "S0b"], [pno])
            S.mm(pso[:, hs], qdB[:, h, :], S1b[:, hs], False, True, ["qdB", "S1b"], [pno])
        for h in range(4):
            hs = slice(h * 128, (h + 1) * 128)
            S.stt(Sa[:, hs], Sb_[:, hs], ebT[:, h, 127:128], kvs[1][1][0:64, hs], ALU.mult, ALU.add, ["Sb", "ebT", kvs[1][0]], ["Sa"])
        S.copy("pool", S0b[:, :], Sa[:, :], ["Sa"], ["S0b"])
        yield
        pnr2, psr2 = PS.get()
        for kc in range(8):
            S.mm(psr2[:, :], xT[:, kc, tsl], wA[:, kc, CR:CR + 512], kc == 0, False, WA + ["xT"], [pnr2])
        S.mm(psr2[:, :], X.onesb[0:1, :], brow[0:1, 768:1280], False, True, ["onesb"] + CST, [pnr2])
        S.act(sr[:, :], psr2[:, :], AF.Silu, [pnr2], ["sr"])
        S.tt("pool", sgn[:, :], sr[:, :], gn_bc[:, :], ALU.mult, ["sr"] + CST, ["sgn"])
        yield
        S.copy("act", osb[:, :], pso[:, :], [pno], ["osb"])
        for h in range(4):
            hs = slice(h * 128, (h + 1) * 128)
            S.op("dve", (lambda a, b, d: (lambda e: e.scalar_tensor_tensor(a, b, 1.0, b, ALU.mult, ALU.mult, accum_out=d)))(junk[:, :], osb[:, hs], ssq[:, h:h + 1]),
                 ["osb"], ["junk", ("ssq", h)])
        S.act(rs4[:, 0:4], ssq[:, :], AF.Ln, [("ssq", h) for h in range(4)] + ["eps"], ["rs4a"], bias=X.eps[:, 0:1], scale=1.0 / 128.0)
        S.act(rs4[:, 4:8], rs4[:, 0:4], AF.Exp, ["rs4a"], ["rs4b"], scale=-0.5)
        yield
        for h in range(4):
            hs = slice(h * 128, (h + 1) * 128)
            S.stt(yabf[:, hs], osb[:, hs], rs4[:, 4 + h:5 + h], sgn[:, hs], ALU.mult, ALU.mult, ["osb", "rs4b", "sgn"], ["yabf"])
        pn, pst = PST[0]
        for h in range(4):
            S.tr(pst[:, h * 128:(h + 1) * 128], yabf[:, h * 128:(h + 1) * 128], X.ident[:, :], ["yabf", "ident"], [pn])
        evac(yTs[:, 0:4, tsl], pst[:, 0:512].rearrange("p (a b) -> p a b", b=128), [pn], [("yTs", "a")])

    def sgu_tile(tt):
        tsl = slice(tt * 128, (tt + 1) * 128)
        pn, ps = PS.get()
        for kc in range(8):
            S.mm(ps[:, :], xT[:, kc, tsl], wA[:, kc, CUV:CUV + 512], kc == 0, False, WA + ["xT"], [pn])
        S.mm(ps[:, :], X.onesb[0:1, :], brow[0:1, 1280:1792], False, True, ["onesb"] + CST, [pn])
        yield
        S.act(zz[:, :], ps[:, :], AF.Gelu, [pn], ["zz"])
        S.gen("dve", "bn_stats", (stats[:, :], zz[:, 256:512]), ["zz"], ["stats"])
        S.gen("dve", "bn_aggr", (mv[:, :], stats[:, :]), ["stats"], ["mv"])
        yield
        S.act(rstd[:, 0:1], mv[:, 1:2], AF.Ln, ["mv", "eps"], ["rstd0"], bias=X.eps[:, 0:1], scale=1.0)
        S.act(rstd[:, 1:2], rstd[:, 0:1], AF.Exp, ["rstd0"], ["rstd1"], scale=-0.5)
        S.ts("dve", vn[:, :], zz[:, 256:512], mv[:, 0:1], rstd[:, 1:2], ALU.subtract, ALU.mult, ["zz", "mv", "rstd1"], ["vn"])
        S.tt("pool", vn[:, :], vn[:, :], sg_bc[:, :], ALU.mult, ["vn"] + CST, ["vn"])
        S.tt("pool", vnb[:, :], vn[:, :], sb_bc[:, :], ALU.add, ["vn"] + CST, ["vnb"])
        yield
        pn2, ps2 = PS.get()
        for g in range(4):
            S.mm(ps2[:, g * 64:(g + 1) * 64], wsT[:, g, :], vnb[:, g * 64:(g + 1) * 64], True, True, ["wsT", "vnb"], [pn2])
        for g in range(4):
            gs = slice(g * 64, (g + 1) * 64)
            S.stt(ybbf[:, gs], ps2[:, gs], bsT[:, g:g + 1], zz[:, gs], ALU.add, ALU.mult, [pn2, "zz"] + CST, ["ybbf"])
        yield
        pn, pst = PST[1]
        for c in range(2):
            S.tr(pst[:, c * 128:(c + 1) * 128], ybbf[:, c * 128:(c + 1) * 128], X.ident[:, :], ["ybbf", "ident"], [pn])
        evac(yTs[:, 4:6, tsl], pst[:, 0:256].rearrange("p (a b) -> p a b", b=128), [pn], [("yTs", "b")])

    def pool_quad(q):
        for c in range(2):
            rxc = ("xcb", c)
            Xc = xcb[c]
            A = pa[c]
            B = pb[c]
            S.tt("pool", A[:, 1:528], Xc[:, 1:528], Xc[:, 0:527], ALU.add, [rxc], [("pa", c)])
            S.tt("pool", B[:, 3:528], A[:, 3:528], A[:, 1:526], ALU.add, [("pa", c)], [("pb", c)])
            if c == 0:
                S.stt(ypre[0:64, 0, :], A[0:64, 16:528], 0.5, Xc[0:64, 16:528], ALU.mult, ALU.subtract, [("pa", c), rxc], [("ypre", 0)])
                S.stt(ypre[64:128, 0, :], B[64:128, 16:528], 0.25, Xc[64:128, 16:528], ALU.mult, ALU.subtract, [("pb", c), rxc], [("ypre", 0)])
            else:
                S.tt("pool", A[:, 7:528], B[:, 7:528], B[:, 3:524], ALU.add, [("pb", c)], [("pa", c)])
                S.tt("pool", B[64:128, 15:528], A[64:128, 15:528], A[64:128, 7:520], ALU.add, [("pa", c)], [("pb", c)])
                S.stt(ypre[0:64, 1, :], A[0:64, 16:528], 0.125, Xc[0:64, 16:528], ALU.mult, ALU.subtract, [("pa", c), rxc], [("ypre", 1)])
                S.stt(ypre[64:128, 1, :], B[64:128, 16:528], 0.0625, Xc[64:128, 16:528], ALU.mult, ALU.subtract, [("pb", c), rxc], [("ypre", 1)])
            if q == 0:
                for (Z, p0, p1, nm) in ((A, 0, 64, "pa"), (B, 64, 128, "pb")):
                    S.tt("dve", ptmp[p0:p1, :], Z[p0:p1, 16:32], invc[p0:p1, c, :], ALU.mult, [(nm, c)] + CST, ["ptmp"])
                    S.tt("dve", ypre[p0:p1, c, 0:16], ptmp[p0:p1, :], Xc[p0:p1, 16:32], ALU.subtract, ["ptmp", rxc], [("ypre", c)])
            pn, ps = PS.get()
            S.mm(ps[:, :], pw[:, c, :], ypre[:, c, :], True, True, [("ypre", c)] + CST, [pn])
            S.ts("dve", yTs[:, 6 + c, :], ps[:, :], psc[:, c:c + 1], None, ALU.mult, None, [pn] + CST, [("yTs", "c", c)])

    for q in range(NQ):
        quad_front(q)
        for tt in range(4):
            interleave([gla_tile(tt), sgu_tile(tt)])
        pool_quad(q)
        S.dma("sp", yT_d[:, q * 512:(q + 1) * 512].rearrange("(k p) t -> p k t", p=128), yTs[:, :, :],
              [("yTs", "a"), ("yTs", "b"), ("yTs", "c", 0), ("yTs", "c", 1)], [], key="yst")


def phase_1b(nc, S, X, sb, l, x_d, yT_d, y_d, D):
    PS = X.PS
    evac = make_evac(S)
    win_d = D["w_in"][l]
    wg = sb("wg", [128, 8, 3072], BF16)
    wup = sb("wup", [128, 8, 1024], BF16)
    wo = sb("wo", [128, 8, 1024], BF16)
    bgate = sb("bgate", [128, 24], F32)
    g_bc = sb("g_bc", [128, 1024], F32)
    b_bc = sb("b_bc", [128, 1024], F32)
    xq = [sb(f"xq{i}", [128, 4, 1024], F32) for i in range(2)]
    xbf = sb("xbf", [128, 4, 1024], BF16)
    xT = sb("xT", [128, 8, 512], BF16)
    yTq = sb("yTq", [128, 8, 512], BF16)
    sg = [sb(f"sg{j}", [128, 512], F32) for j in range(3)]
    mm_ = [sb(f"m{j}", [128, 512], F32) for j in range(3)]
    mT = sb("mT", [128, 8, 512], BF16)
    yb = [sb(f"yb{i}", [128, 1024], F32) for i in range(2)]
    ob = [sb(f"ob{i}", [128, 1024], F32) for i in range(2)]
    stats = [sb(f"st{i}", [128, 2, 6], F32) for i in range(2)]
    mv = [sb(f"mv{i}", [128, 2], F32) for i in range(2)]
    rstd = [sb(f"rstd{i}", [128, 2], F32) for i in range(2)]

    for j in range(3):
        S.dma("pool", wg[:, :, j * 1024:(j + 1) * 1024], win_d[:, GATE0 + j * 1024:GATE0 + (j + 1) * 1024].rearrange("(k p) n -> p k n", p=128), [], [("wg", j)], key=("wA", j))
    for r0, r1, nm in ((0, 4, "w_up_a"), (4, 6, "w_up_b"), (6, 8, "w_up_c")):
        S.dma("pool", wup[:, r0:r1, :], D[nm][l].rearrange("(k p) n -> p k n", p=128), [], [("wup", r0)], key="wup")
    WUP = [("wup", 0), ("wup", 4), ("wup", 6)]
    S.dma("pool", wo[:, :, :], D["w_o"][l].rearrange("(k p) n -> p k n", p=128), [], ["wo"], key="wo")
    S.dma("sp", bgate[:, :], D["b_gateT"][l], [], ["bgate"], key="cst")
    S.dma("sp", g_bc[:, :], D["ln1_g"][l].partition_broadcast(128), [], ["gb"], key="cst")
    S.dma("sp", b_bc[:, :], D["ln1_b"][l].partition_broadcast(128), [], ["gb2"], key="cst")

    for q in range(NQ):
        x_front(S, X, sb, x_d, q, xq, xbf, xT, evac)
        S.dma("pool", yTq[:, :, :], yT_d[:, q * 512:(q + 1) * 512].rearrange("(k p) t -> p k t", p=128), [], ["yTq"], key="yld")
        for fo in range(8):
            fsl = slice(fo * 128, (fo + 1) * 128)
            ups = []
            for (k0, k1) in ((0, 4), (4, 6), (6, 8)):
                pn, ps = PS.get()
                for kc in range(k0, k1):
                    S.mm(ps[:, :], wup[:, kc, fsl], yTq[:, kc, :], kc == k0, kc == k1 - 1, WUP + ["yTq"], [pn])
                ups.append((pn, ps))
            for j in range(3):
                pn, ps = PS.get()
                for kc in range(8):
                    S.mm(ps[:, :], wg[:, kc, j * 1024 + fo * 128:j * 1024 + (fo + 1) * 128], xT[:, kc, :], kc == 0, kc == 7, [("wg", j), "xT"], [pn])
                S.act(sg[j][:, :], ps[:, :], AF.Identity, [pn, "bgate"], [("sg", j)], bias=bgate[:, j * 8 + fo:j * 8 + fo + 1], scale=1.0)
                S.act(sg[j][:, :], sg[j][:, :], AF.Sigmoid, [("sg", j)], [("sg", j)], scale=1.0)
                S.tt("dve", mm_[j][:, :], ups[j][1][:, :], sg[j][:, :], ALU.mult, [ups[j][0], ("sg", j)], [("m", j)])
            S.tt("pool", mm_[0][:, :], mm_[0][:, :], mm_[1][:, :], ALU.add, [("m", 0), ("m", 1)], [("m", 0)])
            S.tt("pool", mT[:, fo, :], mm_[0][:, :], mm_[2][:, :], ALU.add, [("m", 0), ("m", 2)], ["mT"])
        for tt in range(4):
            i = tt % 2
            ry = ("y", i)
            for hf in range(2):
                pn, ps = PS.get()
                for kc in range(8):
                    S.mm(ps[:, :], mT[:, kc, tt * 128:(tt + 1) * 128], wo[:, kc, hf * 512:(hf + 1) * 512], kc == 0, kc == 7, ["mT", "wo"], [pn])
                S.stt(yb[i][:, hf * 512:(hf + 1) * 512], xq[q % 2][:, tt, hf * 512:(hf + 1) * 512], ALPHA, ps[:, :], ALU.mult, ALU.add, [pn, ("x", q % 2)], [ry])
            ln_tail(S, X, yb[i], stats[i], mv[i], rstd[i], g_bc, b_bc, ob[i][:, :], (ry, ("st", i), ("o", i)))
            tok0 = q * 512 + tt * 128
            S.dma("sp", y_d[tok0:tok0 + 128, :], ob[i][:, :], [("o", i)], [], key=("yst", i))


def phase_2(nc, S, X, sb, l, x_d, y_d, D):
    PS = X.PS
    evac = make_evac(S)
    w_bf = {n: sb(n + "_bf", [128, 8, 1024], BF16) for n in ("xa_wq", "xa_wk", "xa_wv", "xa_wo")}
    memT_bf = sb("memT_bf", [128, 8, 256], BF16)
    kT_bf = sb("kT_bf", [128, 8, 256], BF16)
    V_bf = sb("V_bf", [128, 2, 1024], BF16)
    g_bc = sb("g_bc", [128, 1024], F32)
    b_bc = sb("b_bc", [128, 1024], F32)
    xq = [sb(f"xq{i}", [128, 4, 1024], F32) for i in range(2)]
    xbf = sb("xbf", [128, 4, 1024], BF16)
    xT = sb("xT", [128, 8, 512], BF16)
    qT = sb("qT", [128, 8, 512], BF16)
    PT = [sb(f"PT{i}", [128, 2, 512], BF16) for i in range(2)]
    rs = sb("rs", [128, 512], F32)
    oT = sb("oT", [128, 8, 512], BF16)
    yb = [sb(f"yb{i}", [128, 1024], F32) for i in range(2)]
    ob = [sb(f"ob{i}", [128, 1024], F32) for i in range(2)]
    stats = [sb(f"st{i}", [128, 2, 6], F32) for i in range(2)]
    mv = [sb(f"mv{i}", [128, 2], F32) for i in range(2)]
    rstd = [sb(f"rstd{i}", [128, 2], F32) for i in range(2)]
    for n in w_bf:
        S.dma("pool", w_bf[n][:, :, :], D[n][l].rearrange("(k p) n -> p k n", p=128), [], [n], key=n)
    S.dma("pool", memT_bf[:, :, :], D["memT"].rearrange("(k p) n -> p k n", p=128), [], ["memT"], key="cst")
    S.dma("sp", g_bc[:, :], D["ln2_g"][l].partition_broadcast(128), [], ["gb"], key="cst")
    S.dma("sp", b_bc[:, :], D["ln2_b"][l].partition_broadcast(128), [], ["gb2"], key="cst")
    for fc in range(8):
        pn, ps = PS.get()
        for kc in range(8):
            S.mm(ps[:, 0:256], w_bf["xa_wk"][:, kc, fc * 128:(fc + 1) * 128], memT_bf[:, kc, :], kc == 0, kc == 7, ["xa_wk", "memT"], [pn])
        evac(kT_bf[:, fc, :], ps[:, 0:256], [pn], ["kT"])
    for mc in range(2):
        for hf in range(2):
            pn, ps = PS.get()
            for kc in range(8):
                S.mm(ps[:, :], memT_bf[:, kc, mc * 128:(mc + 1) * 128], w_bf["xa_wv"][:, kc, hf * 512:(hf + 1) * 512], kc == 0, kc == 7, ["xa_wv", "memT"], [pn])
            evac(V_bf[:, mc, hf * 512:(hf + 1) * 512], ps[:, :], [pn], ["V"])
    for q in range(NQ):
        xb = xq[q % 2]
        rx = ("x", q % 2)
        x_front(S, X, sb, x_d, q, xq, xbf, xT, evac)
        for fc in range(8):
            pn, ps = PS.get()
            for kc in range(8):
                S.mm(ps[:, :], w_bf["xa_wq"][:, kc, fc * 128:(fc + 1) * 128], xT[:, kc, :], kc == 0, kc == 7, ["xa_wq", "xT"], [pn])
            evac(qT[:, fc, :], ps[:, :], [pn], ["qT"])
        for h in range(4):
            pt = PT[h % 2]
            rpt = ("PT", h % 2)
            for mc in range(2):
                pn, ps = PS.get()
                for dc in range(2):
                    S.mm(ps[:, :], kT_bf[:, h * 2 + dc, mc * 128:(mc + 1) * 128], qT[:, h * 2 + dc, :], dc == 0, dc == 1, ["kT", "qT"], [pn])
                S.act(pt[:, mc, :], ps[:, :], AF.Exp, [pn], [rpt], scale=1.0 / 16.0)
            pn, ps = PS.get()
            for mc in range(2):
                S.mm(ps[:, :], X.ones128[:, :], pt[:, mc, :], mc == 0, mc == 1, ["ones128", rpt], [pn])
            S.gen("dve", "reciprocal", (rs[:, :], ps[:, :]), [pn], ["rs"])
            for dc in range(2):
                pn, ps = PS.get()
                for mc in range(2):
                    S.mm(ps[:, :], V_bf[:, mc, h * 256 + dc * 128:h * 256 + (dc + 1) * 128], pt[:, mc, :], mc == 0, mc == 1, ["V", rpt], [pn])
                S.tt("dve", oT[:, h * 2 + dc, :], ps[:, :], rs[:, :], ALU.mult, [pn, "rs"], ["oT"])
        for tt in range(4):
            i = tt % 2
            ry = ("y", i)
            for hf in range(2):
                pn, ps = PS.get()
                for kc in range(8):
                    S.mm(ps[:, :], oT[:, kc, tt * 128:(tt + 1) * 128], w_bf["xa_wo"][:, kc, hf * 512:(hf + 1) * 512], kc == 0, kc == 7, ["oT", "xa_wo"], [pn])
                S.stt(yb[i][:, hf * 512:(hf + 1) * 512], xb[:, tt, hf * 512:(hf + 1) * 512], ALPHA, ps[:, :], ALU.mult, ALU.add, [pn, rx], [ry])
            ln_tail(S, X, yb[i], stats[i], mv[i], rstd[i], g_bc, b_bc, ob[i][:, :], (ry, ("st", i), ("o", i)))
            tok0 = q * 512 + tt * 128
            S.dma("sp", y_d[tok0:tok0 + 128, :], ob[i][:, :], [("o", i)], [], key=("yst", i))


def phase_3(nc, S, X, sb, l, x_d, y_d, xg_d, yg_d, D):
    PS = X.PS
    PST = X.PST
    evac = make_evac(S)
    wgu_d = D["exp_w_gu"][l]
    wd_d = D["exp_w_down"][l]
    bd_d = D["exp_b_down"][l]
    X.phase_base = sb.off
    wgu = [sb(f"wgu{i}", [128, 8, 2048], BF16) for i in range(2)]
    wd = [sb(f"wd{i}", [128, 8, 1024], BF16) for i in range(2)]
    X.p3_weights_end = sb.off
    bd = [sb(f"bd{i}", [1, 1024], BF16) for i in range(2)]
    bgu = sb("bgu", [128, NE * 16], F32)
    rw = sb("rw", [128, 8, NE], F32)
    rb = sb("rb", [1, NE], F32)
    ident32 = sb("ident32", [128, 128], F32)
    su = sb("su_bf", [128, 128], BF16)
    ones32 = sb("ones32", [1, 128], F32)
    ones4 = sb("ones4", [128, 4], F32)
    eoff = sb("eoff_sb", [128, NE], F32)
    g_bc = sb("g_bc", [128, 1024], F32)
    b_bc = sb("b_bc", [128, 1024], F32)
    xt_ = [sb(f"xt{i}", [128, 1024], F32) for i in range(2)]
    xbf = [sb(f"xbf{i}", [128, 1024], BF16) for i in range(2)]
    xT32_ = [sb(f"xT32{i}", [128, 8, 128], F32) for i in range(2)]
    lg_ = [sb(f"lg{i}", [128, NE], F32) for i in range(2)]
    work_ = [sb(f"work{i}", [128, NE], F32) for i in range(2)]
    tmx_ = [sb(f"tmx{i}", [128, 32], F32) for i in range(2)]
    mk_ = [sb(f"mk{i}", [128, 4], F32) for i in range(2)]
    ohs_ = [sb(f"ohs{i}", [128, 4, NE], F32) for i in range(2)]
    num4_ = [sb(f"num4{i}", [128, 4], F32) for i in range(2)]
    negm_ = [sb(f"negm{i}", [128, 1], F32) for i in range(2)]
    ex_ = [sb(f"ex{i}", [128, NE], F32) for i in range(2)]
    den_ = [sb(f"den{i}", [128, 2], F32) for i in range(2)]
    maskf_ = [sb(f"maskf{i}", [128, NE], F32) for i in range(2)]
    maskb = sb("maskb", [128, NT, NE], BF16)
    destf_ = [sb(f"destf{i}", [128, NE], F32) for i in range(2)]
    junk_ = [sb(f"junk{i}", [128, NE], F32) for i in range(2)]
    d4f = sb("d4f", [128, NT, 4], F32)
    d4i = sb("d4i", [128, NT, 4], I32)
    g4 = sb("g4", [128, NT, 4], F32)
    xgs = [sb(f"xgs{i}", [128, 3, 1024], BF16) for i in range(2)]
    xgT = [sb(f"xgT{i}", [128, 8, CB], BF16) for i in range(2)]
    aT = [sb(f"aT{i}", [128, 8, CB], BF16) for i in range(2)]
    tg = [sb(f"tg{i}", [128, CB], F32) for i in range(2)]
    tsg = [sb(f"tsg{i}", [128, CB], F32) for i in range(2)]
    tl0 = [sb(f"tl0{i}", [128, CB], F32) for i in range(2)]
    tl1 = sb("tl1", [128, CB], F32)
    tgs = sb("tgs", [128, CB], F32)
    ysb = [sb(f"ysb{i}", [128, 1024], F32) for i in range(2)]
    stats = sb("stats", [128, 2, 6], F32)
    mv = sb("mv", [128, 2], F32)
    rstd = sb("rstd", [128, 2], F32)

    S.dma("sp", ident32[:, :], D["ident"], [], ["id32"], key="cst")
    S.dma("pool", su[:, :], D["su"], [], ["su"], key="cst")
    S.dma("sp", eoff[:, :], D["eoff"], [], ["eoff"], key="cst")
    S.dma("sp", rw[:, :, :], D["router_w"][l].rearrange("(k p) n -> p k n", p=128), [], ["rw"], key="cst")
    S.dma("sp", rb[:, :], D["router_b"][l], [], ["rb"], key="cst")
    S.dma("sp", bgu[:, :], D["b_guT"][l], [], ["bgu"], key="cst")
    S.dma("sp", g_bc[:, :], D["ln3_g"][l].partition_broadcast(128), [], ["gb"], key="cst")
    S.dma("sp", b_bc[:, :], D["ln3_b"][l].partition_broadcast(128), [], ["gb2"], key="cst")
    S.memset("dve", ones32[:, :], 1.0, ["ones32"])
    S.memset("dve", ones4[:, :], 1.0, ["ones4"])

    def load_expert(e):
        i = e % 2
        S.dma("pool", wgu[i][:, :, :], wgu_d[e].rearrange("(k p) n -> p k n", p=128), [], [("wgu", i, 0), ("wgu", i, 1)], key=("wgu", i))
        S.dma("pool", wd[i][:, :, :], wd_d[e].rearrange("(k p) n -> p k n", p=128), [], [("wd", i)], key=("wd", i))
        S.dma("pool", bd[i][:, :], bd_d[e:e + 1, :], [], [("bd", i)], key=("bd", i))

    load_expert(0)
    load_expert(1)

    def stt_acc(out, in0, in1, accum, reads, writes):
        S.op("dve", (lambda a, b, c, d: (lambda e: e.scalar_tensor_tensor(a, b, 1.0, c, ALU.mult, ALU.mult, accum_out=d)))(out, in0, in1, accum), reads, writes)

    def route_tile(j, bq):
        S.dma("sp", xt_[bq][:, :], x_d[j * 128:(j + 1) * 128, :], [], [("xt", bq)], key=("xt", bq))
        xb = xbf[j % 2]
        rxb = ("xbf", j % 2)
        S.copy("pool", xb[:, :], xt_[bq][:, :], [("xt", bq)], [rxb])
        for fc in range(8):
            pn, ps = PS.get()
            S.tr(ps[:, 0:128], xt_[bq][:, fc * 128:(fc + 1) * 128], ident32[:, :], [("xt", bq), "id32"], [pn])
            evac(xT32_[bq][:, fc, :], ps[:, 0:128], [pn], [("xT32", bq)])
        yield
        pn, ps = PS.get()
        for kc in range(8):
            S.mm(ps[:, 0:NE], xT32_[bq][:, kc, :], rw[:, kc, :], kc == 0, False, [("xT32", bq), "rw"], [pn])
        S.mm(ps[:, 0:NE], ones32[0:1, :], rb[0:1, :], False, True, ["ones32", "rb"], [pn])
        S.copy("dve", lg_[bq][:, :], ps[:, 0:NE], [pn], [("lg", bq)])
        S.copy("dve", work_[bq][:, :], lg_[bq][:, :], [("lg", bq)], [("work", bq)])
        for k in range(4):
            yield
            S.tt("dve", tmx_[bq][:, 0:16], work_[bq][:, 0:16], work_[bq][:, 16:32], ALU.max, [("work", bq)], [("tmx", bq)])
            S.tt("dve", tmx_[bq][:, 16:24], tmx_[bq][:, 0:8], tmx_[bq][:, 8:16], ALU.max, [("tmx", bq)], [("tmx", bq)])
            S.tt("dve", tmx_[bq][:, 24:28], tmx_[bq][:, 16:20], tmx_[bq][:, 20:24], ALU.max, [("tmx", bq)], [("tmx", bq)])
            S.tt("dve", tmx_[bq][:, 28:30], tmx_[bq][:, 24:26], tmx_[bq][:, 26:28], ALU.max, [("tmx", bq)], [("tmx", bq)])
            S.tt("dve", mk_[bq][:, k:k + 1], tmx_[bq][:, 28:29], tmx_[bq][:, 29:30], ALU.max, [("tmx", bq)], [("mk", bq)])
            S.ts("dve", ohs_[bq][:, k, :], work_[bq][:, :], mk_[bq][:, k:k + 1], None, ALU.is_equal, None, [("work", bq), ("mk", bq)], [("ohs", bq)])
            S.stt(work_[bq][:, :], ohs_[bq][:, k, :], -1e30, work_[bq][:, :], ALU.mult, ALU.add, [("ohs", bq), ("work", bq)], [("work", bq)])
        yield
        S.ts("dve", maskf_[bq][:, :], lg_[bq][:, :], mk_[bq][:, 3:4], None, ALU.is_ge, None, [("lg", bq), ("mk", bq)], [("maskf", bq)])
        S.copy("dve", maskb[:, j, :], maskf_[bq][:, :], [("maskf", bq)], [("maskb", j)])
        S.ts("dve", negm_[bq][:, :], mk_[bq][:, 0:1], -1.0, None, ALU.mult, None, [("mk", bq)], [("negm", bq)])
        S.act(ex_[bq][:, :], lg_[bq][:, :], AF.Exp, [("lg", bq), ("negm", bq)], [("ex", bq)], bias=negm_[bq][:, 0:1], scale=1.0)
        yield
        pn, ps = PS.get()
        for i in range(j):
            S.mm(ps[:, 0:NE], X.ones128[:, :], maskb[:, i, :], i == 0, False, ["ones128", ("maskb", i)], [pn])
        S.mm(ps[:, 0:NE], su[:, :], maskb[:, j, :], j == 0, True, ["su", ("maskb", j)], [pn])
        S.ts("dve", destf_[bq][:, :], ps[:, 0:NE], float(CAP - 1), None, ALU.min, None, [pn], [("destf", bq)])
        S.tt("dve", destf_[bq][:, :], destf_[bq][:, :], eoff[:, :], ALU.add, [("destf", bq), "eoff"], [("destf", bq)])
        yield
        for k in range(4):
            stt_acc(junk_[bq][:, :], ohs_[bq][:, k, :], destf_[bq][:, :], d4f[:, j, k:k + 1], [("ohs", bq), ("destf", bq)], [("junk", bq), ("d4f", j)])
            stt_acc(junk_[bq][:, :], ohs_[bq][:, k, :], ex_[bq][:, :], num4_[bq][:, k:k + 1], [("ohs", bq), ("ex", bq)], [("junk", bq), ("num4", bq)])
        stt_acc(junk_[bq][:, 0:4], num4_[bq][:, :], ones4[:, :], den_[bq][:, 0:1], [("num4", bq), "ones4"], [("junk", bq), ("den0", bq)])
        S.gen("dve", "reciprocal", (den_[bq][:, 1:2], den_[bq][:, 0:1]), [("den0", bq)], [("den1", bq)])
        S.ts("dve", g4[:, j, :], num4_[bq][:, :], den_[bq][:, 1:2], None, ALU.mult, None, [("num4", bq), ("den1", bq)], [("g4", j)])
        yield
        S.copy("dve", d4i[:, j, :], d4f[:, j, :], [("d4f", j)], [("d4i", j)])
        for k in range(4):
            S.op("pool", (lambda o_, i_, idx: (lambda e: e.indirect_dma_start(o_, bass.IndirectOffsetOnAxis(ap=idx, axis=0), i_, None)))(xg_d[:, :], xb[:, :], d4i[:, j, k:k + 1]),
                 [rxb, ("d4i", j)], [("xg", j, k)], key="xgsc")


    for j0 in range(0, NT, 2):
        interleave([route_tile(j0, 0), route_tile(j0 + 1, 1)])

    XG_ALL = [("xg", j, k) for j in range(NT) for k in range(4)]

    def load_xg(e_, bi_):
        t0_, t1_ = BLOCKS[bi_]
        b_ = (e_ * len(BLOCKS) + bi_) % 2
        for st in range(t1_ - t0_):
            r0 = e_ * CAP + (t0_ + st) * 128
            S.dma("sp", xgs[b_][:, st, :], xg_d[r0:r0 + 128, :], XG_ALL, [("xgs", b_, st)], key=("xgs", b_, st))
    YG_ALL = []
    for e in range(NE):
        i = e % 2
        for bi, (t0, t1) in enumerate(BLOCKS):
            nst = t1 - t0
            cb = nst * 128
            bb = (e * len(BLOCKS) + bi) % 2
            if e == 0 and bi == 0:
                load_xg(0, 0)
            nb = e * len(BLOCKS) + bi + 1
            if nb < NE * len(BLOCKS):
                load_xg(nb // len(BLOCKS), nb % len(BLOCKS))
            for fc in range(8):
                pn, pst = PST[fc % 2]
                for st in range(nst):
                    S.tr(pst[:, st * 128:(st + 1) * 128], xgs[bb][:, st, fc * 128:(fc + 1) * 128], X.ident[:, :], [("xgs", bb, st), "ident"], [pn])
                evac(xgT[bb][:, fc, 0:cb], pst[:, 0:cb], [pn], [("xgT", bb)])
            def stage_a(jj):
                d2 = jj % 2
                png, psg = PS.get()
                for kc in range(8):
                    S.mm(psg[:, 0:cb], wgu[i][:, kc, jj * 128:(jj + 1) * 128], xgT[bb][:, kc, 0:cb], kc == 0, kc == 7, [("wgu", i, 0), ("xgT", bb)], [png])
                pnl, psl = PS.get()
                for kc in range(8):
                    S.mm(psl[:, 0:cb], wgu[i][:, kc, 1024 + jj * 128:1024 + (jj + 1) * 128], xgT[bb][:, kc, 0:cb], kc == 0, kc == 7, [("wgu", i, 1), ("xgT", bb)], [pnl])
                cg = e * 16 + jj
                cl = e * 16 + 8 + jj
                S.ts("dve", tg[d2][:, 0:cb], psg[:, 0:cb], bgu[:, cg:cg + 1], 7.0, ALU.add, ALU.min, [png, "bgu"], [("tg", d2)])
                S.act(tsg[d2][:, 0:cb], tg[d2][:, 0:cb], AF.Sigmoid, [("tg", d2)], [("tsg", d2)], scale=1.702)
                S.act(tl0[d2][:, 0:cb], psl[:, 0:cb], AF.Identity, [pnl, "bgu"], [("tl0", d2)], bias=bgu[:, cl:cl + 1], scale=1.0)

            def stage_b(jj):
                d2 = jj % 2
                S.ts("dve", tl1[:, 0:cb], tl0[d2][:, 0:cb], -7.0, 7.0, ALU.max, ALU.min, [("tl0", d2)], ["tl1"])
                S.tt("dve", tgs[:, 0:cb], tg[d2][:, 0:cb], tsg[d2][:, 0:cb], ALU.mult, [("tg", d2), ("tsg", d2)], ["tgs"])
                S.stt(aT[bb][:, jj, 0:cb], tl1[:, 0:cb], 1.0, tgs[:, 0:cb], ALU.add, ALU.mult, ["tl1", "tgs"], [("aT", bb)])

            stage_a(0)
            for jj in range(1, 8):
                stage_a(jj)
                stage_b(jj - 1)
            stage_b(7)
            for st in range(nst):
                yi = st % 2
                for hf in range(2):
                    pn, ps = PS.get()
                    for kc in range(8):
                        S.mm(ps[:, :], aT[bb][:, kc, st * 128:(st + 1) * 128], wd[i][:, kc, hf * 512:(hf + 1) * 512], kc == 0, False, [("aT", bb), ("wd", i)], [pn])
                    S.mm(ps[:, :], X.onesb[0:1, :], bd[i][0:1, hf * 512:(hf + 1) * 512], False, True, ["onesb", ("bd", i)], [pn])
                    evac(ysb[yi][:, hf * 512:(hf + 1) * 512], ps[:, :], [pn], [("ysb", yi)])
                r0 = e * CAP + (t0 + st) * 128
                S.dma("sp", yg_d[r0:r0 + 128, :], ysb[yi][:, :], [("ysb", yi)], [("yg", e, t0 + st)], key="ygst")
                YG_ALL.append(("yg", e, t0 + st))
        if e + 2 < NE:
            load_expert(e + 2)

    S.barrier()
    keep = sb.off
    sb.reset(X.phase_base)
    yk = [[sb(f"yk{b}{i}", [128, 1024], F32) for i in range(4)] for b in range(2)]
    acc2 = [sb(f"acc{b}", [128, 1024], F32) for b in range(2)]
    ob2 = [sb(f"ob{b}", [128, 1024], F32) for b in range(2)]
    xt2 = [sb(f"xt{b}", [128, 1024], F32) for b in range(2)]
    st2 = [sb(f"stc{b}", [128, 2, 6], F32) for b in range(2)]
    mv2 = [sb(f"mvc{b}", [128, 2], F32) for b in range(2)]
    rs2 = [sb(f"rsc{b}", [128, 2], F32) for b in range(2)]
    assert sb.off <= X.p3_weights_end
    sb.reset(keep)
    def fetch(j):
        b = j % 2
        S.dma("sp", xt2[b][:, :], x_d[j * 128:(j + 1) * 128, :], [], [("xt", b)], key=("xtc", b))
        for k in range(4):
            S.op("pool", (lambda o_, i_, idx: (lambda e: e.indirect_dma_start(o_, None, i_, bass.IndirectOffsetOnAxis(ap=idx, axis=0))))(yk[b][k][:, :], yg_d[:, :], d4i[:, j, k:k + 1]),
                 [], [("yk", b, k)], key=("ykg", b, k))

    fetch(0)
    for j in range(NT):
        b = j % 2
        if j + 1 < NT:
            fetch(j + 1)
        S.ts("dve", acc2[b][:, :], yk[b][0][:, :], g4[:, j, 0:1], None, ALU.mult, None, [("yk", b, 0)], [("acc", b)])
        for k in range(1, 4):
            S.stt(acc2[b][:, :], yk[b][k][:, :], g4[:, j, k:k + 1], acc2[b][:, :], ALU.mult, ALU.add, [("yk", b, k), ("acc", b)], [("acc", b)])
        S.stt(acc2[b][:, :], xt2[b][:, :], ALPHA, acc2[b][:, :], ALU.mult, ALU.add, [("xt", b), ("acc", b)], [("acc", b)])
        ln_tail(S, X, acc2[b], st2[b], mv2[b], rs2[b], g_bc, b_bc, ob2[b][:, :], (("acc", b), ("stc", b), ("ob", b)), eng_g="dve")
        S.dma("sp", y_d[j * 128:(j + 1) * 128, :], ob2[b][:, :], [("ob", b)], [], key=("yst3", b))


def consts():
    s = np.arange(128)[:, None]; t = np.arange(128)[None, :]
    same = (s // 64) == (t // 64)
    c = {}
    c['ident'] = np.eye(128, dtype=np.float32)
    c['tri_i'] = np.where((s <= t) & same, -1.0 / 16.0, 0.0).astype(np.float32)
    c['tri_a'] = np.where((s > t) & same, -1.0 / 16.0, 0.0).astype(np.float32)
    c['cmask'] = ((s <= t) & same).astype(np.float32)
    c['cmfull'] = (s <= t).astype(np.float32)
    c['su'] = (s < t).astype(np.float32)
    return c

def invc_for(first_half):
    out = np.zeros((128, 2, 16), np.float32)
    tt = np.arange(16)
    for gi, w in enumerate((2, 4, 8, 16)):
        cc, p0 = gi // 2, (gi % 2) * 64
        cnt = np.minimum(tt + 1, w) if first_half else np.full(16, w)
        out[p0:p0 + 64, cc, :] = (1.0 / cnt).astype(np.float32)[None, :]
    return out

def mixer_a_inputs(d, l):
    b_in = d['b_in'][l]
    m = {}
    m['w_in'] = d['w_in'][l]
    m['bqk'] = np.ascontiguousarray(np.concatenate([b_in[0:256].reshape(4, 64).T, b_in[256:512].reshape(4, 64).T], axis=1))
    m['bglow'] = np.ascontiguousarray(b_in[1536:1552].reshape(16, 1))
    m['brow'] = np.ascontiguousarray(np.concatenate([b_in[256:512], b_in[512:1024], b_in[1024:1536], b_in[1552:2064]]).reshape(1, 1792))
    m['bxc'] = np.ascontiguousarray(b_in[2064:2320].reshape(2, 128).T)
    m['wg2'] = d['gla_wg2'][l]
    m['bg'] = np.ascontiguousarray(d['gla_bg'][l].reshape(1, 256))
    m['gnorm'] = d['gla_norm_g'][l]
    m['sgu_g'] = d['sgu_ln_g'][l]
    m['sgu_b'] = d['sgu_ln_b'][l]
    m['wsT'] = np.ascontiguousarray(d['sgu_ws'][l].transpose(2, 0, 1))
    m['bsT'] = np.ascontiguousarray(d['sgu_bs'][l].T)
    pw = d['pool_w'][l]
    bd = np.zeros((128, 2, 128), np.float32)
    for gi in range(4):
        cc, p0 = gi // 2, (gi % 2) * 64
        bd[p0:p0 + 64, cc, p0:p0 + 64] = pw[gi]
    m['pwbd'] = bd
    m['pscT'] = np.ascontiguousarray(d['pool_scale'][l].reshape(2, 128).T)
    return m

def mixer_b_inputs(d, l):
    b_in = d['b_in'][l]
    m = {}
    m['w_in'] = d['w_in'][l]
    m['b_gateT'] = np.ascontiguousarray(b_in[2320:5392].reshape(24, 128).T)
    m['w_up'] = np.ascontiguousarray(np.concatenate([d['w_up_a'][l], d['w_up_b'][l], d['w_up_c'][l]], axis=0))
    m['w_o'] = d['w_o'][l]
    m['ln_g'] = d['ln1_g'][l]
    m['ln_b'] = d['ln1_b'][l]
    return m


N_CORES = 8


def build_program():
    nc = bass.Bass("TRN2", target_bir_lowering=False)
    D = {}

    def din(name, shape):
        D[name] = nc.dram_tensor(name, shape, F32, kind="ExternalInput").ap()
        return D[name]

    x_in = din("x", [T, 1024])
    din("memT", [1024, 256])
    din("w_in", [2, 1024, 5392])
    din("bqk", [2, 64, 8]); din("bglow", [2, 16, 1]); din("brow", [2, 1, 1792]); din("bxc", [2, 128, 2])
    din("wg2", [2, 16, 256]); din("bg", [2, 1, 256]); din("gnorm", [2, 512]); din("sgu_g", [2, 256]); din("sgu_b", [2, 256])
    din("wsT", [2, 128, 4, 128]); din("bsT", [2, 128, 4]); din("pwbd", [2, 128, 2, 128]); din("pscT", [2, 128, 2])
    din("b_gateT", [2, 128, 24])
    din("w_up_a", [2, 512, 1024]); din("w_up_b", [2, 256, 1024]); din("w_up_c", [2, 256, 1024]); din("w_o", [2, 1024, 1024])
    for n in ("ln1_g", "ln1_b", "ln2_g", "ln2_b", "ln3_g", "ln3_b"):
        din(n, [2, 1024])
    for n in ("xa_wq", "xa_wk", "xa_wv", "xa_wo"):
        din(n, [2, 1024, 1024])
    din("router_w", [2, 1024, NE]); din("router_b", [2, 1, NE])
    din("exp_w_gu", [2, NE, 1024, 2048]); din("exp_w_down", [2, NE, 1024, 1024])
    din("b_guT", [2, 128, NE * 16]); din("exp_b_down", [2, NE, 1024])
    for n in ("ident", "tri_i", "tri_a", "cmask", "cmfull", "su"):
        din(n, [128, 128])
    din("eoff", [128, NE]); din("invc", [128, 2, 16])
    y_out = nc.dram_tensor("y", [T, 1024], F32, kind="ExternalOutput").ap()
    xa = nc.dram_tensor("xa_s", [T, 1024], F32, kind="Internal").ap()
    xb = nc.dram_tensor("xb_s", [T, 1024], F32, kind="Internal").ap()
    xc = nc.dram_tensor("xc_s", [T, 1024], F32, kind="Internal").ap()
    yT = nc.dram_tensor("yT_s", [1024, T], F32, kind="Internal").ap()
    xg = nc.dram_tensor("xg_s", [NE * CAP, 1024], BF16, kind="Internal").ap()
    yg = nc.dram_tensor("yg_s", [NE * CAP, 1024], F32, kind="Internal").ap()

    S = Sched(nc)
    sb = Arena(nc)
    X = Ctx()
    X.PS = PsumPool(nc, [f"ps{i}" for i in range(6)])
    X.PST = [(f"pst{i}", nc.alloc_psum_tensor(f"pst{i}", [128, 1024], BF16)) for i in range(2)]
    X.ident = sb("ident_bf", [128, 128], BF16)
    X.onesb = sb("onesb", [1, 128], BF16)
    X.ones128 = sb("ones128", [128, 128], BF16)
    X.eps = sb("eps_t", [128, 1], F32)
    base = sb.off
    S.dma("pool", X.ident[:, :], D["ident"], [], ["ident"], key="cst")
    S.memset("dve", X.onesb[:, :], 1.0, ["onesb"])
    S.memset("dve", X.ones128[:, :], 1.0, ["ones128"])
    S.memset("dve", X.eps[:, :], LN_EPS, ["eps"])
    S.barrier()
    src = x_in
    for l in range(2):
        dst = y_out if l == 1 else xc
        sb.reset(base); phase_1a(nc, S, X, sb, l, src, yT, D); S.barrier(); print("sbuf 1a", sb.off)
        sb.reset(base); phase_1b(nc, S, X, sb, l, src, yT, xa, D); S.barrier(); print("sbuf 1b", sb.off)
        sb.reset(base); phase_2(nc, S, X, sb, l, xa, xb, D); S.barrier(); print("sbuf 2", sb.off)
        sb.reset(base); phase_3(nc, S, X, sb, l, xb, dst, xg, yg, D); S.barrier(); print("sbuf 3", sb.off)
        src = xc
    counts = S.emit()
    print("instr counts", counts, "sems", S.nsem)
    return nc


def host_inputs(d):
    cst = consts()
    L = 2
    a = [mixer_a_inputs(d, l) for l in range(L)]
    m = {}
    m['w_in'] = d['w_in']
    for k in ('bqk', 'bglow', 'brow', 'bxc', 'wg2', 'bg', 'gnorm', 'sgu_g', 'sgu_b', 'wsT', 'bsT', 'pwbd', 'pscT'):
        m[k] = np.ascontiguousarray(np.stack([a[l][k] for l in range(L)]))
    m['b_gateT'] = np.ascontiguousarray(np.stack([d['b_in'][l][2320:5392].reshape(24, 128).T for l in range(L)]))
    for k in ('w_up_a', 'w_up_b', 'w_up_c', 'w_o', 'ln1_g', 'ln1_b', 'ln2_g', 'ln2_b', 'ln3_g', 'ln3_b',
              'xa_wq', 'xa_wk', 'xa_wv', 'xa_wo', 'router_w', 'exp_w_gu', 'exp_w_down', 'exp_b_down'):
        m[k] = d[k]
    m['router_b'] = np.ascontiguousarray(d['router_b'].reshape(L, 1, NE))
    m['b_guT'] = np.ascontiguousarray(np.stack([d['exp_b_gu'][l].reshape(NE, 16, 128).transpose(2, 0, 1).reshape(128, NE * 16) for l in range(L)]))
    for k in ('ident', 'tri_i', 'tri_a', 'cmask', 'cmfull', 'su'):
        m[k] = cst[k]
    m['eoff'] = np.tile((np.arange(NE) * CAP).astype(np.float32)[None, :], (128, 1))
    m['invc'] = invc_for(True)
    return m


def kernel(**inputs):
    d = {k: np.asarray(v) for k, v in inputs.items()}
    X = np.ascontiguousarray(d['x'], dtype=np.float32)
    B = X.shape[0]
    shared = host_inputs(d)
    nc = build_program()
    owner = {0: 0, 1: 1, 4: 2, 5: 3, 2: 0, 3: 1, 6: 2, 7: 3}
    in_maps = []
    for c in range(N_CORES):
        b = owner[c]
        m = dict(shared)
        m['x'] = np.ascontiguousarray(X[b])
        m['memT'] = np.ascontiguousarray(d['mem'][b].T)
        in_maps.append(m)
    res = run_bass_kernel_spmd(nc, in_maps, core_ids=list(range(N_CORES)))
    out = np.stack([np.asarray(res.results[c]['y']) for c in (0, 1, 4, 5)])
    return out.astype(np.float32)
```
